# Optimizing a Trainium2 kernel written in Bass

```python
import jax, jax.numpy as jnp
from jax import lax
import numpy as np

D_MODEL = 2048
BATCH = 4
SEQ = 2048
DEPTH = 4

N_MIXERS = 3
N_ATTN_LAYERS = (DEPTH + 2) // 3
N_RWKV_LAYERS = (DEPTH + 1) // 3
N_CONV_LAYERS = DEPTH // 3
PLE_DIM = 256
NORM_EPS = 1e-6
NEG_INF = -1e30

ATTN_GROUPS = ((128, 1), (512, 4), (2048, 16))
N_GROUPS = 3
ATTN_HEADS = 16
ATTN_HEAD_DIM = D_MODEL // ATTN_HEADS
ATTN_BLOCK = 128

RWKV_HEAD_SIZE = 64
RWKV_HEADS = D_MODEL // RWKV_HEAD_SIZE
RWKV_DECAY_LORA = 96
RWKV_A_LORA = 96
RWKV_GATE_LORA = 256
RWKV_GN_EPS = 6.4e-4

CONV_WIDTH = 3

D_FF = 5632
FFN_CONV_WIDTH = 3

kernel_name = "hybrid_dilatedattn_rwkv7_shortconv_convffn"


def rms_norm(x, g):
    xf = x.astype(jnp.float32)
    y = xf * lax.rsqrt(jnp.mean(xf * xf, axis=-1, keepdims=True) + NORM_EPS)
    return (y * g.astype(jnp.float32)).astype(x.dtype)


def causal_dwconv(u, w):
    width = w.shape[0]
    s = u.shape[1]
    up = jnp.pad(u, ((0, 0), (width - 1, 0), (0, 0)))
    y = up[:, 0:s] * w[0]
    for j in range(1, width):
        y = y + up[:, j:j + s] * w[j]
    return y


def alibi_slopes():
    n = N_GROUPS * ATTN_HEADS
    idx = jnp.arange(1, n + 1, dtype=jnp.float32)
    return (2.0 ** (-8.0 * idx / n)).reshape(N_GROUPS, ATTN_HEADS)


def dilated_window_attention(q, k, v, window, dilation, slopes):
    b, s, h, e = q.shape
    n_back = window // dilation
    L = s // dilation
    nb = -(-L // ATTN_BLOCK)
    Lp = nb * ATTN_BLOCK

    def to_blocks(t):
        t = t.reshape(b, L, dilation, h, e)
        t = jnp.pad(t, ((0, 0), (0, Lp - L), (0, 0), (0, 0), (0, 0)))
        return t.reshape(b, nb, ATTN_BLOCK, dilation, h, e)

    def with_prev(t):
        prev = jnp.pad(t, ((0, 0), (1, 0), (0, 0), (0, 0), (0, 0), (0, 0)))[:, :-1]
        return jnp.concatenate([prev, t], axis=2)

    qb = to_blocks(q)
    kw = with_prev(to_blocks(k))
    vw = with_prev(to_blocks(v))

    scale = ATTN_HEAD_DIM ** -0.5
    scores = jnp.einsum('bnqrhe,bnkrhe->bnrhqk', qb, kw,
                        preferred_element_type=jnp.float32) * scale
    qi = jnp.arange(ATTN_BLOCK)[:, None]
    kj = jnp.arange(2 * ATTN_BLOCK)[None, :]
    dist = qi + ATTN_BLOCK - kj
    first = (jnp.arange(nb) == 0)[:, None, None]
    valid = (dist >= 0) & (dist <= n_back) & jnp.logical_not(first & (kj < ATTN_BLOCK))
    bias = -(slopes.astype(jnp.float32)[:, None, None] * (dist * dilation).astype(jnp.float32)[None])
    scores = scores + bias[None, None, None]
    scores = jnp.where(valid[None, :, None, None], scores, NEG_INF)
    m = jnp.max(scores, axis=-1, keepdims=True)
    pr = jnp.exp(scores - m)
    denom = jnp.sum(pr, axis=-1, keepdims=True)
    out = jnp.einsum('bnrhqk,bnkrhe->bnqrhe', (pr / denom).astype(v.dtype), vw)
    lse = (m + jnp.log(denom))[..., 0]
    out = out.reshape(b, Lp, dilation, h, e)[:, :L].reshape(b, s, h, e)
    lse = lse.transpose(0, 1, 4, 2, 3).reshape(b, Lp, dilation, h)[:, :L].reshape(b, s, h)
    return out, lse


def attention_mixer(h, w_qkv, w_o, slopes):
    b, s, _ = h.shape
    qkv = (h @ w_qkv).reshape(b, s, N_GROUPS, 3, ATTN_HEADS, ATTN_HEAD_DIM)
    outs, lses = [], []
    for g, (window, dil) in enumerate(ATTN_GROUPS):
        o, l = dilated_window_attention(qkv[:, :, g, 0], qkv[:, :, g, 1], qkv[:, :, g, 2],
                                        window, dil, slopes[g])
        outs.append(o)
        lses.append(l)
    outs = jnp.stack(outs, 0)
    alpha = jax.nn.softmax(jnp.stack(lses, 0), axis=0)
    o = jnp.sum(alpha[..., None].astype(outs.dtype) * outs, axis=0)
    return o.reshape(b, s, ATTN_HEADS * ATTN_HEAD_DIM) @ w_o


def rwkv7_mixer(h, mu, w_rkv, w0, w_w1, w_w2, a0, w_a1, w_a2, w_g1, w_g2,
                k_k, k_a, r_k, ln_g, ln_b, w_o):
    b, s, d = h.shape
    H, N = RWKV_HEADS, RWKV_HEAD_SIZE
    xx = jnp.pad(h, ((0, 0), (1, 0), (0, 0)))[:, :-1] - h
    x_rkv = h[None] + xx[None] * mu[:3, None, None, :]
    xw = h + xx * mu[3]
    xa = h + xx * mu[4]
    xg = h + xx * mu[5]
    rkv = jnp.einsum('nbsd,nde->nbse', x_rkv, w_rkv)
    r, k, v = rkv[0], rkv[1], rkv[2]
    w_log = -jax.nn.softplus(-(w0 + jnp.tanh(xw @ w_w1) @ w_w2)) - 0.5
    a = jax.nn.sigmoid(a0 + (xa @ w_a1) @ w_a2)
    g = jax.nn.sigmoid(xg @ w_g1) @ w_g2

    kk = (k * k_k).reshape(b, s, H, N).astype(jnp.float32)
    kk = kk / jnp.maximum(jnp.sqrt(jnp.sum(kk * kk, axis=-1, keepdims=True)), 1e-12)
    k = k * (1 + (a - 1) * k_a)
    decay = jnp.exp(-jnp.exp(w_log.astype(jnp.float32)))

    rh = r.reshape(b, s, H, N)
    kh = k.reshape(b, s, H, N)
    vh = v.reshape(b, s, H, N)
    ah = a.reshape(b, s, H, N).astype(jnp.float32)

    def tm(t):
        return jnp.moveaxis(t.astype(jnp.float32), 1, 0)

    seq_in = (tm(rh), tm(decay.reshape(b, s, H, N)), tm(kh), tm(vh), tm(-kk), tm(kk * ah))

    def step(state, inp):
        r_t, w_t, k_t, v_t, a_t, b_t = inp
        sa = jnp.einsum('bhvk,bhk->bhv', state, a_t)
        state = (state * w_t[:, :, None, :] + sa[..., None] * b_t[:, :, None, :]
                 + v_t[..., None] * k_t[:, :, None, :])
        y_t = jnp.einsum('bhvk,bhk->bhv', state, r_t)
        return state, y_t

    state0 = jnp.zeros((b, H, N, N), jnp.float32)
    _, y = lax.scan(step, state0, seq_in)
    y = jnp.moveaxis(y, 0, 1)
    mean = jnp.mean(y, axis=-1, keepdims=True)
    var = jnp.mean(jnp.square(y - mean), axis=-1, keepdims=True)
    y = ((y - mean) * lax.rsqrt(var + RWKV_GN_EPS)).reshape(b, s, d)
    y = (y * ln_g.astype(jnp.float32) + ln_b.astype(jnp.float32)).astype(h.dtype)
    bonus = jnp.sum(rh * kh * r_k, axis=-1, keepdims=True) * vh
    y = y + bonus.reshape(b, s, d)
    return (y * g) @ w_o


def short_conv_mixer(h, w_in, conv_w, w_out):
    bcu = h @ w_in
    gate_b, gate_c, u = jnp.split(bcu, 3, axis=-1)
    y = causal_dwconv(gate_c * u, conv_w)
    return (gate_b * y) @ w_out


def conv_ffn(h, w_gu, conv_w, conv_b, w_down):
    gate, up = jnp.split(h @ w_gu, 2, axis=-1)
    gate = causal_dwconv(gate, conv_w) + conv_b
    return (jax.nn.silu(gate) * up) @ w_down


def setup_inputs(seed: int = 0) -> dict:
    key = jax.random.key(seed)
    ks = iter(jax.random.split(key, 64))
    f32 = jnp.float32
    D, F = D_MODEL, D_FF

    def nrm(shape, scale):
        return jax.random.normal(next(ks), shape, f32) * scale

    def gain(shape):
        return 1.0 + nrm(shape, 0.02)

    nA, nB, nC = N_ATTN_LAYERS, N_RWKV_LAYERS, N_CONV_LAYERS
    inp = {}
    inp['x'] = nrm((BATCH, SEQ, D), 1.0)
    inp['p'] = nrm((DEPTH, BATCH, SEQ, PLE_DIM), 1.0)
    inp['attn_norm'] = gain((nA, D))
    inp['attn_w_qkv'] = nrm((nA, D, N_GROUPS * 3 * ATTN_HEADS * ATTN_HEAD_DIM), D ** -0.5)
    inp['attn_w_o'] = nrm((nA, ATTN_HEADS * ATTN_HEAD_DIM, D), (ATTN_HEADS * ATTN_HEAD_DIM) ** -0.5)
    inp['rwkv_norm'] = gain((nB, D))
    inp['rwkv_mu'] = jax.random.uniform(next(ks), (nB, 6, D), f32)
    inp['rwkv_w_rkv'] = nrm((nB, 3, D, D), D ** -0.5)
    inp['rwkv_w0'] = jax.random.uniform(next(ks), (nB, D), f32, -6.0, -1.0)
    inp['rwkv_w_w1'] = nrm((nB, D, RWKV_DECAY_LORA), D ** -0.5)
    inp['rwkv_w_w2'] = nrm((nB, RWKV_DECAY_LORA, D), RWKV_DECAY_LORA ** -0.5)
    inp['rwkv_a0'] = nrm((nB, D), 0.1)
    inp['rwkv_w_a1'] = nrm((nB, D, RWKV_A_LORA), D ** -0.5)
    inp['rwkv_w_a2'] = nrm((nB, RWKV_A_LORA, D), RWKV_A_LORA ** -0.5)
    inp['rwkv_w_g1'] = nrm((nB, D, RWKV_GATE_LORA), D ** -0.5)
    inp['rwkv_w_g2'] = nrm((nB, RWKV_GATE_LORA, D), RWKV_GATE_LORA ** -0.5)
    inp['rwkv_k_k'] = 0.85 + nrm((nB, D), 0.02)
    inp['rwkv_k_a'] = 1.0 + nrm((nB, D), 0.02)
    inp['rwkv_r_k'] = nrm((nB, RWKV_HEADS, RWKV_HEAD_SIZE), 0.1)
    inp['rwkv_ln_g'] = gain((nB, D))
    inp['rwkv_ln_b'] = nrm((nB, D), 0.01)
    inp['rwkv_w_o'] = nrm((nB, D, D), D ** -0.5)
    inp['conv_norm'] = gain((nC, D))
    inp['conv_w_in'] = nrm((nC, D, 3 * D), D ** -0.5)
    inp['conv_w'] = nrm((nC, CONV_WIDTH, D), CONV_WIDTH ** -0.5)
    inp['conv_w_out'] = nrm((nC, D, D), D ** -0.5)
    inp['ffn_norm'] = gain((DEPTH, D))
    inp['ffn_w_gu'] = nrm((DEPTH, D, 2 * F), D ** -0.5)
    inp['ffn_conv_w'] = nrm((DEPTH, FFN_CONV_WIDTH, F), FFN_CONV_WIDTH ** -0.5)
    inp['ffn_conv_b'] = nrm((DEPTH, F), 0.01)
    inp['ffn_w_down'] = nrm((DEPTH, F, D), F ** -0.5)
    inp['ple_w_proj'] = nrm((DEPTH, PLE_DIM, D), PLE_DIM ** -0.5)
    inp['ple_norm'] = gain((DEPTH, D))
    inp['ple_w_gate'] = nrm((DEPTH, D, D), D ** -0.5)
    inp['final_norm'] = gain((D,))
    return inp


def reference(x, p, attn_norm, attn_w_qkv, attn_w_o,
              rwkv_norm, rwkv_mu, rwkv_w_rkv, rwkv_w0, rwkv_w_w1, rwkv_w_w2,
              rwkv_a0, rwkv_w_a1, rwkv_w_a2, rwkv_w_g1, rwkv_w_g2,
              rwkv_k_k, rwkv_k_a, rwkv_r_k, rwkv_ln_g, rwkv_ln_b, rwkv_w_o,
              conv_norm, conv_w_in, conv_w, conv_w_out,
              ffn_norm, ffn_w_gu, ffn_conv_w, ffn_conv_b, ffn_w_down,
              ple_w_proj, ple_norm, ple_w_gate, final_norm):
    slopes = alibi_slopes()
    for i in range(DEPTH):
        kind, j = i % N_MIXERS, i // N_MIXERS
        if kind == 0:
            h = rms_norm(x, attn_norm[j])
            x = x + attention_mixer(h, attn_w_qkv[j], attn_w_o[j], slopes)
        elif kind == 1:
            h = rms_norm(x, rwkv_norm[j])
            x = x + rwkv7_mixer(h, rwkv_mu[j], rwkv_w_rkv[j], rwkv_w0[j], rwkv_w_w1[j], rwkv_w_w2[j],
                                rwkv_a0[j], rwkv_w_a1[j], rwkv_w_a2[j], rwkv_w_g1[j], rwkv_w_g2[j],
                                rwkv_k_k[j], rwkv_k_a[j], rwkv_r_k[j], rwkv_ln_g[j], rwkv_ln_b[j],
                                rwkv_w_o[j])
        else:
            h = rms_norm(x, conv_norm[j])
            x = x + short_conv_mixer(h, conv_w_in[j], conv_w[j], conv_w_out[j])
        h = rms_norm(x, ffn_norm[i])
        x = x + conv_ffn(h, ffn_w_gu[i], ffn_conv_w[i], ffn_conv_b[i], ffn_w_down[i])
        gate = jax.nn.sigmoid(rms_norm(x, ple_norm[i]) @ ple_w_gate[i])
        x = x + gate * (p[i] @ ple_w_proj[i])
    return rms_norm(x, final_norm)
```

```python
import contextlib
import numpy as np
import concourse.bass as bass
import concourse.mybir as mybir
from concourse.bass_utils import run_bass_kernel_spmd

F32 = mybir.dt.float32
BF16 = mybir.dt.bfloat16
AF = mybir.ActivationFunctionType
ALU = mybir.AluOpType
AX = mybir.AxisListType

D = 2048; S = 2048; NB = 4; DEPTH = 4; FF = 5632; PLE = 256
NCH = 16; NFC = 44; T = 1024; NTT = 2; TT = 512
EPS = 1e-6
NPASS = S // T


class Tile:
    def __init__(self, t, name):
        self.t = t; self.name = name; self.w = {}; self.r = {}; self.dsem = None

    def __getitem__(self, idx):
        return View(self, self.t[idx])

    def v(self, ap):
        return View(self, ap)


class View:
    def __init__(self, tile, ap):
        self.tile = tile; self.ap = ap


class Eng:
    def __init__(self, kb, h, name):
        self.h = h; self.name = name; self.sid = kb.new_sem("e_" + name); self.waited = {}


class KB:
    def __init__(self, nc, es):
        self.nc = nc; self.es = es
        self.sems = []; self.cnt = []
        self.pe = Eng(self, nc.tensor, "pe"); self.act = Eng(self, nc.scalar, "act")
        self.dve = Eng(self, nc.vector, "dve"); self.pool = Eng(self, nc.gpsimd, "pool")
        self.sp = Eng(self, nc.sync, "sp")
        self.engs = [self.pe, self.act, self.dve, self.pool, self.sp]
        self.free_dsems = {"sp": [self.new_sem("d%d" % i) for i in range(64)], "pool": [self.new_sem("g%d" % i) for i in range(32)]}
        self.n_inst = 0

    def new_sem(self, name):
        h = self.es.enter_context(self.nc.semaphore(name))
        self.sems.append(h); self.cnt.append(0)
        return len(self.sems) - 1

    def op(self, E, fn, reads, writes, inc=True, dma_tile=None, nowait_writes=False):
        waits = {}
        for v in reads:
            for k, val in v.tile.w.items():
                if waits.get(k, 0) < val: waits[k] = val
        for v in ([] if nowait_writes else writes):
            for dd in (v.tile.w, v.tile.r):
                for k, val in dd.items():
                    if waits.get(k, 0) < val: waits[k] = val
        for k, val in waits.items():
            if E.waited.get(k, 0) < val:
                E.h.wait_ge(self.sems[k], val); E.waited[k] = val
        ins = fn()
        self.n_inst += 1
        if dma_tile is not None:
            if dma_tile.dsem is None:
                dma_tile.dsem = {}
            if E.name not in dma_tile.dsem:
                dma_tile.dsem[E.name] = self.free_dsems[E.name].pop()
            k = dma_tile.dsem[E.name]
            self.cnt[k] += 16
            ins.then_inc(self.sems[k], 16)
            tok = (k, self.cnt[k])
        elif inc:
            k = E.sid
            self.cnt[k] += 1
            ins.then_inc(self.sems[k], 1)
            tok = (k, self.cnt[k])
        else:
            tok = (E.sid, self.cnt[E.sid] + 1)
        for v in reads:
            if v.tile.r.get(tok[0], 0) < tok[1]: v.tile.r[tok[0]] = tok[1]
        for v in writes:
            v.tile.w = {tok[0]: tok[1]}; v.tile.r = {}
        return ins

    def release(self, tiles):
        for t in tiles:
            if t.dsem is not None:
                for q, k in t.dsem.items(): self.free_dsems[q].append(k)
                t.dsem = None

    def barrier(self):
        for E in self.engs:
            for k in range(len(self.sems)):
                val = self.cnt[k]
                if val > 0 and E.waited.get(k, 0) < val:
                    E.h.wait_ge(self.sems[k], val); E.waited[k] = val

    def mm(self, out, lhsT, rhs, start=True, stop=True):
        wr = [out] if (start or stop) else []
        return self.op(self.pe, lambda: self.nc.tensor.matmul(out.ap, lhsT.ap, rhs.ap, start=start, stop=stop),
                       [lhsT, rhs], wr, inc=True, nowait_writes=not start)

    def transpose(self, out, in_, ident):
        return self.op(self.pe, lambda: self.nc.tensor.transpose(out.ap, in_.ap, ident.ap), [in_, ident], [out])

    def activation(self, out, in_, func, bias=None, scale=1.0, accum=None, E=None):
        E = E or self.act
        reads = [in_]; kw = {}
        if isinstance(bias, View): reads.append(bias); kw["bias"] = bias.ap
        elif bias is not None: kw["bias"] = bias
        if isinstance(scale, View): reads.append(scale); kw["scale"] = scale.ap
        else: kw["scale"] = scale
        writes = [out]
        if accum is not None: writes.append(accum); kw["accum_out"] = accum.ap
        return self.op(E, lambda: E.h.activation(out=out.ap, in_=in_.ap, func=func, **kw), reads, writes)

    def tt(self, E, out, in0, in1, op):
        return self.op(E, lambda: E.h.tensor_tensor(out.ap, in0.ap, in1.ap, op), [in0, in1], [out])

    def ts(self, E, out, in0, s1, s2, op0, op1=None):
        reads = [in0]
        a1 = s1.ap if isinstance(s1, View) else s1
        a2 = s2.ap if isinstance(s2, View) else s2
        if isinstance(s1, View): reads.append(s1)
        if isinstance(s2, View): reads.append(s2)
        if op1 is None:
            return self.op(E, lambda: E.h.tensor_scalar(out.ap, in0.ap, a1, None, op0), reads, [out])
        return self.op(E, lambda: E.h.tensor_scalar(out.ap, in0.ap, a1, a2, op0, op1), reads, [out])

    def stt(self, E, out, in0, sc, in1, op0, op1):
        reads = [in0, in1]
        a = sc.ap if isinstance(sc, View) else sc
        if isinstance(sc, View): reads.append(sc)
        return self.op(E, lambda: E.h.scalar_tensor_tensor(out.ap, in0.ap, a, in1.ap, op0, op1), reads, [out])

    def copy(self, E, out, in_):
        if E is self.act:
            return self.activation(out, in_, AF.Copy)
        return self.op(E, lambda: E.h.tensor_copy(out.ap, in_.ap), [in_], [out])

    def reduce(self, E, out, in_, op, axis=AX.X):
        return self.op(E, lambda: E.h.tensor_reduce(out.ap, in_.ap, axis, op), [in_], [out])

    def memset(self, E, out, val):
        return self.op(E, lambda: E.h.memset(out.ap, val), [], [out])

    def dma(self, Q, out, in_, sb=None):
        if sb is None:
            sb = out.tile if getattr(out.tile, "is_sb", False) else in_.tile
        return self.op(Q, lambda: Q.h.dma_start(out=out.ap, in_=in_.ap), [in_], [out], dma_tile=sb)


class Phase:
    def __init__(self, kb, name):
        self.kb = kb; self.name = name; self.es = contextlib.ExitStack(); self.tiles = []; self.n = 0

    def sb(self, shape, dt, name=None):
        self.n += 1
        nm = "%s_%s%d" % (self.name, name or "t", self.n)
        t = Tile(self.es.enter_context(self.kb.nc.sbuf_tensor(nm, list(shape), dt)), nm)
        t.is_sb = True
        self.tiles.append(t)
        return t

    def ps(self, shape, dt, name=None):
        self.n += 1
        nm = "%s_%s%d" % (self.name, name or "p", self.n)
        t = Tile(self.es.enter_context(self.kb.nc.psum_tensor(nm, list(shape), dt)), nm)
        self.tiles.append(t)
        return t

    def close(self):
        self.kb.barrier()
        self.kb.release(self.tiles)
        self.es.close()


def vec_layout():
    cols = {}; n = 0

    def add(name, w):
        nonlocal n
        cols[name] = n; n += w
    for j in range(2): add("attn_norm%d" % j, NCH)
    add("rwkv_norm", NCH); add("conv_norm", NCH)
    for i in range(DEPTH): add("ffn_norm%d" % i, NCH)
    for i in range(DEPTH): add("ple_norm%d" % i, NCH)
    add("final_norm", NCH)
    for j in range(6): add("mu%d" % j, NCH)
    for nm in ("w0", "a0", "k_k", "k_a", "ln_g", "ln_b", "r_k"): add(nm, NCH)
    for j in range(3): add("conv_w%d" % j, NCH)
    for i in range(DEPTH):
        for j in range(3): add("fcw%d_%d" % (i, j), NFC)
        add("fcb%d" % i, NFC)
    return cols, n


VCOL, NV = vec_layout()
C_ID = 0; C_ONES = 128; C_D1 = 256; C_SU = 512; C_IU = 640; C_SL = 768; C_BD = 896; C_EPS = 1024; C_GNE = 1025; C_BD1 = 1032; NCONST = 1160


def build_consts():
    c = np.zeros((128, NCONST), np.float32)
    c[:, C_ID:C_ID + 128] = np.eye(128)
    c[:, C_ONES:C_ONES + 128] = 1.0 / D
    qi = np.arange(128)[:, None]; kj = np.arange(256)[None, :]
    dist = qi - kj + 128
    c[:, C_D1:C_D1 + 256] = np.where((dist >= 0) & (dist <= 128), -dist.astype(np.float32), -1.0e6)
    i = np.arange(128)[:, None]; t = np.arange(128)[None, :]
    c[:, C_SU:C_SU + 128] = (i < t); c[:, C_IU:C_IU + 128] = (i <= t); c[:, C_SL:C_SL + 128] = (i > t)
    c[:, C_BD:C_BD + 128] = ((i // 64) == (t // 64)) / 64.0
    c[:, C_EPS] = EPS
    c[:, C_GNE] = 6.4e-4
    c[:, C_BD1:C_BD1 + 128] = ((i // 64) == (t // 64))
    return c


def pc(v, n):
    return np.ascontiguousarray(np.asarray(v, np.float32).reshape(n, 128).T)


class Prog:
    def __init__(self, nc, es, phases):
        self.nc = nc; self.kb = KB(nc, es); self.es = es; self.phases = phases
        kb = self.kb
        dt = lambda name, shape, kind="ExternalInput": nc.dram_tensor(name, list(shape), F32, kind=kind).ap()
        self.xT_in = dt("xT", [NCH, 128, S])
        self.pT = dt("pT", [DEPTH, 2, 128, S])
        self.vecs_d = dt("vecs", [128, NV]); self.consts_d = dt("consts", [128, NCONST])
        self.W = {}
        for name in wnames(phases):
            self.W[name] = dt(name, WSHAPES[name])
        self.out = Tile(dt("outT", [NCH, 128, S], kind="ExternalOutput"), "outT")
        self.xT = Tile(nc.dram_tensor("xT_s", [NCH, 128, S], F32).ap(), "xT_s")
        self.xin = Tile(self.xT_in, "xT_in")
        self.wtile = Tile(None, "weights")
        self.pp = Phase(kb, "g")
        self.vecs = self.pp.sb([128, NV], F32, "vecs"); self.consts = self.pp.sb([128, NCONST], F32, "consts")
        self.ffn_halo = self.pp.sb([128, NFC, 2], F32, "fhalo")
        self.identb = self.pp.sb([128, 128], BF16, "identb")
        kb.dma(kb.sp, self.vecs[:], View(Tile(self.vecs_d, "vd"), self.vecs_d))
        kb.dma(kb.sp, self.consts[:], View(Tile(self.consts_d, "cd"), self.consts_d))
        kb.copy(kb.dve, self.identb[:], self.consts[:, C_ID:C_ID + 128])
        self.xcur = self.xin
        self.build()

    def vcol(self, name, c=0):
        k = VCOL[name] + c
        return self.vecs[:, k:k + 1]

    def norm(self, ph, xsrc, t0, gname, HT, ps_pool, out_dram=None):
        kb = self.kb
        ones = self.consts[:, C_ONES:C_ONES + 128]
        k = 0
        for j in range(NTT):
            ps = ps_pool[j % len(ps_pool)]
            tsl = slice(t0 + j * TT, t0 + (j + 1) * TT)
            for c in range(NCH):
                xt = ph.xn[k % 4]; k += 1
                kb.dma(kb.sp, xt[:], xsrc[c, :, tsl])
                sq = ph.sq[c % 2]
                kb.activation(sq[:], xt[:], AF.Square)
                kb.mm(ps[:], ones, sq[:], start=(c == 0), stop=(c == NCH - 1))
            rstd = ph.rstd
            kb.activation(rstd[:], ps[:], AF.Sqrt, bias=self.consts[:, C_EPS:C_EPS + 1])
            kb.op(kb.dve, lambda: self.nc.vector.reciprocal(rstd.t[:], rstd.t[:]), [rstd[:]], [rstd[:]])
            for c in range(NCH):
                xt = ph.xn[k % 4]; k += 1
                kb.dma(kb.sp, xt[:], xsrc[c, :, tsl])
                E = kb.dve
                if out_dram is None:
                    kb.stt(E, HT[:, c, j * TT:(j + 1) * TT], xt[:], self.vcol(gname, c), rstd[:], ALU.mult, ALU.mult)
                else:
                    kb.stt(E, xt[:], xt[:], self.vcol(gname, c), rstd[:], ALU.mult, ALU.mult)
                    kb.dma(kb.sp, out_dram[c, :, tsl], xt[:])

    def norm_tiles(self, ph):
        ph.xn = [ph.sb([128, TT], F32, "xn") for _ in range(4)]
        ph.sq = [ph.sb([128, TT], F32, "sq") for _ in range(2)]
        ph.rstd = ph.sb([128, TT], F32, "rstd")

    def stream(self, ph, slots, loads, compute):
        kb = self.kb
        ns = len(slots)
        n = len(loads)
        for i in range(min(ns - 1, n)):
            loads[i](slots[i % ns])
        for i in range(n):
            if i + ns - 1 < n:
                loads[i + ns - 1](slots[(i + ns - 1) % ns])
            compute(i, slots[i % ns])

    def wload(self, slot_view, wap):
        kb = self.kb
        kb.dma(kb.pool, slot_view, View(self.wtile, wap), sb=slot_view.tile)

    def ffn(self, li):
        kb = self.kb
        Wgu = self.W["ffn_w_gu"][li].rearrange("(kc p) n -> p kc n", p=128)
        Wdn = self.W["ffn_w_down"][li].rearrange("(kc p) n -> p kc n", p=128)
        for s in range(NPASS):
            t0 = s * T
            ph = Phase(kb, "ffn%d_%d" % (li, s))
            HT = ph.sb([128, NCH, T], BF16, "HT")
            ACT = ph.sb([128, NFC, T], BF16, "ACT")
            slots = [ph.sb([128, NFC * 128], BF16, "ws") for _ in range(3)]
            G = [ph.sb([128, T + 2], F32, "G") for _ in range(2)]
            Cv = [ph.sb([128, T], F32, "Cv") for _ in range(2)]
            xr = [ph.sb([128, T], F32, "xr") for _ in range(2)]
            pss = [ph.ps([128, TT], F32, "ps") for _ in range(8)]
            self.norm_tiles(ph)
            self.norm(ph, self.xcur, t0, "ffn_norm%d" % li, HT, pss[0:2])
            def mk_load_gu(f):
                def ld(slot):
                    sv = slot.t[:, 0:NCH * 256].rearrange("p (k n) -> p k n", k=NCH)
                    self.wload(slot.v(sv[:, :, 0:128]), Wgu[:, :, f * 128:(f + 1) * 128])
                    self.wload(slot.v(sv[:, :, 128:256]), Wgu[:, :, FF + f * 128:FF + (f + 1) * 128])
                return ld

            def comp_gu(f, slot):
                sv = slot.t[:, 0:NCH * 256].rearrange("p (k n) -> p k n", k=NCH)
                g = G[f % 2]; cv = Cv[f % 2]
                if s == 0:
                    kb.memset(kb.dve, g[:, 0:2], 0.0)
                else:
                    kb.copy(kb.dve, g[:, 0:2], self.ffn_halo[:, f, :])
                pg = [pss[(4 * f + j) % 8] for j in range(2)]
                pu = [pss[(4 * f + 2 + j) % 8] for j in range(2)]
                for j in range(NTT):
                    for c in range(NCH):
                        kb.mm(pg[j][:], slot.v(sv[:, c, 0:128]), HT[:, c, j * TT:(j + 1) * TT], start=(c == 0), stop=(c == NCH - 1))
                    kb.copy(kb.act, g[:, 2 + j * TT: 2 + (j + 1) * TT], pg[j][:])
                for j in range(NTT):
                    for c in range(NCH):
                        kb.mm(pu[j][:], slot.v(sv[:, c, 128:256]), HT[:, c, j * TT:(j + 1) * TT], start=(c == 0), stop=(c == NCH - 1))
                kb.copy(kb.dve, self.ffn_halo[:, f, :], g[:, T:T + 2])
                w = lambda j: self.vcol("fcw%d_%d" % (li, j), f)
                kb.ts(kb.dve, cv[:], g[:, 2:T + 2], w(2), self.vcol("fcb%d" % li, f), ALU.mult, ALU.add)
                kb.stt(kb.dve, cv[:], g[:, 1:T + 1], w(1), cv[:], ALU.mult, ALU.add)
                kb.stt(kb.dve, cv[:], g[:, 0:T], w(0), cv[:], ALU.mult, ALU.add)
                kb.activation(cv[:], cv[:], AF.Silu)
                for j in range(NTT):
                    kb.tt(kb.dve, ACT[:, f, j * TT:(j + 1) * TT], cv[:, j * TT:(j + 1) * TT], pu[j][:], ALU.mult)
            self.stream(ph, slots, [mk_load_gu(f) for f in range(NFC)], comp_gu)

            def mk_load_dn(d):
                def ld(slot):
                    sv = slot.t[:, :].rearrange("p (k n) -> p k n", k=NFC)
                    self.wload(slot.v(sv), Wdn[:, :, d * 128:(d + 1) * 128])
                return ld

            def comp_dn(d, slot):
                sv = slot.t[:, :].rearrange("p (k n) -> p k n", k=NFC)
                x = xr[d % 2]
                kb.dma(kb.sp, x[:], self.xcur[d, :, t0:t0 + T])
                for j in range(NTT):
                    p = pss[(2 * d + j) % 8]
                    for c in range(NFC):
                        kb.mm(p[:], slot.v(sv[:, c, :]), ACT[:, c, j * TT:(j + 1) * TT], start=(c == 0), stop=(c == NFC - 1))
                    kb.tt(kb.dve, x[:, j * TT:(j + 1) * TT], x[:, j * TT:(j + 1) * TT], p[:], ALU.add)
                kb.dma(kb.sp, self.xT[d, :, t0:t0 + T], x[:])
            self.stream(ph, slots, [mk_load_dn(d) for d in range(NCH)], comp_dn)
            ph.close()
        self.xcur = self.xT

    def ple(self, li):
        kb = self.kb
        Wg = self.W["ple_w_gate"][li].rearrange("(kc p) n -> p kc n", p=128)
        Wp = self.W["ple_w_proj"][li].rearrange("(kc p) n -> p kc n", p=128)
        pT = Tile(self.pT, "pT")
        ones = self.consts[:, C_ONES:C_ONES + 128]
        for s in range(NPASS):
            t0 = s * T
            ph = Phase(kb, "ple%d_%d" % (li, s))
            HT = ph.sb([128, NCH, T], BF16, "HT")
            PT = ph.sb([128, 2, T], BF16, "PT")
            X = [ph.sb([128, T], F32, "X") for _ in range(NCH)]
            slots = [ph.sb([128, 18 * 128], BF16, "ws") for _ in range(3)]
            sg = [ph.sb([128, T], F32, "sg") for _ in range(2)]
            sq = [ph.sb([128, TT], F32, "sq") for _ in range(2)]
            rstd = [ph.sb([128, TT], F32, "rstd") for _ in range(NTT)]
            pss = [ph.ps([128, TT], F32, "ps") for _ in range(8)]
            for c in range(2):
                kb.dma(kb.pool, PT[:, c, :], pT[li, c, :, t0:t0 + T])
            for c in range(NCH):
                kb.dma(kb.sp, X[c][:], self.xcur[c, :, t0:t0 + T])
            for j in range(NTT):
                sl = slice(j * TT, (j + 1) * TT)
                for c in range(NCH):
                    kb.activation(sq[c % 2][:], X[c][:, sl], AF.Square)
                    kb.mm(pss[j][:], ones, sq[c % 2][:], start=(c == 0), stop=(c == NCH - 1))
                kb.activation(rstd[j][:], pss[j][:], AF.Sqrt, bias=self.consts[:, C_EPS:C_EPS + 1])
                kb.op(kb.dve, lambda r_=rstd[j]: self.nc.vector.reciprocal(r_.t[:], r_.t[:]), [rstd[j][:]], [rstd[j][:]])
                for c in range(NCH):
                    kb.stt(kb.dve, HT[:, c, sl], X[c][:, sl], self.vcol("ple_norm%d" % li, c), rstd[j][:], ALU.mult, ALU.mult)

            def mk_load(d):
                def ld(slot):
                    sv = slot.t[:, :].rearrange("p (k n) -> p k n", k=18)
                    self.wload(slot.v(sv[:, 0:16, :]), Wg[:, :, d * 128:(d + 1) * 128])
                    self.wload(slot.v(sv[:, 16:18, :]), Wp[:, :, d * 128:(d + 1) * 128])
                return ld

            def comp(d, slot):
                sv = slot.t[:, :].rearrange("p (k n) -> p k n", k=18)
                x = X[d]; g = sg[d % 2]
                for j in range(NTT):
                    pa = pss[(4 * d + j) % 8]; pb = pss[(4 * d + 2 + j) % 8]
                    sl = slice(j * TT, (j + 1) * TT)
                    for c in range(NCH):
                        kb.mm(pa[:], slot.v(sv[:, c, :]), HT[:, c, sl], start=(c == 0), stop=(c == NCH - 1))
                    for c in range(2):
                        kb.mm(pb[:], slot.v(sv[:, 16 + c, :]), PT[:, c, sl], start=(c == 0), stop=(c == 1))
                    kb.activation(g[:, sl], pa[:], AF.Sigmoid)
                    kb.tt(kb.dve, g[:, sl], g[:, sl], pb[:], ALU.mult)
                    kb.tt(kb.dve, x[:, sl], x[:, sl], g[:, sl], ALU.add)
                kb.dma(kb.sp, self.xT[d, :, t0:t0 + T], x[:])
            self.stream(ph, slots, [mk_load(d) for d in range(NCH)], comp)
            ph.close()
        self.xcur = self.xT

    def proj_to_dram(self, ph, HT, Wv, KC, ntiles, colfn, dstfn, pss, slots, stage, scalefn=None, dil=None):
        kb = self.kb
        def mk_load(i):
            def ld(slot):
                sv = slot.t[:, 0:KC * 128].rearrange("p (k n) -> p k n", k=KC)
                self.wload(slot.v(sv), Wv[:, :, colfn(i):colfn(i) + 128])
            return ld

        def comp(i, slot):
            sv = slot.t[:, 0:KC * 128].rearrange("p (k n) -> p k n", k=KC)
            st = stage[i % len(stage)]
            d = dil(i) if dil else 1
            for j in range(NTT):
                p = pss[(2 * i + j) % len(pss)]
                for c in range(KC):
                    kb.mm(p[:], slot.v(sv[:, c, :]), HT[:, c, j * TT:(j + 1) * TT], start=(c == 0), stop=(c == KC - 1))
                sc = scalefn(i) if scalefn else 1.0
                if d == 1:
                    kb.activation(st[:, j * TT:(j + 1) * TT], p[:], AF.Copy, scale=sc)
                else:
                    o = st.t[:, :].rearrange("p (r l) -> p r l", r=d)[:, :, j * TT // d:(j + 1) * TT // d]
                    i_ = p.t[:, :].rearrange("p (l r) -> p r l", r=d)
                    kb.activation(st.v(o), p.v(i_), AF.Copy, scale=sc)
            dstfn(i, st)
        self.stream(ph, slots, [mk_load(i) for i in range(ntiles)], comp)

    def proj_residual(self, ph, HT, Wv, KC, t0, pss, slots, xr):
        kb = self.kb
        def mk_load(d):
            def ld(slot):
                sv = slot.t[:, 0:KC * 128].rearrange("p (k n) -> p k n", k=KC)
                self.wload(slot.v(sv), Wv[:, :, d * 128:(d + 1) * 128])
            return ld

        def comp(d, slot):
            sv = slot.t[:, 0:KC * 128].rearrange("p (k n) -> p k n", k=KC)
            x = xr[d % 2]
            kb.dma(kb.sp, x[:], self.xcur[d, :, t0:t0 + T])
            for j in range(NTT):
                p = pss[(2 * d + j) % len(pss)]
                for c in range(KC):
                    kb.mm(p[:], slot.v(sv[:, c, :]), HT[:, c, j * TT:(j + 1) * TT], start=(c == 0), stop=(c == KC - 1))
                kb.tt(kb.dve, x[:, j * TT:(j + 1) * TT], x[:, j * TT:(j + 1) * TT], p[:], ALU.add)
            kb.dma(kb.sp, self.xT[d, :, t0:t0 + T], x[:])
        self.stream(ph, slots, [mk_load(d) for d in range(NCH)], comp)

    def attn(self, j_, li, parts="ABC", nheads=16):
        kb = self.kb; nc = self.nc
        DIL = (1, 4, 16)
        slopes = [[2.0 ** (-8.0 * (g * 16 + h + 1) / 48.0) for h in range(16)] for g in range(3)]
        if not hasattr(self, "qkvT"):
            self.qkvT = Tile(nc.dram_tensor("qkvT_s", [144, 128, S], BF16).ap(), "qkvT")
            self.oT = Tile(nc.dram_tensor("oT_s", [NCH, 128, S], BF16).ap(), "oT")
        Wqkv = self.W["attn_w_qkv"][j_].rearrange("(kc p) n -> p kc n", p=128)
        Wo = self.W["attn_w_o"][j_].rearrange("(kc p) n -> p kc n", p=128)
        for s in (range(NPASS) if "A" in parts else []):
            t0 = s * T
            ph = Phase(kb, "atA%d_%d" % (li, s))
            HT = ph.sb([128, NCH, T], BF16, "HT")
            slots = [ph.sb([128, NCH * 128], BF16, "ws") for _ in range(4)]
            stage = [ph.sb([128, T], BF16, "st") for _ in range(3)]
            pss = [ph.ps([128, TT], F32, "ps") for _ in range(8)]
            self.norm_tiles(ph)
            self.norm(ph, self.xcur, t0, "attn_norm%d" % j_, HT, pss[0:2])

            def dst(i, st):
                d = DIL[i // 48]; L = S // d; Lh = T // d
                dv = self.qkvT.t[i].rearrange("p (r l) -> p r l", r=d)[:, :, s * Lh:(s + 1) * Lh]
                sv = st.t[:, :].rearrange("p (r l) -> p r l", r=d)
                kb.dma(kb.sp, self.qkvT.v(dv), st.v(sv))
            self.proj_to_dram(ph, HT, Wqkv, NCH, 144, lambda i: i * 128, dst, pss, slots, stage,
                              scalefn=lambda i: (128.0 ** -0.5 if (i // 16) % 3 == 0 else 1.0),
                              dil=lambda i: DIL[i // 48])
            ph.close()
        ph = Phase(kb, "atB%d" % li)
        ident = self.consts[:, C_ID:C_ID + 128]
        ones1 = ph.sb([128, 128], F32, "ones1"); kb.memset(kb.dve, ones1[:], 1.0)
        qkv = [[ph.sb([128, S], BF16, "qkv") for _ in range(9)] for _ in range(2)]
        OgT = [ph.sb([128, S], F32, "OgT") for _ in range(3)]
        LgT2 = [[ph.sb([128, S], F32, "LgT") for _ in range(3)] for _ in range(2)]
        Vtm = [ph.sb([128, 16, 128], BF16, "Vtm") for _ in range(2)]
        Sb = [ph.sb([128, 256], F32, "Sb") for _ in range(3)]
        Pb = [ph.sb([128, 256], BF16, "Pb") for _ in range(3)]
        PT = [ph.sb([128, 2, 128], BF16, "PT") for _ in range(3)]
        Otm = [ph.sb([128, 128], F32, "Otm") for _ in range(3)]
        Dg = [ph.sb([128, 128], F32, "Dg") for _ in range(3)]
        m3 = ph.sb([128, S], F32, "m3"); ssum = ph.sb([128, S], F32, "ssum")
        Ob = [ph.sb([128, S], BF16, "Ob") for _ in range(2)]
        ps_s = [ph.ps([128, 256], F32, "pS") for _ in range(2)]
        ps_t = [ph.ps([128, 2, 128], BF16, "pT") for _ in range(2)]
        ps_v = [ph.ps([128, 128], BF16, "pV") for _ in range(1)]
        ps_o = [ph.ps([128, 128], F32, "pO") for _ in range(1)]
        ps_ot = [ph.ps([128, 128], F32, "pOT") for _ in range(1)]
        ps_l = [ph.ps([128, 128], F32, "pL") for _ in range(1)]
        sm = [ph.sb([128, 8], F32, "sm") for _ in range(10)]
        units = []
        for h in (range(nheads) if "B" in parts else []):
            for g in range(3):
                for i in range(16):
                    units.append((h, g, i))

        def load_head(h):
            bufs = qkv[h % 2]
            for g in range(3):
                for jj in range(3):
                    kb.dma(kb.sp, bufs[g * 3 + jj][:], self.qkvT[(g * 3 + jj) * 16 + h])

        def prep_group(h, g):
            VT_ = qkv[h % 2][g * 3 + 2]
            vt = Vtm[(h * 3 + g) % 2]
            for i in range(16):
                pv = ps_v[0]
                kb.transpose(pv[:], VT_[:, i * 128:(i + 1) * 128], self.identb[:])
                kb.copy(kb.act, vt[:, i, :], pv[:])

        def geom(u):
            h, g, i = units[u]
            d = DIL[g]; nb = (S // d) // 128
            n = i % nb; nk = 2 if n > 0 else 1
            return h, g, i, d, nb, n, nk

        def st0(u):
            h, g, i, d, nb, n, nk = geom(u)
            if g == 0 and i == 0 and h + 1 < nheads: load_head(h + 1)
            if i == 0: prep_group(h, g)
            QT_, KT_ = qkv[h % 2][g * 3], qkv[h % 2][g * 3 + 1]
            k0 = (i - 1) * 128 if n > 0 else i * 128
            kb.mm(ps_s[u % 2][:, 0:nk * 128], QT_[:, i * 128:(i + 1) * 128], KT_[:, k0:k0 + nk * 128])

        def st1(u):
            h, g, i, d, nb, n, nk = geom(u)
            W_ = nk * 128; b = u % 3; s4 = sm[u % 10]; pS = ps_s[u % 2]
            coef = slopes[g][h] * d
            dv = self.consts[:, C_D1:C_D1 + 256] if nk == 2 else self.consts[:, C_D1 + 128:C_D1 + 256]
            kb.stt(kb.dve, Sb[b][:, 0:W_], dv, coef, pS[:, 0:W_], ALU.mult, ALU.add)
            kb.op(kb.dve, lambda a=s4, sb_=Sb[b], w_=W_: nc.vector.tensor_reduce(a.t[:, 1:2], sb_.t[:, 0:w_], AX.X, ALU.max, negate=True), [Sb[b][:, 0:W_]], [s4[:, 1:2]])

        def st2(u):
            h, g, i, d, nb, n, nk = geom(u)
            W_ = nk * 128; b = u % 3; s4 = sm[u % 10]
            kb.activation(Pb[b][:, 0:W_], Sb[b][:, 0:W_], AF.Exp, bias=s4[:, 1:2], accum=s4[:, 2:3])
            kb.activation(s4[:, 3:4], s4[:, 2:3], AF.Ln)

        def st3(u):
            h, g, i, d, nb, n, nk = geom(u)
            b = u % 3; pT = ps_t[u % 2]; s4 = sm[u % 10]
            for kt in range(nk):
                kb.transpose(pT[:, kt, :], Pb[b][:, kt * 128:(kt + 1) * 128], self.identb[:])
            kb.tt(kb.dve, s4[:, 4:5], s4[:, 3:4], s4[:, 1:2], ALU.subtract)
            kb.op(kb.dve, lambda a=s4: nc.vector.reciprocal(a.t[:, 5:6], a.t[:, 2:3]), [s4[:, 2:3]], [s4[:, 5:6]])

        def st4(u):
            h, g, i, d, nb, n, nk = geom(u)
            b = u % 3; pT = ps_t[u % 2]; s4 = sm[u % 10]
            kb.copy(kb.act, PT[b][:, 0:nk, :], pT[:, 0:nk, :])
            kb.activation(Dg[b][:], ident, AF.Copy, scale=s4[:, 4:5])

        def st5(u):
            h, g, i, d, nb, n, nk = geom(u)
            b = u % 3; pO = ps_o[0]; vt = Vtm[(h * 3 + g) % 2]
            for kt in range(nk):
                kb.mm(pO[:], PT[b][:, kt, :], vt[:, i - (nk - 1) + kt, :], start=(kt == 0), stop=(kt == nk - 1))
            kb.mm(ps_l[0][:], ones1[:], Dg[b][:])

        def st6(u):
            h, g, i, d, nb, n, nk = geom(u)
            b = u % 3; s4 = sm[u % 10]; pO = ps_o[0]
            kb.ts(kb.dve, Otm[b][:], pO[:], s4[:, 5:6], None, ALU.mult)
            r = i // nb
            c0 = r + d * 128 * n
            LgT = LgT2[h % 2]
            kb.copy(kb.dve, LgT[g].v(LgT[g].t[:, c0:c0 + d * 127 + 1:d]), ps_l[0][:])

        def st7(u):
            b = u % 3
            kb.transpose(ps_ot[0][:], Otm[b][:], ident)

        def st8(u):
            h, g, i, d, nb, n, nk = geom(u)
            r = i // nb
            c0 = r + d * 128 * n
            kb.copy(kb.act, OgT[g].v(OgT[g].t[:, c0:c0 + d * 127 + 1:d]), ps_ot[0][:])
            if g == 2 and i == 15: combine(h)

        def combine(h):
            LgT = LgT2[h % 2]
            kb.tt(kb.dve, m3[:], LgT[0][:], LgT[1][:], ALU.max)
            kb.tt(kb.dve, m3[:], m3[:], LgT[2][:], ALU.max)
            for g in range(3):
                kb.tt(kb.dve, LgT[g][:], LgT[g][:], m3[:], ALU.subtract)
                kb.activation(LgT[g][:], LgT[g][:], AF.Exp)
            kb.tt(kb.dve, ssum[:], LgT[0][:], LgT[1][:], ALU.add)
            kb.tt(kb.dve, ssum[:], ssum[:], LgT[2][:], ALU.add)
            kb.op(kb.dve, lambda: nc.vector.reciprocal(ssum.t[:], ssum.t[:]), [ssum[:]], [ssum[:]])
            for g in range(3):
                kb.tt(kb.dve if g != 1 else kb.dve, OgT[g][:], OgT[g][:], LgT[g][:], ALU.mult)
            kb.tt(kb.dve, OgT[0][:], OgT[0][:], OgT[1][:], ALU.add)
            kb.tt(kb.dve, OgT[0][:], OgT[0][:], OgT[2][:], ALU.add)
            ob = Ob[h % 2]
            kb.tt(kb.dve, ob[:], OgT[0][:], ssum[:], ALU.mult)
            kb.dma(kb.sp, self.oT[h], ob[:])

        stages = [st0, st1, st2, st3, st4, st5, st6, st7, st8]
        if units: load_head(0)
        for tick in range(len(units) + len(stages) - 1):
            for si in reversed(range(len(stages))):
                u = tick - si
                if 0 <= u < len(units): stages[si](u)
        ph.close()
        for s in (range(NPASS) if "C" in parts else []):
            t0 = s * T
            ph = Phase(kb, "atC%d_%d" % (li, s))
            HT = ph.sb([128, NCH, T], BF16, "HT")
            slots = [ph.sb([128, NCH * 128], BF16, "ws") for _ in range(3)]
            xr = [ph.sb([128, T], F32, "xr") for _ in range(2)]
            pss = [ph.ps([128, TT], F32, "ps") for _ in range(8)]
            for c in range(NCH):
                kb.dma(kb.sp, HT[:, c, :], self.oT[c, :, t0:t0 + T])
            self.proj_residual(ph, HT, Wo, NCH, t0, pss, slots, xr)
            ph.close()
        self.xcur = self.xT

    def conv(self):
        kb = self.kb
        Win = self.W["conv_w_in"][0].rearrange("(kc p) n -> p kc n", p=128)
        Wout = self.W["conv_w_out"][0].rearrange("(kc p) n -> p kc n", p=128)
        if not hasattr(self, "cv_halo"):
            self.cv_halo = self.pp.sb([128, NCH, 2], F32, "chalo")
        for s in range(NPASS):
            t0 = s * T
            ph = Phase(kb, "cv%d" % s)
            HT = ph.sb([128, NCH, T], BF16, "HT")
            ZT = ph.sb([128, NCH, T], BF16, "ZT")
            slots = [ph.sb([128, NCH * 3 * 128], BF16, "ws") for _ in range(3)]
            G = [ph.sb([128, T + 2], F32, "G") for _ in range(2)]
            Cv = [ph.sb([128, T], F32, "Cv") for _ in range(2)]
            Cs = [ph.sb([128, T], F32, "Cs") for _ in range(2)]
            xr = [ph.sb([128, T], F32, "xr") for _ in range(2)]
            pss = [ph.ps([128, TT], F32, "ps") for _ in range(8)]
            self.norm_tiles(ph)
            self.norm(ph, self.xcur, t0, "conv_norm", HT, pss[0:2])

            def mk_load(f):
                def ld(slot):
                    sv = slot.t[:, :].rearrange("p (k n) -> p k n", k=NCH)
                    for q in range(3):
                        self.wload(slot.v(sv[:, :, q * 128:(q + 1) * 128]), Win[:, :, q * D + f * 128:q * D + (f + 1) * 128])
                return ld

            def comp(f, slot):
                sv = slot.t[:, :].rearrange("p (k n) -> p k n", k=NCH)
                g = G[f % 2]; cv = Cv[f % 2]; cs = Cs[f % 2]
                if s == 0:
                    kb.memset(kb.dve, g[:, 0:2], 0.0)
                else:
                    kb.copy(kb.dve, g[:, 0:2], self.cv_halo[:, f, :])
                for j in range(NTT):
                    sl = slice(j * TT, (j + 1) * TT)
                    pc_, pu_, pb_ = pss[(3 * j) % 8], pss[(3 * j + 1) % 8], pss[(3 * j + 2) % 8]
                    for q, p in ((1, pc_), (2, pu_), (0, pb_)):
                        for c in range(NCH):
                            kb.mm(p[:], slot.v(sv[:, c, q * 128:(q + 1) * 128]), HT[:, c, sl], start=(c == 0), stop=(c == NCH - 1))
                    kb.copy(kb.act, cs[:, sl], pc_[:])
                    kb.tt(kb.dve, g[:, 2 + j * TT:2 + (j + 1) * TT], cs[:, sl], pu_[:], ALU.mult)
                    kb.copy(kb.act, cs[:, sl], pb_[:])
                kb.copy(kb.dve, self.cv_halo[:, f, :], g[:, T:T + 2])
                w = lambda j: self.vcol("conv_w%d" % j, f)
                kb.ts(kb.dve, cv[:], g[:, 2:T + 2], w(2), None, ALU.mult)
                kb.stt(kb.dve, cv[:], g[:, 1:T + 1], w(1), cv[:], ALU.mult, ALU.add)
                kb.stt(kb.dve, cv[:], g[:, 0:T], w(0), cv[:], ALU.mult, ALU.add)
                kb.tt(kb.dve, ZT[:, f, :], cv[:], cs[:], ALU.mult)
            self.stream(ph, slots, [mk_load(f) for f in range(NCH)], comp)
            self.proj_residual(ph, ZT, Wout, NCH, t0, pss, slots, xr)
            ph.close()
        self.xcur = self.xT


    def rwkv(self, nchunks=16, parts="ABC", dbg=9):
        kb = self.kb; nc = self.nc
        scr = lambda name, dt=F32: Tile(nc.dram_tensor(name, [NCH, 128, S], dt).ap(), name)
        R_, K_, V_, LW_, A_, G_ = [scr("rw_%s" % n) for n in "rkvwag"]
        YG = scr("rw_yg", BF16)
        hlast = self.pp.sb([128, NCH, 1], BF16, "hlast")
        negw0 = self.pp.sb([128, NCH], F32, "negw0")
        kb.ts(kb.dve, negw0[:], self.vecs[:, VCOL["w0"]:VCOL["w0"] + NCH], -1.0, None, ALU.mult)
        rearr = lambda w: w.rearrange("(kc p) n -> p kc n", p=128)
        for s in (range(NPASS) if "A" in parts else []):
            t0 = s * T
            ph = Phase(kb, "rwA%d" % s)
            HT = ph.sb([128, NCH, T], BF16, "HT"); XX = ph.sb([128, NCH, T], BF16, "XX"); Xn = ph.sb([128, NCH, T], BF16, "Xn")
            slots = [ph.sb([128, NCH * 256], BF16, "ws") for _ in range(3)]
            stage = [ph.sb([128, T], F32, "st") for _ in range(2)]
            T1 = ph.sb([128, 2, T], BF16, "T1")
            W2 = ph.sb([128, 2, D], BF16, "W2")
            pss = [ph.ps([128, TT], F32, "ps") for _ in range(8)]
            self.norm_tiles(ph)
            self.norm(ph, self.xcur, t0, "rwkv_norm", HT, pss[0:2])
            for c in range(NCH):
                if s == 0:
                    kb.ts(kb.dve, XX[:, c, 0:1], HT[:, c, 0:1], -1.0, None, ALU.mult)
                else:
                    kb.tt(kb.dve, XX[:, c, 0:1], hlast[:, c, :], HT[:, c, 0:1], ALU.subtract)
                kb.tt(kb.dve, XX[:, c, 1:T], HT[:, c, 0:T - 1], HT[:, c, 1:T], ALU.subtract)
            for c in range(NCH):
                kb.copy(kb.dve, hlast[:, c, :], HT[:, c, T - 1:T])

            def mix(n):
                for c in range(NCH):
                    kb.stt(kb.dve, Xn[:, c, :], XX[:, c, :], self.vcol("mu%d" % n, c), HT[:, c, :], ALU.mult, ALU.add)
            for n, dst in enumerate([R_, K_, V_]):
                mix(n)
                self.proj_to_dram(ph, Xn, rearr(self.W["rwkv_w_rkv"][0][n]), NCH, 16, lambda i: i * 128,
                                  lambda i, st, dst=dst: kb.dma(kb.sp, dst[i, :, t0:t0 + T], st[:]), pss, slots, stage)

            def lora(n, w1name, w2name, rnk, func1, epi, dst):
                mix(n)
                nm = (rnk + 127) // 128; rp = min(rnk, 128)
                sl = slots[0]
                sv = sl.t[:, 0:NCH * rnk].rearrange("p (k n) -> p k n", k=NCH)
                self.wload(sl.v(sv), rearr(self.W[w1name][0]))
                w2v = self.W[w2name][0].rearrange("(m p) n -> p m n", p=rp)
                self.wload(W2[0:rp, 0:nm, :], w2v)
                k = 0
                for m in range(nm):
                    for j in range(NTT):
                        p = pss[k % 8]; k += 1
                        for c in range(NCH):
                            kb.mm(p[0:rp, :], sl.v(sv[:, c, m * 128:m * 128 + rp]), Xn[:, c, j * TT:(j + 1) * TT], start=(c == 0), stop=(c == NCH - 1))
                        kb.activation(T1[0:rp, m, j * TT:(j + 1) * TT], p[0:rp, :], func1)
                for i in range(NCH):
                    st = stage[i % 2]
                    for j in range(NTT):
                        p = pss[k % 8]; k += 1
                        for m in range(nm):
                            kb.mm(p[:], W2[0:rp, m, i * 128:(i + 1) * 128], T1[0:rp, m, j * TT:(j + 1) * TT], start=(m == 0), stop=(m == nm - 1))
                        epi(i, st[:, j * TT:(j + 1) * TT], p)
                    kb.dma(kb.sp, dst[i, :, t0:t0 + T], st[:])

            def epi_w(i, o, p):
                kb.activation(o, p[:], AF.Exp, bias=negw0[:, i:i + 1], scale=-1.0)
                kb.ts(kb.dve, o, o, 1.0, None, ALU.add)
                kb.op(kb.dve, lambda: nc.vector.reciprocal(o.ap, o.ap), [o], [o])
                kb.ts(kb.dve, o, o, -0.6065306597126334, None, ALU.mult)
            lora(3, "rwkv_w_w1", "rwkv_w_w2", 96, AF.Tanh, epi_w, LW_)
            lora(4, "rwkv_w_a1", "rwkv_w_a2", 96, AF.Copy, lambda i, o, p: kb.activation(o, p[:], AF.Sigmoid, bias=self.vcol("a0", i)), A_)
            lora(5, "rwkv_w_g1", "rwkv_w_g2", 256, AF.Sigmoid, lambda i, o, p: kb.copy(kb.act, o, p[:]), G_)
            ph.close()
        NBK = S // 128
        ph = Phase(kb, "rwB")
        cst = lambda c0: self.consts[:, c0:c0 + 128]
        ident, SU, IU, SL, BD, BD1 = cst(C_ID), cst(C_SU), cst(C_IU), cst(C_SL), cst(C_BD), cst(C_BD1)
        bufs = [ph.sb([128, S], F32, "b%d" % i) for i in range(11)]
        r, k, v, lw, a, g, B1, B2, B3, B4, B5 = bufs
        gC = ph.sb([128, NBK], F32, "gC")
        GB = 4
        TMs = [[ph.sb([128, GB, 128], F32, n) for n in ("Vtm", "VP0", "VP1", "Btm", "Ktm")] for _ in range(2)]
        UM = [[[ph.sb([128, 128], F32, "um") for _ in range(4)] for _ in range(2 * GB)] for _ in range(2)]
        QQ = [[ph.sb([128, 128], F32, "qq") for _ in range(6)] for _ in range(2 * GB)]
        STP = ph.sb([128, 128], F32, "STP"); RH = ph.sb([128, 128], F32, "RH"); SAB = ph.sb([128, 128], F32, "SAB")
        SAP = [ph.sb([128, 128], F32, "SAP") for _ in range(2)]
        t1 = ph.sb([128, 128], F32, "t1")
        ygb = ph.sb([128, S], BF16, "ygb")
        pg = [ph.ps([128, 128], F32, "pg") for _ in range(2)]
        pd = [ph.ps([128, 128], F32, "pd") for _ in range(3)]
        ptr = ph.ps([128, 128], F32, "ptr")
        pq = [ph.ps([128, 512], F32, "pq") for _ in range(2)]
        for q in range(2):
            kb.memset(kb.dve, TMs[q][1][:], 0.0); kb.memset(kb.dve, TMs[q][2][:], 0.0)
        kb.memset(kb.dve, SAP[0][:], 0.0); kb.memset(kb.dve, SAP[1][:], 0.0)
        recip = lambda o: kb.op(kb.dve, lambda: nc.vector.reciprocal(o.ap, o.ap), [o], [o])
        for c in (range(nchunks) if "B" in parts else []):
            for buf, src in ((r, R_), (k, K_), (v, V_), (lw, LW_), (a, A_), (g, G_)):
                kb.dma(kb.sp, buf[:], src[c])
            kb.ts(kb.dve, B1[:], k[:], self.vcol("k_k", c), None, ALU.mult)
            kb.activation(B2[:], B1[:], AF.Square)
            for j in range(4):
                sl = slice(j * 512, (j + 1) * 512)
                kb.mm(pq[j % 2][:], BD1, B2[:, sl])
                kb.activation(B3[:, sl], pq[j % 2][:], AF.Sqrt)
            kb.ts(kb.dve, B3[:], B3[:], 1e-12, None, ALU.max)
            recip(B3[:])
            kb.tt(kb.dve, B1[:], B1[:], B3[:], ALU.mult)
            kb.ts(kb.dve, B2[:], a[:], -1.0, self.vcol("k_a", c), ALU.add, ALU.mult)
            kb.stt(kb.dve, k[:], B2[:], 1.0, k[:], ALU.add, ALU.mult)
            kb.tt(kb.dve, a[:], B1[:], a[:], ALU.mult)
            v3 = lambda t_: t_.t[:, :].rearrange("p (n q) -> p n q", q=128)
            src = lw; pp2 = [B2, B3]; sh = 1; q = 0
            while sh < 128:
                dst = pp2[q % 2]; q += 1
                kb.tt(kb.dve, dst.v(v3(dst)[:, :, sh:128]), src.v(v3(src)[:, :, sh:128]), src.v(v3(src)[:, :, 0:128 - sh]), ALU.add)
                kb.copy(kb.dve, dst.v(v3(dst)[:, :, 0:sh]), src.v(v3(src)[:, :, 0:sh]))
                src = dst; sh *= 2
            cum = src; oth = B3 if cum is B2 else B2
            kb.activation(B4[:], cum[:], AF.Exp)
            kb.activation(B5[:], cum[:], AF.Exp, scale=-1.0)
            kb.tt(kb.dve, lw[:], cum[:], lw[:], ALU.subtract)
            kb.activation(lw[:], lw[:], AF.Exp)
            kb.copy(kb.dve, gC[:], B4.v(v3(B4)[:, :, 127]))
            kb.stt(kb.dve, B1[:], B1[:], -1.0, lw[:], ALU.mult, ALU.mult)
            kb.tt(kb.dve, a[:], a[:], B5[:], ALU.mult)
            kb.tt(kb.dve, B5[:], k[:], B5[:], ALU.mult)
            kb.tt(kb.dve, B4[:], r[:], B4[:], ALU.mult)
            At, Bt, Kt, Rt, Y = B1, a, B5, B4, lw
            kb.memset(kb.dve, STP[:], 0.0)
            def front_a(gb, sl_):
                TM = TMs[sl_]; UMs = UM[sl_]
                Vtm, VP0, VP1, Btm, Ktm = TM
                for bi in range(GB):
                    n = gb * GB + bi; bs = slice(n * 128, (n + 1) * 128)
                    kb.transpose(ptr[:], v[:, bs], ident)
                    kb.copy(kb.act, Vtm[:, bi, :], ptr[:])
                    kb.copy(kb.dve, VP0[:, bi, 0:64], Vtm[:, bi, 0:64])
                    kb.copy(kb.dve, VP1[:, bi, 64:128], Vtm[:, bi, 64:128])
                    kb.transpose(ptr[:], Bt[:, bs], ident)
                    kb.copy(kb.dve, Btm[:, bi, :], ptr[:])
                    kb.transpose(ptr[:], Kt[:, bs], ident)
                    kb.copy(kb.act, Ktm[:, bi, :], ptr[:])
                gi_ = 0
                for u in range(2 * GB):
                    bi, hh = u // 2, u % 2
                    n = gb * GB + bi; bs = slice(n * 128, (n + 1) * 128); hp = slice(64 * hh, 64 * hh + 64)
                    Uak, Urb, Urk, XT = UMs[u]; Qa, Qb, La, Lb, Pa, Pb = QQ[u]
                    for (lt, rt, msk, dst) in ((Bt, At, SU, Qa), (At, Bt, SL, La), (Kt, At, SU, Uak), (Bt, Rt, IU, Urb), (Kt, Rt, IU, Urk)):
                        p = pg[gi_ % 2]; gi_ += 1
                        kb.mm(p[:], lt[hp, bs], rt[hp, bs])
                        kb.tt(kb.dve, dst[:], p[:], msk, ALU.mult)
                    kb.tt(kb.dve, Pa[:], Qa[:], ident, ALU.add)

            dstate = {"di": 0}

            def dbl_level(sl_, lev):
                for u in range(2 * GB):
                    src_i = lev % 2
                    Qc, Lc, Pc = QQ[u][0 + src_i], QQ[u][2 + src_i], QQ[u][4 + src_i]
                    Qn, Ln, Pn = QQ[u][1 - src_i], QQ[u][3 - src_i], QQ[u][5 - src_i]
                    last = (lev == 5)
                    p = pd[dstate["di"] % 3]; dstate["di"] += 1
                    kb.mm(p[:], Qc[:], Lc[:])
                    kb.copy(kb.act, Ln[:], p[:])
                    if not last:
                        p = pd[dstate["di"] % 3]; dstate["di"] += 1
                        kb.transpose(p[:], Ln[:], ident)
                        kb.copy(kb.act if u % 2 else kb.dve, Qn[:], p[:])
                    p = pd[dstate["di"] % 3]; dstate["di"] += 1
                    kb.mm(p[:], Ln[:], Pc[:])
                    outP = UM[sl_][u][3] if last else Pn
                    kb.tt(kb.dve, outP[:], p[:], Pc[:], ALU.add)

            def seq_block(gb, sl_, bi):
                TM = TMs[sl_]; UMs = UM[sl_]
                Vtm, VP0, VP1, Btm, Ktm = TM
                n = gb * GB + bi; bs = slice(n * 128, (n + 1) * 128)
                p1 = pq[0]; p2 = pq[1]
                for hh in range(2):
                    hp = slice(64 * hh, 64 * hh + 64); hc = slice(64 * hh, 64 * hh + 64)
                    kb.mm(p1[:, hc], At[hp, bs], STP[hp, hc], start=True, stop=False)
                    kb.mm(p1[:, hc], UMs[2 * bi + hh][0][:], Vtm[:, bi, hc], start=False, stop=True)
                kb.copy(kb.act, RH[:], p1[:, 0:128])
                for hh in range(2):
                    hc = slice(64 * hh, 64 * hh + 64)
                    kb.mm(p2[:, hc], UMs[2 * bi + hh][3][:], RH[:, hc])
                kb.copy(kb.act, SAB[:], p2[:, 0:128])
                kb.copy(kb.dve, SAP[0][:, 0:64], SAB[:, 0:64])
                kb.copy(kb.act, SAP[1][:, 64:128], SAB[:, 64:128])
                py = p1
                kb.mm(py[:, 128:256], STP[:], Rt[:, bs], start=True, stop=False)
                for hh in range(2):
                    kb.mm(py[:, 128:256], SAP[hh][:], UMs[2 * bi + hh][1][:], start=False, stop=False)
                kb.mm(py[:, 128:256], VP0[:, bi, :], UMs[2 * bi][2][:], start=False, stop=False)
                kb.mm(py[:, 128:256], VP1[:, bi, :], UMs[2 * bi + 1][2][:], start=False, stop=True)
                kb.copy(kb.act, Y[:, bs], py[:, 128:256])
                pst = p2
                kb.mm(pst[:, 128:256], Btm[:, bi, :], SAB[:], start=True, stop=False)
                kb.mm(pst[:, 128:256], Ktm[:, bi, :], Vtm[:, bi, :], start=False, stop=True)
                kb.stt(kb.dve, t1[:], pst[:, 128:256], gC[:, n:n + 1], BD1, ALU.mult, ALU.mult)
                kb.stt(kb.dve, STP[:], STP[:], gC[:, n:n + 1], t1[:], ALU.mult, ALU.add)

            if dbg < 2: continue
            NG = NBK // GB
            front_a(0, 0)
            for lev in range(6): dbl_level(0, lev)
            for gb in range(NG):
                sl_ = gb % 2
                if gb + 1 < NG:
                    front_a(gb + 1, 1 - sl_)
                    order = ["L0", "L1", "S0", "L2", "S1", "L3", "S2", "L4", "S3", "L5"]
                else:
                    order = ["S0", "S1", "S2", "S3"]
                for o in order:
                    if o[0] == "L": dbl_level(1 - sl_, int(o[1]))
                    else: seq_block(gb, sl_, int(o[1]))
            if dbg < 6: continue
            for j in range(4):
                sl = slice(j * 512, (j + 1) * 512)
                kb.mm(pq[0][:], BD, Y[:, sl])
                kb.tt(kb.dve, B2[:, sl], Y[:, sl], pq[0][:], ALU.subtract)
                kb.activation(B3[:, sl], B2[:, sl], AF.Square)
                kb.mm(pq[1][:], BD, B3[:, sl])
                kb.activation(B3[:, sl], pq[1][:], AF.Sqrt, bias=self.consts[:, C_GNE:C_GNE + 1])
            recip(B3[:])
            kb.tt(kb.dve, B2[:], B2[:], B3[:], ALU.mult)
            kb.ts(kb.dve, B2[:], B2[:], self.vcol("ln_g", c), self.vcol("ln_b", c), ALU.mult, ALU.add)
            kb.tt(kb.dve, B3[:], r[:], k[:], ALU.mult)
            kb.ts(kb.dve, B3[:], B3[:], self.vcol("r_k", c), None, ALU.mult)
            for j in range(4):
                sl = slice(j * 512, (j + 1) * 512)
                kb.mm(pq[j % 2][:], BD1, B3[:, sl])
                kb.tt(kb.dve, B1[:, sl], pq[j % 2][:], v[:, sl], ALU.mult)
            kb.tt(kb.dve, B2[:], B2[:], B1[:], ALU.add)
            kb.tt(kb.dve, ygb[:], B2[:], g[:], ALU.mult)
            kb.dma(kb.sp, YG[c], ygb[:])
        ph.close()
        Wo = rearr(self.W["rwkv_w_o"][0])
        for s in (range(NPASS) if "C" in parts else []):
            t0 = s * T
            ph = Phase(kb, "rwC%d" % s)
            HT = ph.sb([128, NCH, T], BF16, "HT")
            slots = [ph.sb([128, NCH * 128], BF16, "ws") for _ in range(3)]
            xr = [ph.sb([128, T], F32, "xr") for _ in range(2)]
            pss = [ph.ps([128, TT], F32, "ps") for _ in range(8)]
            for c in range(NCH):
                kb.dma(kb.sp, HT[:, c, :], YG[c, :, t0:t0 + T])
            self.proj_residual(ph, HT, Wo, NCH, t0, pss, slots, xr)
            ph.close()
        self.xcur = self.xT

    def final(self):
        kb = self.kb
        for s in range(NPASS):
            ph = Phase(kb, "fin%d" % s)
            pss = [ph.ps([128, TT], F32, "ps") for _ in range(2)]
            self.norm_tiles(ph)
            self.norm(ph, self.xcur, s * T, "final_norm", None, pss, out_dram=self.out)
            ph.close()

    def copy_x(self):
        kb = self.kb
        ph = Phase(kb, "cp")
        xt = [ph.sb([128, S], F32, "x") for _ in range(2)]
        for c in range(NCH):
            kb.dma(kb.sp, xt[c % 2][:], self.xcur[c, :, :])
            kb.dma(kb.sp, self.out[c, :, :], xt[c % 2][:])
        ph.close()

    def build(self):
        for p in self.phases:
            if p[0] == "ffn": self.ffn(p[1])
            elif p[0] == "ple": self.ple(p[1])
            elif p[0] == "attn": self.attn(*p[1:])
            elif p[0] == "rwkv": self.rwkv(*p[1:])
            elif p[0] == "conv": self.conv()
            elif p[0] == "final": self.final()
            elif p[0] == "copy": self.copy_x()
        self.kb.barrier()
        self.pp.es.close()


WSHAPES = {
    "attn_w_qkv": [2, D, 9 * D], "attn_w_o": [2, D, D],
    "rwkv_w_rkv": [1, 3, D, D], "rwkv_w_w1": [1, D, 96], "rwkv_w_w2": [1, 96, D],
    "rwkv_w_a1": [1, D, 96], "rwkv_w_a2": [1, 96, D], "rwkv_w_g1": [1, D, 256], "rwkv_w_g2": [1, 256, D],
    "rwkv_w_o": [1, D, D],
    "conv_w_in": [1, D, 3 * D], "conv_w_out": [1, D, D],
    "ffn_w_gu": [DEPTH, D, 2 * FF], "ffn_w_down": [DEPTH, FF, D],
    "ple_w_proj": [DEPTH, PLE, D], "ple_w_gate": [DEPTH, D, D],
}

def wnames(phases):
    pre = {"ffn": "ffn_", "ple": "ple_", "attn": "attn_", "rwkv": "rwkv_", "conv": "conv_"}
    used = set(pre[p[0]] for p in phases if p[0] in pre)
    return [n for n in WSHAPES if any(n.startswith(u) for u in used)]


FULL_PHASES = []
for _i in range(DEPTH):
    _k, _j = _i % 3, _i // 3
    FULL_PHASES.append(("attn", _j, _i) if _k == 0 else (("rwkv",) if _k == 1 else ("conv",)))
    FULL_PHASES.append(("ffn", _i)); FULL_PHASES.append(("ple", _i))
FULL_PHASES.append(("final",))


def build_nc(phases):
    nc = bass.Bass("TRN2", target_bir_lowering=False)
    es = contextlib.ExitStack()
    Prog(nc, es, phases)
    es.close()
    return nc


def make_vecs(inp):
    v = np.zeros((128, NV), np.float32)

    def put(name, arr, n=NCH):
        v[:, VCOL[name]:VCOL[name] + n] = pc(arr, n)
    for j in range(2): put("attn_norm%d" % j, inp["attn_norm"][j])
    put("rwkv_norm", inp["rwkv_norm"][0]); put("conv_norm", inp["conv_norm"][0])
    for i in range(DEPTH):
        put("ffn_norm%d" % i, inp["ffn_norm"][i]); put("ple_norm%d" % i, inp["ple_norm"][i])
        for j in range(3): put("fcw%d_%d" % (i, j), inp["ffn_conv_w"][i, j], NFC)
        put("fcb%d" % i, inp["ffn_conv_b"][i], NFC)
    put("final_norm", inp["final_norm"])
    for j in range(6): put("mu%d" % j, inp["rwkv_mu"][0, j])
    for nm in ("w0", "a0", "k_k", "k_a", "ln_g", "ln_b"): put(nm, inp["rwkv_" + nm][0])
    put("r_k", np.asarray(inp["rwkv_r_k"][0]).reshape(-1))
    for j in range(3): put("conv_w%d" % j, inp["conv_w"][0, j])
    return v


def run(inputs, phases, ncores=8):
    inp = {k: np.asarray(v) for k, v in inputs.items()}
    nc = build_nc(phases)
    vecs = make_vecs(inp); consts = build_consts()
    in_maps = []
    for c in range(ncores):
        b = c % NB
        m = {"xT": np.ascontiguousarray(inp["x"][b].T.reshape(NCH, 128, S)),
             "pT": np.ascontiguousarray(inp["p"][:, b].transpose(0, 2, 1).reshape(DEPTH, 2, 128, S)),
             "vecs": vecs, "consts": consts}
        for name in wnames(phases):
            m[name] = np.ascontiguousarray(inp[name], dtype=np.float32)
        in_maps.append(m)
    res = run_bass_kernel_spmd(nc, in_maps, core_ids=list(range(ncores)))
    outs = [res.results[b]["outT"].reshape(D, S).T for b in range(min(NB, ncores))]
    return np.stack(outs, 0)


def kernel(**inputs):
    out = run(inputs, FULL_PHASES)
    return np.ascontiguousarray(out.astype(np.float32))
```

```python
import contextlib
import numpy as np
import concourse.bass as bass
import concourse.mybir as mybir
from concourse.bass_utils import run_bass_kernel_spmd

F32 = mybir.dt.float32
BF16 = mybir.dt.bfloat16
AF = mybir.ActivationFunctionType
ALU = mybir.AluOpType
AX = mybir.AxisListType

D = 2048; S = 2048; NB = 4; DEPTH = 4; FF = 5632; PLE = 256
NCH = 16; NFC = 44; T = 1024; NTT = 2; TT = 512
EPS = 1e-6
RW_NSETS = 5
NPASS = S // T


class Tile:
    def __init__(self, t, name):
        self.t = t; self.name = name; self.w = {}; self.r = {}; self.dsem = None

    def __getitem__(self, idx):
        return View(self, self.t[idx])

    def v(self, ap):
        return View(self, ap)


class View:
    def __init__(self, tile, ap):
        self.tile = tile; self.ap = ap


class Eng:
    def __init__(self, kb, h, name):
        self.h = h; self.name = name; self.sid = kb.new_sem("e_" + name); self.waited = {}


class KB:
    def __init__(self, nc, es):
        self.nc = nc; self.es = es
        self.sems = []; self.cnt = []
        self.pe = Eng(self, nc.tensor, "pe"); self.act = Eng(self, nc.scalar, "act")
        self.dve = Eng(self, nc.vector, "dve"); self.pool = Eng(self, nc.gpsimd, "pool")
        self.sp = Eng(self, nc.sync, "sp")
        self.engs = [self.pe, self.act, self.dve, self.pool, self.sp]
        self.free_dsems = {"sp": [self.new_sem("d%d" % i) for i in range(64)], "pool": [self.new_sem("g%d" % i) for i in range(32)]}
        self.n_inst = 0

    def new_sem(self, name):
        h = self.es.enter_context(self.nc.semaphore(name))
        self.sems.append(h); self.cnt.append(0)
        return len(self.sems) - 1

    def op(self, E, fn, reads, writes, inc=True, dma_tile=None, nowait_writes=False):
        waits = {}
        for v in reads:
            for k, val in v.tile.w.items():
                if waits.get(k, 0) < val: waits[k] = val
        for v in ([] if nowait_writes else writes):
            for dd in (v.tile.w, v.tile.r):
                for k, val in dd.items():
                    if waits.get(k, 0) < val: waits[k] = val
        for k, val in waits.items():
            if E.waited.get(k, 0) < val:
                E.h.wait_ge(self.sems[k], val); E.waited[k] = val
        ins = fn()
        self.n_inst += 1
        if dma_tile is not None:
            if dma_tile.dsem is None:
                dma_tile.dsem = {}
            if E.name not in dma_tile.dsem:
                dma_tile.dsem[E.name] = self.free_dsems[E.name].pop()
            k = dma_tile.dsem[E.name]
            self.cnt[k] += 16
            ins.then_inc(self.sems[k], 16)
            tok = (k, self.cnt[k])
        elif inc:
            k = E.sid
            self.cnt[k] += 1
            ins.then_inc(self.sems[k], 1)
            tok = (k, self.cnt[k])
        else:
            tok = (E.sid, self.cnt[E.sid] + 1)
        for v in reads:
            if v.tile.r.get(tok[0], 0) < tok[1]: v.tile.r[tok[0]] = tok[1]
        for v in writes:
            v.tile.w = {tok[0]: tok[1]}; v.tile.r = {}
        return ins

    def release(self, tiles):
        for t in tiles:
            if t.dsem is not None:
                for q, k in t.dsem.items(): self.free_dsems[q].append(k)
                t.dsem = None

    def barrier(self):
        for E in self.engs:
            for k in range(len(self.sems)):
                val = self.cnt[k]
                if val > 0 and E.waited.get(k, 0) < val:
                    E.h.wait_ge(self.sems[k], val); E.waited[k] = val

    def mm(self, out, lhsT, rhs, start=True, stop=True):
        wr = [out] if (start or stop) else []
        return self.op(self.pe, lambda: self.nc.tensor.matmul(out.ap, lhsT.ap, rhs.ap, start=start, stop=stop),
                       [lhsT, rhs], wr, inc=True, nowait_writes=not start)

    def transpose(self, out, in_, ident):
        return self.op(self.pe, lambda: self.nc.tensor.transpose(out.ap, in_.ap, ident.ap), [in_, ident], [out])

    def activation(self, out, in_, func, bias=None, scale=1.0, accum=None, E=None):
        E = E or self.act
        reads = [in_]; kw = {}
        if isinstance(bias, View): reads.append(bias); kw["bias"] = bias.ap
        elif bias is not None: kw["bias"] = bias
        if isinstance(scale, View): reads.append(scale); kw["scale"] = scale.ap
        else: kw["scale"] = scale
        writes = [out]
        if accum is not None: writes.append(accum); kw["accum_out"] = accum.ap
        return self.op(E, lambda: E.h.activation(out=out.ap, in_=in_.ap, func=func, **kw), reads, writes)

    def tt(self, E, out, in0, in1, op):
        return self.op(E, lambda: E.h.tensor_tensor(out.ap, in0.ap, in1.ap, op), [in0, in1], [out])

    def ts(self, E, out, in0, s1, s2, op0, op1=None):
        reads = [in0]
        a1 = s1.ap if isinstance(s1, View) else s1
        a2 = s2.ap if isinstance(s2, View) else s2
        if isinstance(s1, View): reads.append(s1)
        if isinstance(s2, View): reads.append(s2)
        if op1 is None:
            return self.op(E, lambda: E.h.tensor_scalar(out.ap, in0.ap, a1, None, op0), reads, [out])
        return self.op(E, lambda: E.h.tensor_scalar(out.ap, in0.ap, a1, a2, op0, op1), reads, [out])

    def stt(self, E, out, in0, sc, in1, op0, op1):
        reads = [in0, in1]
        a = sc.ap if isinstance(sc, View) else sc
        if isinstance(sc, View): reads.append(sc)
        return self.op(E, lambda: E.h.scalar_tensor_tensor(out.ap, in0.ap, a, in1.ap, op0, op1), reads, [out])

    def copy(self, E, out, in_):
        if E is self.act:
            return self.activation(out, in_, AF.Copy)
        return self.op(E, lambda: E.h.tensor_copy(out.ap, in_.ap), [in_], [out])

    def reduce(self, E, out, in_, op, axis=AX.X):
        return self.op(E, lambda: E.h.tensor_reduce(out.ap, in_.ap, axis, op), [in_], [out])

    def memset(self, E, out, val):
        return self.op(E, lambda: E.h.memset(out.ap, val), [], [out])

    def dma(self, Q, out, in_, sb=None):
        if sb is None:
            sb = out.tile if getattr(out.tile, "is_sb", False) else in_.tile
        return self.op(Q, lambda: Q.h.dma_start(out=out.ap, in_=in_.ap), [in_], [out], dma_tile=sb)


class Phase:
    def __init__(self, kb, name):
        self.kb = kb; self.name = name; self.es = contextlib.ExitStack(); self.tiles = []; self.n = 0

    def sb(self, shape, dt, name=None):
        self.n += 1
        nm = "%s_%s%d" % (self.name, name or "t", self.n)
        t = Tile(self.es.enter_context(self.kb.nc.sbuf_tensor(nm, list(shape), dt)), nm)
        t.is_sb = True
        self.tiles.append(t)
        return t

    def ps(self, shape, dt, name=None):
        self.n += 1
        nm = "%s_%s%d" % (self.name, name or "p", self.n)
        t = Tile(self.es.enter_context(self.kb.nc.psum_tensor(nm, list(shape), dt)), nm)
        self.tiles.append(t)
        return t

    def close(self):
        self.kb.barrier()
        self.kb.release(self.tiles)
        self.es.close()


def vec_layout():
    cols = {}; n = 0

    def add(name, w):
        nonlocal n
        cols[name] = n; n += w
    for j in range(2): add("attn_norm%d" % j, NCH)
    add("rwkv_norm", NCH); add("conv_norm", NCH)
    for i in range(DEPTH): add("ffn_norm%d" % i, NCH)
    for i in range(DEPTH): add("ple_norm%d" % i, NCH)
    add("final_norm", NCH)
    for j in range(6): add("mu%d" % j, NCH)
    for nm in ("w0", "a0", "k_k", "k_a", "ln_g", "ln_b", "r_k"): add(nm, NCH)
    for j in range(3): add("conv_w%d" % j, NCH)
    for i in range(DEPTH):
        for j in range(3): add("fcw%d_%d" % (i, j), NFC)
        add("fcb%d" % i, NFC)
    return cols, n


VCOL, NV = vec_layout()
C_ID = 0; C_ONES = 128; C_D1 = 256; C_SU = 512; C_IU = 640; C_SL = 768; C_BD = 896; C_EPS = 1024; C_GNE = 1025; C_BD1 = 1032; NCONST = 1160


def build_consts():
    c = np.zeros((128, NCONST), np.float32)
    c[:, C_ID:C_ID + 128] = np.eye(128)
    c[:, C_ONES:C_ONES + 128] = 1.0 / D
    qi = np.arange(128)[:, None]; kj = np.arange(256)[None, :]
    dist = qi - kj + 128
    c[:, C_D1:C_D1 + 256] = np.where((dist >= 0) & (dist <= 128), -dist.astype(np.float32), -1.0e6)
    i = np.arange(128)[:, None]; t = np.arange(128)[None, :]
    c[:, C_SU:C_SU + 128] = (i < t); c[:, C_IU:C_IU + 128] = (i <= t); c[:, C_SL:C_SL + 128] = (i > t)
    c[:, C_BD:C_BD + 128] = ((i // 64) == (t // 64)) / 64.0
    c[:, C_EPS] = EPS
    c[:, C_GNE] = 6.4e-4
    c[:, C_BD1:C_BD1 + 128] = ((i // 64) == (t // 64))
    return c


def pc(v, n):
    return np.ascontiguousarray(np.asarray(v, np.float32).reshape(n, 128).T)


class Prog:
    def __init__(self, nc, es, phases):
        self.nc = nc; self.kb = KB(nc, es); self.es = es; self.phases = phases
        kb = self.kb
        dt = lambda name, shape, kind="ExternalInput": nc.dram_tensor(name, list(shape), F32, kind=kind).ap()
        self.xT_in = dt("xT", [NCH, 128, S])
        self.pT = dt("pT", [DEPTH, 2, 128, S])
        self.vecs_d = dt("vecs", [128, NV]); self.consts_d = dt("consts", [128, NCONST])
        self.W = {}
        for name in wnames(phases):
            self.W[name] = dt(name, WSHAPES[name])
        self.out = Tile(dt("outT", [NCH, 128, S], kind="ExternalOutput"), "outT")
        self.xT = Tile(nc.dram_tensor("xT_s", [NCH, 128, S], F32).ap(), "xT_s")
        self.xin = Tile(self.xT_in, "xT_in")
        self.wtile = Tile(None, "weights")
        self.pp = Phase(kb, "g")
        self.vecs = self.pp.sb([128, NV], F32, "vecs"); self.consts = self.pp.sb([128, NCONST], F32, "consts")
        self.ffn_halo = self.pp.sb([128, NFC, 2], F32, "fhalo")
        self.identb = self.pp.sb([128, 128], BF16, "identb")
        kb.dma(kb.sp, self.vecs[:], View(Tile(self.vecs_d, "vd"), self.vecs_d))
        kb.dma(kb.sp, self.consts[:], View(Tile(self.consts_d, "cd"), self.consts_d))
        kb.copy(kb.dve, self.identb[:], self.consts[:, C_ID:C_ID + 128])
        self.xcur = self.xin
        self.build()

    def vcol(self, name, c=0):
        k = VCOL[name] + c
        return self.vecs[:, k:k + 1]

    def norm(self, ph, xsrc, t0, gname, HT, ps_pool, out_dram=None):
        kb = self.kb
        ones = self.consts[:, C_ONES:C_ONES + 128]
        k = 0
        for j in range(NTT):
            ps = ps_pool[j % len(ps_pool)]
            tsl = slice(t0 + j * TT, t0 + (j + 1) * TT)
            for c in range(NCH):
                xt = ph.xn[k % 4]; k += 1
                kb.dma(kb.sp, xt[:], xsrc[c, :, tsl])
                sq = ph.sq[c % 2]
                kb.activation(sq[:], xt[:], AF.Square)
                kb.mm(ps[:], ones, sq[:], start=(c == 0), stop=(c == NCH - 1))
            rstd = ph.rstd
            kb.activation(rstd[:], ps[:], AF.Sqrt, bias=self.consts[:, C_EPS:C_EPS + 1])
            kb.op(kb.dve, lambda: self.nc.vector.reciprocal(rstd.t[:], rstd.t[:]), [rstd[:]], [rstd[:]])
            for c in range(NCH):
                xt = ph.xn[k % 4]; k += 1
                kb.dma(kb.sp, xt[:], xsrc[c, :, tsl])
                E = kb.dve
                if out_dram is None:
                    kb.stt(E, HT[:, c, j * TT:(j + 1) * TT], xt[:], self.vcol(gname, c), rstd[:], ALU.mult, ALU.mult)
                else:
                    kb.stt(E, xt[:], xt[:], self.vcol(gname, c), rstd[:], ALU.mult, ALU.mult)
                    kb.dma(kb.sp, out_dram[c, :, tsl], xt[:])

    def norm_tiles(self, ph):
        ph.xn = [ph.sb([128, TT], F32, "xn") for _ in range(4)]
        ph.sq = [ph.sb([128, TT], F32, "sq") for _ in range(2)]
        ph.rstd = ph.sb([128, TT], F32, "rstd")

    def stream(self, ph, slots, loads, compute):
        kb = self.kb
        ns = len(slots)
        n = len(loads)
        for i in range(min(ns - 1, n)):
            loads[i](slots[i % ns])
        for i in range(n):
            if i + ns - 1 < n:
                loads[i + ns - 1](slots[(i + ns - 1) % ns])
            compute(i, slots[i % ns])

    def wload(self, slot_view, wap):
        kb = self.kb
        kb.dma(kb.pool, slot_view, View(self.wtile, wap), sb=slot_view.tile)

    def ffn(self, li):
        kb = self.kb
        Wgu = self.W["ffn_w_gu"][li].rearrange("(kc p) n -> p kc n", p=128)
        Wdn = self.W["ffn_w_down"][li].rearrange("(kc p) n -> p kc n", p=128)
        for s in range(NPASS):
            t0 = s * T
            ph = Phase(kb, "ffn%d_%d" % (li, s))
            HT = ph.sb([128, NCH, T], BF16, "HT")
            ACT = ph.sb([128, NFC, T], BF16, "ACT")
            slots = [ph.sb([128, NFC * 128], BF16, "ws") for _ in range(3)]
            G = [ph.sb([128, T + 2], F32, "G") for _ in range(2)]
            Cv = [ph.sb([128, T], F32, "Cv") for _ in range(2)]
            xr = [ph.sb([128, T], F32, "xr") for _ in range(2)]
            pss = [ph.ps([128, TT], F32, "ps") for _ in range(8)]
            self.norm_tiles(ph)
            self.norm(ph, self.xcur, t0, "ffn_norm%d" % li, HT, pss[0:2])
            def mk_load_gu(f):
                def ld(slot):
                    sv = slot.t[:, 0:NCH * 256].rearrange("p (k n) -> p k n", k=NCH)
                    self.wload(slot.v(sv[:, :, 0:128]), Wgu[:, :, f * 128:(f + 1) * 128])
                    self.wload(slot.v(sv[:, :, 128:256]), Wgu[:, :, FF + f * 128:FF + (f + 1) * 128])
                return ld

            def comp_gu(f, slot):
                sv = slot.t[:, 0:NCH * 256].rearrange("p (k n) -> p k n", k=NCH)
                g = G[f % 2]; cv = Cv[f % 2]
                if s == 0:
                    kb.memset(kb.dve, g[:, 0:2], 0.0)
                else:
                    kb.copy(kb.dve, g[:, 0:2], self.ffn_halo[:, f, :])
                pg = [pss[(4 * f + j) % 8] for j in range(2)]
                pu = [pss[(4 * f + 2 + j) % 8] for j in range(2)]
                for j in range(NTT):
                    for c in range(NCH):
                        kb.mm(pg[j][:], slot.v(sv[:, c, 0:128]), HT[:, c, j * TT:(j + 1) * TT], start=(c == 0), stop=(c == NCH - 1))
                    kb.copy(kb.act, g[:, 2 + j * TT: 2 + (j + 1) * TT], pg[j][:])
                for j in range(NTT):
                    for c in range(NCH):
                        kb.mm(pu[j][:], slot.v(sv[:, c, 128:256]), HT[:, c, j * TT:(j + 1) * TT], start=(c == 0), stop=(c == NCH - 1))
                kb.copy(kb.dve, self.ffn_halo[:, f, :], g[:, T:T + 2])
                w = lambda j: self.vcol("fcw%d_%d" % (li, j), f)
                kb.ts(kb.dve, cv[:], g[:, 2:T + 2], w(2), self.vcol("fcb%d" % li, f), ALU.mult, ALU.add)
                kb.stt(kb.dve, cv[:], g[:, 1:T + 1], w(1), cv[:], ALU.mult, ALU.add)
                kb.stt(kb.dve, cv[:], g[:, 0:T], w(0), cv[:], ALU.mult, ALU.add)
                kb.activation(cv[:], cv[:], AF.Silu)
                for j in range(NTT):
                    kb.tt(kb.dve, ACT[:, f, j * TT:(j + 1) * TT], cv[:, j * TT:(j + 1) * TT], pu[j][:], ALU.mult)
            self.stream(ph, slots, [mk_load_gu(f) for f in range(NFC)], comp_gu)

            def mk_load_dn(d):
                def ld(slot):
                    sv = slot.t[:, :].rearrange("p (k n) -> p k n", k=NFC)
                    self.wload(slot.v(sv), Wdn[:, :, d * 128:(d + 1) * 128])
                return ld

            def comp_dn(d, slot):
                sv = slot.t[:, :].rearrange("p (k n) -> p k n", k=NFC)
                x = xr[d % 2]
                kb.dma(kb.sp, x[:], self.xcur[d, :, t0:t0 + T])
                for j in range(NTT):
                    p = pss[(2 * d + j) % 8]
                    for c in range(NFC):
                        kb.mm(p[:], slot.v(sv[:, c, :]), ACT[:, c, j * TT:(j + 1) * TT], start=(c == 0), stop=(c == NFC - 1))
                    kb.tt(kb.dve, x[:, j * TT:(j + 1) * TT], x[:, j * TT:(j + 1) * TT], p[:], ALU.add)
                kb.dma(kb.sp, self.xT[d, :, t0:t0 + T], x[:])
            self.stream(ph, slots, [mk_load_dn(d) for d in range(NCH)], comp_dn)
            ph.close()
        self.xcur = self.xT

    def ple(self, li):
        kb = self.kb
        Wg = self.W["ple_w_gate"][li].rearrange("(kc p) n -> p kc n", p=128)
        Wp = self.W["ple_w_proj"][li].rearrange("(kc p) n -> p kc n", p=128)
        pT = Tile(self.pT, "pT")
        ones = self.consts[:, C_ONES:C_ONES + 128]
        for s in range(NPASS):
            t0 = s * T
            ph = Phase(kb, "ple%d_%d" % (li, s))
            HT = ph.sb([128, NCH, T], BF16, "HT")
            PT = ph.sb([128, 2, T], BF16, "PT")
            X = [ph.sb([128, T], F32, "X") for _ in range(NCH)]
            slots = [ph.sb([128, 18 * 128], BF16, "ws") for _ in range(3)]
            sg = [ph.sb([128, T], F32, "sg") for _ in range(2)]
            sq = [ph.sb([128, TT], F32, "sq") for _ in range(2)]
            rstd = [ph.sb([128, TT], F32, "rstd") for _ in range(NTT)]
            pss = [ph.ps([128, TT], F32, "ps") for _ in range(8)]
            for c in range(2):
                kb.dma(kb.pool, PT[:, c, :], pT[li, c, :, t0:t0 + T])
            for c in range(NCH):
                kb.dma(kb.sp, X[c][:], self.xcur[c, :, t0:t0 + T])
            for j in range(NTT):
                sl = slice(j * TT, (j + 1) * TT)
                for c in range(NCH):
                    kb.activation(sq[c % 2][:], X[c][:, sl], AF.Square)
                    kb.mm(pss[j][:], ones, sq[c % 2][:], start=(c == 0), stop=(c == NCH - 1))
                kb.activation(rstd[j][:], pss[j][:], AF.Sqrt, bias=self.consts[:, C_EPS:C_EPS + 1])
                kb.op(kb.dve, lambda r_=rstd[j]: self.nc.vector.reciprocal(r_.t[:], r_.t[:]), [rstd[j][:]], [rstd[j][:]])
                for c in range(NCH):
                    kb.stt(kb.dve, HT[:, c, sl], X[c][:, sl], self.vcol("ple_norm%d" % li, c), rstd[j][:], ALU.mult, ALU.mult)

            def mk_load(d):
                def ld(slot):
                    sv = slot.t[:, :].rearrange("p (k n) -> p k n", k=18)
                    self.wload(slot.v(sv[:, 0:16, :]), Wg[:, :, d * 128:(d + 1) * 128])
                    self.wload(slot.v(sv[:, 16:18, :]), Wp[:, :, d * 128:(d + 1) * 128])
                return ld

            def comp(d, slot):
                sv = slot.t[:, :].rearrange("p (k n) -> p k n", k=18)
                x = X[d]; g = sg[d % 2]
                for j in range(NTT):
                    pa = pss[(4 * d + j) % 8]; pb = pss[(4 * d + 2 + j) % 8]
                    sl = slice(j * TT, (j + 1) * TT)
                    for c in range(NCH):
                        kb.mm(pa[:], slot.v(sv[:, c, :]), HT[:, c, sl], start=(c == 0), stop=(c == NCH - 1))
                    for c in range(2):
                        kb.mm(pb[:], slot.v(sv[:, 16 + c, :]), PT[:, c, sl], start=(c == 0), stop=(c == 1))
                    kb.activation(g[:, sl], pa[:], AF.Sigmoid)
                    kb.tt(kb.dve, g[:, sl], g[:, sl], pb[:], ALU.mult)
                    kb.tt(kb.dve, x[:, sl], x[:, sl], g[:, sl], ALU.add)
                kb.dma(kb.sp, self.xT[d, :, t0:t0 + T], x[:])
            self.stream(ph, slots, [mk_load(d) for d in range(NCH)], comp)
            ph.close()
        self.xcur = self.xT

    def proj_to_dram(self, ph, HT, Wv, KC, ntiles, colfn, dstfn, pss, slots, stage, scalefn=None, dil=None):
        kb = self.kb
        def mk_load(i):
            def ld(slot):
                sv = slot.t[:, 0:KC * 128].rearrange("p (k n) -> p k n", k=KC)
                self.wload(slot.v(sv), Wv[:, :, colfn(i):colfn(i) + 128])
            return ld

        def comp(i, slot):
            sv = slot.t[:, 0:KC * 128].rearrange("p (k n) -> p k n", k=KC)
            st = stage[i % len(stage)]
            d = dil(i) if dil else 1
            for j in range(NTT):
                p = pss[(2 * i + j) % len(pss)]
                for c in range(KC):
                    kb.mm(p[:], slot.v(sv[:, c, :]), HT[:, c, j * TT:(j + 1) * TT], start=(c == 0), stop=(c == KC - 1))
                sc = scalefn(i) if scalefn else 1.0
                if d == 1:
                    kb.activation(st[:, j * TT:(j + 1) * TT], p[:], AF.Copy, scale=sc)
                else:
                    o = st.t[:, :].rearrange("p (r l) -> p r l", r=d)[:, :, j * TT // d:(j + 1) * TT // d]
                    i_ = p.t[:, :].rearrange("p (l r) -> p r l", r=d)
                    kb.activation(st.v(o), p.v(i_), AF.Copy, scale=sc)
            dstfn(i, st)
        self.stream(ph, slots, [mk_load(i) for i in range(ntiles)], comp)

    def proj_residual(self, ph, HT, Wv, KC, t0, pss, slots, xr):
        kb = self.kb
        def mk_load(d):
            def ld(slot):
                sv = slot.t[:, 0:KC * 128].rearrange("p (k n) -> p k n", k=KC)
                self.wload(slot.v(sv), Wv[:, :, d * 128:(d + 1) * 128])
            return ld

        def comp(d, slot):
            sv = slot.t[:, 0:KC * 128].rearrange("p (k n) -> p k n", k=KC)
            x = xr[d % 2]
            kb.dma(kb.sp, x[:], self.xcur[d, :, t0:t0 + T])
            for j in range(NTT):
                p = pss[(2 * d + j) % len(pss)]
                for c in range(KC):
                    kb.mm(p[:], slot.v(sv[:, c, :]), HT[:, c, j * TT:(j + 1) * TT], start=(c == 0), stop=(c == KC - 1))
                kb.tt(kb.dve, x[:, j * TT:(j + 1) * TT], x[:, j * TT:(j + 1) * TT], p[:], ALU.add)
            kb.dma(kb.sp, self.xT[d, :, t0:t0 + T], x[:])
        self.stream(ph, slots, [mk_load(d) for d in range(NCH)], comp)

    def attn(self, j_, li, parts="ABC", nheads=16):
        kb = self.kb; nc = self.nc
        DIL = (1, 4, 16)
        slopes = [[2.0 ** (-8.0 * (g * 16 + h + 1) / 48.0) for h in range(16)] for g in range(3)]
        if not hasattr(self, "qkvT"):
            self.qkvT = Tile(nc.dram_tensor("qkvT_s", [144, 128, S], BF16).ap(), "qkvT")
            self.oT = Tile(nc.dram_tensor("oT_s", [NCH, 128, S], BF16).ap(), "oT")
        Wqkv = self.W["attn_w_qkv"][j_].rearrange("(kc p) n -> p kc n", p=128)
        Wo = self.W["attn_w_o"][j_].rearrange("(kc p) n -> p kc n", p=128)
        for s in (range(NPASS) if "A" in parts else []):
            t0 = s * T
            ph = Phase(kb, "atA%d_%d" % (li, s))
            HT = ph.sb([128, NCH, T], BF16, "HT")
            slots = [ph.sb([128, NCH * 128], BF16, "ws") for _ in range(4)]
            stage = [ph.sb([128, T], BF16, "st") for _ in range(3)]
            pss = [ph.ps([128, TT], F32, "ps") for _ in range(8)]
            self.norm_tiles(ph)
            self.norm(ph, self.xcur, t0, "attn_norm%d" % j_, HT, pss[0:2])

            def dst(i, st):
                d = DIL[i // 48]; L = S // d; Lh = T // d
                dv = self.qkvT.t[i].rearrange("p (r l) -> p r l", r=d)[:, :, s * Lh:(s + 1) * Lh]
                sv = st.t[:, :].rearrange("p (r l) -> p r l", r=d)
                kb.dma(kb.sp, self.qkvT.v(dv), st.v(sv))
            self.proj_to_dram(ph, HT, Wqkv, NCH, 144, lambda i: i * 128, dst, pss, slots, stage,
                              scalefn=lambda i: (128.0 ** -0.5 if (i // 16) % 3 == 0 else 1.0),
                              dil=lambda i: DIL[i // 48])
            ph.close()
        ph = Phase(kb, "atB%d" % li)
        ident = self.consts[:, C_ID:C_ID + 128]
        ones1 = ph.sb([128, 128], F32, "ones1"); kb.memset(kb.dve, ones1[:], 1.0)
        qkv = [[ph.sb([128, S], BF16, "qkv") for _ in range(9)] for _ in range(2)]
        OgT = [ph.sb([128, S], F32, "OgT") for _ in range(3)]
        LgT2 = [[ph.sb([128, S], F32, "LgT") for _ in range(3)] for _ in range(2)]
        Vtm = [ph.sb([128, 16, 128], BF16, "Vtm") for _ in range(2)]
        Sb = [ph.sb([128, 256], F32, "Sb") for _ in range(3)]
        Pb = [ph.sb([128, 256], BF16, "Pb") for _ in range(3)]
        PT = [ph.sb([128, 2, 128], BF16, "PT") for _ in range(3)]
        Otm = [ph.sb([128, 128], F32, "Otm") for _ in range(3)]
        Dg = [ph.sb([128, 128], F32, "Dg") for _ in range(3)]
        m3 = ph.sb([128, S], F32, "m3"); ssum = ph.sb([128, S], F32, "ssum")
        Ob = [ph.sb([128, S], BF16, "Ob") for _ in range(2)]
        ps_s = [ph.ps([128, 256], F32, "pS") for _ in range(2)]
        ps_t = [ph.ps([128, 2, 128], BF16, "pT") for _ in range(2)]
        ps_v = [ph.ps([128, 128], BF16, "pV") for _ in range(1)]
        ps_o = [ph.ps([128, 128], F32, "pO") for _ in range(1)]
        ps_ot = [ph.ps([128, 128], F32, "pOT") for _ in range(1)]
        ps_l = [ph.ps([128, 128], F32, "pL") for _ in range(1)]
        sm = [ph.sb([128, 8], F32, "sm") for _ in range(10)]
        units = []
        for h in (range(nheads) if "B" in parts else []):
            for g in range(3):
                for i in range(16):
                    units.append((h, g, i))

        def load_head(h):
            bufs = qkv[h % 2]
            for g in range(3):
                for jj in range(3):
                    kb.dma(kb.sp, bufs[g * 3 + jj][:], self.qkvT[(g * 3 + jj) * 16 + h])

        def prep_group(h, g):
            VT_ = qkv[h % 2][g * 3 + 2]
            vt = Vtm[(h * 3 + g) % 2]
            for i in range(16):
                pv = ps_v[0]
                kb.transpose(pv[:], VT_[:, i * 128:(i + 1) * 128], self.identb[:])
                kb.copy(kb.act, vt[:, i, :], pv[:])

        def geom(u):
            h, g, i = units[u]
            d = DIL[g]; nb = (S // d) // 128
            n = i % nb; nk = 2 if n > 0 else 1
            return h, g, i, d, nb, n, nk

        def st0(u):
            h, g, i, d, nb, n, nk = geom(u)
            if g == 0 and i == 0 and h + 1 < nheads: load_head(h + 1)
            if i == 0: prep_group(h, g)
            QT_, KT_ = qkv[h % 2][g * 3], qkv[h % 2][g * 3 + 1]
            k0 = (i - 1) * 128 if n > 0 else i * 128
            kb.mm(ps_s[u % 2][:, 0:nk * 128], QT_[:, i * 128:(i + 1) * 128], KT_[:, k0:k0 + nk * 128])

        def st1(u):
            h, g, i, d, nb, n, nk = geom(u)
            W_ = nk * 128; b = u % 3; s4 = sm[u % 10]; pS = ps_s[u % 2]
            coef = slopes[g][h] * d
            dv = self.consts[:, C_D1:C_D1 + 256] if nk == 2 else self.consts[:, C_D1 + 128:C_D1 + 256]
            kb.stt(kb.dve, Sb[b][:, 0:W_], dv, coef, pS[:, 0:W_], ALU.mult, ALU.add)
            kb.op(kb.dve, lambda a=s4, sb_=Sb[b], w_=W_: nc.vector.tensor_reduce(a.t[:, 1:2], sb_.t[:, 0:w_], AX.X, ALU.max, negate=True), [Sb[b][:, 0:W_]], [s4[:, 1:2]])

        def st2(u):
            h, g, i, d, nb, n, nk = geom(u)
            W_ = nk * 128; b = u % 3; s4 = sm[u % 10]
            kb.activation(Pb[b][:, 0:W_], Sb[b][:, 0:W_], AF.Exp, bias=s4[:, 1:2], accum=s4[:, 2:3])
            kb.activation(s4[:, 3:4], s4[:, 2:3], AF.Ln)

        def st3(u):
            h, g, i, d, nb, n, nk = geom(u)
            b = u % 3; pT = ps_t[u % 2]; s4 = sm[u % 10]
            for kt in range(nk):
                kb.transpose(pT[:, kt, :], Pb[b][:, kt * 128:(kt + 1) * 128], self.identb[:])
            kb.tt(kb.dve, s4[:, 4:5], s4[:, 3:4], s4[:, 1:2], ALU.subtract)
            kb.op(kb.dve, lambda a=s4: nc.vector.reciprocal(a.t[:, 5:6], a.t[:, 2:3]), [s4[:, 2:3]], [s4[:, 5:6]])

        def st4(u):
            h, g, i, d, nb, n, nk = geom(u)
            b = u % 3; pT = ps_t[u % 2]; s4 = sm[u % 10]
            kb.copy(kb.act, PT[b][:, 0:nk, :], pT[:, 0:nk, :])
            kb.activation(Dg[b][:], ident, AF.Copy, scale=s4[:, 4:5])

        def st5(u):
            h, g, i, d, nb, n, nk = geom(u)
            b = u % 3; pO = ps_o[0]; vt = Vtm[(h * 3 + g) % 2]
            for kt in range(nk):
                kb.mm(pO[:], PT[b][:, kt, :], vt[:, i - (nk - 1) + kt, :], start=(kt == 0), stop=(kt == nk - 1))
            kb.mm(ps_l[0][:], ones1[:], Dg[b][:])

        def st6(u):
            h, g, i, d, nb, n, nk = geom(u)
            b = u % 3; s4 = sm[u % 10]; pO = ps_o[0]
            kb.ts(kb.dve, Otm[b][:], pO[:], s4[:, 5:6], None, ALU.mult)
            r = i // nb
            c0 = r + d * 128 * n
            LgT = LgT2[h % 2]
            kb.copy(kb.dve, LgT[g].v(LgT[g].t[:, c0:c0 + d * 127 + 1:d]), ps_l[0][:])

        def st7(u):
            b = u % 3
            kb.transpose(ps_ot[0][:], Otm[b][:], ident)

        def st8(u):
            h, g, i, d, nb, n, nk = geom(u)
            r = i // nb
            c0 = r + d * 128 * n
            kb.copy(kb.act, OgT[g].v(OgT[g].t[:, c0:c0 + d * 127 + 1:d]), ps_ot[0][:])
            if g == 2 and i == 15: combine(h)

        def combine(h):
            LgT = LgT2[h % 2]
            kb.tt(kb.dve, m3[:], LgT[0][:], LgT[1][:], ALU.max)
            kb.tt(kb.dve, m3[:], m3[:], LgT[2][:], ALU.max)
            for g in range(3):
                kb.tt(kb.dve, LgT[g][:], LgT[g][:], m3[:], ALU.subtract)
                kb.activation(LgT[g][:], LgT[g][:], AF.Exp)
            kb.tt(kb.dve, ssum[:], LgT[0][:], LgT[1][:], ALU.add)
            kb.tt(kb.dve, ssum[:], ssum[:], LgT[2][:], ALU.add)
            kb.op(kb.dve, lambda: nc.vector.reciprocal(ssum.t[:], ssum.t[:]), [ssum[:]], [ssum[:]])
            for g in range(3):
                kb.tt(kb.dve if g != 1 else kb.dve, OgT[g][:], OgT[g][:], LgT[g][:], ALU.mult)
            kb.tt(kb.dve, OgT[0][:], OgT[0][:], OgT[1][:], ALU.add)
            kb.tt(kb.dve, OgT[0][:], OgT[0][:], OgT[2][:], ALU.add)
            ob = Ob[h % 2]
            kb.tt(kb.dve, ob[:], OgT[0][:], ssum[:], ALU.mult)
            kb.dma(kb.sp, self.oT[h], ob[:])

        stages = [st0, st1, st2, st3, st4, st5, st6, st7, st8]
        if units: load_head(0)
        for tick in range(len(units) + len(stages) - 1):
            for si in reversed(range(len(stages))):
                u = tick - si
                if 0 <= u < len(units): stages[si](u)
        ph.close()
        for s in (range(NPASS) if "C" in parts else []):
            t0 = s * T
            ph = Phase(kb, "atC%d_%d" % (li, s))
            HT = ph.sb([128, NCH, T], BF16, "HT")
            slots = [ph.sb([128, NCH * 128], BF16, "ws") for _ in range(3)]
            xr = [ph.sb([128, T], F32, "xr") for _ in range(2)]
            pss = [ph.ps([128, TT], F32, "ps") for _ in range(8)]
            for c in range(NCH):
                kb.dma(kb.sp, HT[:, c, :], self.oT[c, :, t0:t0 + T])
            self.proj_residual(ph, HT, Wo, NCH, t0, pss, slots, xr)
            ph.close()
        self.xcur = self.xT

    def conv(self):
        kb = self.kb
        Win = self.W["conv_w_in"][0].rearrange("(kc p) n -> p kc n", p=128)
        Wout = self.W["conv_w_out"][0].rearrange("(kc p) n -> p kc n", p=128)
        if not hasattr(self, "cv_halo"):
            self.cv_halo = self.pp.sb([128, NCH, 2], F32, "chalo")
        for s in range(NPASS):
            t0 = s * T
            ph = Phase(kb, "cv%d" % s)
            HT = ph.sb([128, NCH, T], BF16, "HT")
            ZT = ph.sb([128, NCH, T], BF16, "ZT")
            slots = [ph.sb([128, NCH * 3 * 128], BF16, "ws") for _ in range(3)]
            G = [ph.sb([128, T + 2], F32, "G") for _ in range(2)]
            Cv = [ph.sb([128, T], F32, "Cv") for _ in range(2)]
            Cs = [ph.sb([128, T], F32, "Cs") for _ in range(2)]
            xr = [ph.sb([128, T], F32, "xr") for _ in range(2)]
            pss = [ph.ps([128, TT], F32, "ps") for _ in range(8)]
            self.norm_tiles(ph)
            self.norm(ph, self.xcur, t0, "conv_norm", HT, pss[0:2])

            def mk_load(f):
                def ld(slot):
                    sv = slot.t[:, :].rearrange("p (k n) -> p k n", k=NCH)
                    for q in range(3):
                        self.wload(slot.v(sv[:, :, q * 128:(q + 1) * 128]), Win[:, :, q * D + f * 128:q * D + (f + 1) * 128])
                return ld

            def comp(f, slot):
                sv = slot.t[:, :].rearrange("p (k n) -> p k n", k=NCH)
                g = G[f % 2]; cv = Cv[f % 2]; cs = Cs[f % 2]
                if s == 0:
                    kb.memset(kb.dve, g[:, 0:2], 0.0)
                else:
                    kb.copy(kb.dve, g[:, 0:2], self.cv_halo[:, f, :])
                for j in range(NTT):
                    sl = slice(j * TT, (j + 1) * TT)
                    pc_, pu_, pb_ = pss[(3 * j) % 8], pss[(3 * j + 1) % 8], pss[(3 * j + 2) % 8]
                    for q, p in ((1, pc_), (2, pu_), (0, pb_)):
                        for c in range(NCH):
                            kb.mm(p[:], slot.v(sv[:, c, q * 128:(q + 1) * 128]), HT[:, c, sl], start=(c == 0), stop=(c == NCH - 1))
                    kb.copy(kb.act, cs[:, sl], pc_[:])
                    kb.tt(kb.dve, g[:, 2 + j * TT:2 + (j + 1) * TT], cs[:, sl], pu_[:], ALU.mult)
                    kb.copy(kb.act, cs[:, sl], pb_[:])
                kb.copy(kb.dve, self.cv_halo[:, f, :], g[:, T:T + 2])
                w = lambda j: self.vcol("conv_w%d" % j, f)
                kb.ts(kb.dve, cv[:], g[:, 2:T + 2], w(2), None, ALU.mult)
                kb.stt(kb.dve, cv[:], g[:, 1:T + 1], w(1), cv[:], ALU.mult, ALU.add)
                kb.stt(kb.dve, cv[:], g[:, 0:T], w(0), cv[:], ALU.mult, ALU.add)
                kb.tt(kb.dve, ZT[:, f, :], cv[:], cs[:], ALU.mult)
            self.stream(ph, slots, [mk_load(f) for f in range(NCH)], comp)
            self.proj_residual(ph, ZT, Wout, NCH, t0, pss, slots, xr)
            ph.close()
        self.xcur = self.xT


    def rwkv(self, nchunks=16, parts="ABC", dbg=9):
        kb = self.kb; nc = self.nc
        scr = lambda name, dt=F32: Tile(nc.dram_tensor(name, [NCH, 128, S], dt).ap(), name)
        R_, K_, V_, LW_, A_, G_ = [scr("rw_%s" % n) for n in "rkvwag"]
        YG = scr("rw_yg", BF16)
        hlast = self.pp.sb([128, NCH, 1], BF16, "hlast")
        negw0 = self.pp.sb([128, NCH], F32, "negw0")
        kb.ts(kb.dve, negw0[:], self.vecs[:, VCOL["w0"]:VCOL["w0"] + NCH], -1.0, None, ALU.mult)
        rearr = lambda w: w.rearrange("(kc p) n -> p kc n", p=128)
        for s in (range(NPASS) if "A" in parts else []):
            t0 = s * T
            ph = Phase(kb, "rwA%d" % s)
            HT = ph.sb([128, NCH, T], BF16, "HT"); XX = ph.sb([128, NCH, T], BF16, "XX"); Xn = ph.sb([128, NCH, T], BF16, "Xn")
            slots = [ph.sb([128, NCH * 256], BF16, "ws") for _ in range(3)]
            stage = [ph.sb([128, T], F32, "st") for _ in range(2)]
            T1 = ph.sb([128, 2, T], BF16, "T1")
            W2 = ph.sb([128, 2, D], BF16, "W2")
            pss = [ph.ps([128, TT], F32, "ps") for _ in range(8)]
            self.norm_tiles(ph)
            self.norm(ph, self.xcur, t0, "rwkv_norm", HT, pss[0:2])
            for c in range(NCH):
                if s == 0:
                    kb.ts(kb.dve, XX[:, c, 0:1], HT[:, c, 0:1], -1.0, None, ALU.mult)
                else:
                    kb.tt(kb.dve, XX[:, c, 0:1], hlast[:, c, :], HT[:, c, 0:1], ALU.subtract)
                kb.tt(kb.dve, XX[:, c, 1:T], HT[:, c, 0:T - 1], HT[:, c, 1:T], ALU.subtract)
            for c in range(NCH):
                kb.copy(kb.dve, hlast[:, c, :], HT[:, c, T - 1:T])

            def mix(n):
                for c in range(NCH):
                    kb.stt(kb.dve, Xn[:, c, :], XX[:, c, :], self.vcol("mu%d" % n, c), HT[:, c, :], ALU.mult, ALU.add)
            for n, dst in enumerate([R_, K_, V_]):
                mix(n)
                self.proj_to_dram(ph, Xn, rearr(self.W["rwkv_w_rkv"][0][n]), NCH, 16, lambda i: i * 128,
                                  lambda i, st, dst=dst: kb.dma(kb.sp, dst[i, :, t0:t0 + T], st[:]), pss, slots, stage)

            def lora(n, w1name, w2name, rnk, func1, epi, dst):
                mix(n)
                nm = (rnk + 127) // 128; rp = min(rnk, 128)
                sl = slots[0]
                sv = sl.t[:, 0:NCH * rnk].rearrange("p (k n) -> p k n", k=NCH)
                self.wload(sl.v(sv), rearr(self.W[w1name][0]))
                w2v = self.W[w2name][0].rearrange("(m p) n -> p m n", p=rp)
                self.wload(W2[0:rp, 0:nm, :], w2v)
                k = 0
                for m in range(nm):
                    for j in range(NTT):
                        p = pss[k % 8]; k += 1
                        for c in range(NCH):
                            kb.mm(p[0:rp, :], sl.v(sv[:, c, m * 128:m * 128 + rp]), Xn[:, c, j * TT:(j + 1) * TT], start=(c == 0), stop=(c == NCH - 1))
                        kb.activation(T1[0:rp, m, j * TT:(j + 1) * TT], p[0:rp, :], func1)
                for i in range(NCH):
                    st = stage[i % 2]
                    for j in range(NTT):
                        p = pss[k % 8]; k += 1
                        for m in range(nm):
                            kb.mm(p[:], W2[0:rp, m, i * 128:(i + 1) * 128], T1[0:rp, m, j * TT:(j + 1) * TT], start=(m == 0), stop=(m == nm - 1))
                        epi(i, st[:, j * TT:(j + 1) * TT], p)
                    kb.dma(kb.sp, dst[i, :, t0:t0 + T], st[:])

            def epi_w(i, o, p):
                kb.activation(o, p[:], AF.Exp, bias=negw0[:, i:i + 1], scale=-1.0)
                kb.ts(kb.dve, o, o, 1.0, None, ALU.add)
                kb.op(kb.dve, lambda: nc.vector.reciprocal(o.ap, o.ap), [o], [o])
                kb.ts(kb.dve, o, o, -0.6065306597126334, None, ALU.mult)
            lora(3, "rwkv_w_w1", "rwkv_w_w2", 96, AF.Tanh, epi_w, LW_)
            lora(4, "rwkv_w_a1", "rwkv_w_a2", 96, AF.Copy, lambda i, o, p: kb.activation(o, p[:], AF.Sigmoid, bias=self.vcol("a0", i)), A_)
            lora(5, "rwkv_w_g1", "rwkv_w_g2", 256, AF.Sigmoid, lambda i, o, p: kb.copy(kb.act, o, p[:]), G_)
            ph.close()
        ph = Phase(kb, "rwB")
        cst = lambda c0: self.consts[:, c0:c0 + 128]
        ident, SU, IU, SL, BD, BD1 = cst(C_ID), cst(C_SU), cst(C_IU), cst(C_SL), cst(C_BD), cst(C_BD1)
        GB = 4; GW = GB * 128; NQ = S // GW
        NSETS = RW_NSETS
        BN = ("r", "k", "v", "lw", "a", "g", "B1", "B2", "B3", "B4", "B5")
        sets = []
        for si in range(NSETS):
            d_ = {n: ph.sb([128, GW], F32, n) for n in BN}
            d_["gC"] = ph.sb([128, GB], F32, "gC"); d_["yg"] = ph.sb([128, GW], BF16, "yg")
            sets.append(d_)
        TMs = [[ph.sb([128, GB, 128], F32, n) for n in ("Vtm", "VP0", "VP1", "Btm", "Ktm")] for _ in range(2)]
        UM = [[[ph.sb([128, 128], F32, "um") for _ in range(4)] for _ in range(2 * GB)] for _ in range(2)]
        QQ = [[ph.sb([128, 128], F32, "qq") for _ in range(6)] for _ in range(2 * GB)]
        STP = ph.sb([128, 128], F32, "STP"); RH = ph.sb([128, 128], F32, "RH"); SAB = ph.sb([128, 128], F32, "SAB")
        SAP = [ph.sb([128, 128], F32, "SAP") for _ in range(2)]
        t1 = ph.sb([128, 128], F32, "t1")
        pg = [ph.ps([128, 128], F32, "pg") for _ in range(2)]
        pd = [ph.ps([128, 128], F32, "pd") for _ in range(3)]
        ptr = ph.ps([128, 512], F32, "ptr")
        pq = [ph.ps([128, 512], F32, "pq") for _ in range(2)]
        for q in range(2):
            kb.memset(kb.dve, TMs[q][1][:], 0.0); kb.memset(kb.dve, TMs[q][2][:], 0.0)
        kb.memset(kb.dve, SAP[0][:], 0.0); kb.memset(kb.dve, SAP[1][:], 0.0)
        recip = lambda o: kb.op(kb.dve, lambda: nc.vector.reciprocal(o.ap, o.ap), [o], [o])
        groups = [(c, q) for c in (range(nchunks) if "B" in parts else []) for q in range(NQ)]
        NGR = len(groups)
        SRC = (("r", R_), ("k", K_), ("v", V_), ("lw", LW_), ("a", A_), ("g", G_))

        def load(gi):
            c, q = groups[gi]; S_ = sets[gi % NSETS]
            for n, src in SRC:
                kb.dma(kb.sp, S_[n][:], src[c, :, q * GW:(q + 1) * GW])

        def prep_ops(gi):
            c, q = groups[gi]; S_ = sets[gi % NSETS]
            r, k, v, lw, a, g, B1, B2, B3, B4, B5 = [S_[n] for n in BN]
            v3 = lambda t_: t_.t[:, :].rearrange("p (n q) -> p n q", q=128)
            ops = []
            A = ops.append
            A(lambda: kb.ts(kb.dve, B1[:], k[:], self.vcol("k_k", c), None, ALU.mult))
            A(lambda: kb.activation(B2[:], B1[:], AF.Square))
            A(lambda: kb.mm(ptr[:], BD1, B2[:]))
            A(lambda: kb.activation(B3[:], ptr[:], AF.Sqrt))
            A(lambda: kb.ts(kb.dve, B3[:], B3[:], 1e-12, None, ALU.max))
            A(lambda: recip(B3[:]))
            A(lambda: kb.tt(kb.dve, B1[:], B1[:], B3[:], ALU.mult))
            A(lambda: kb.ts(kb.dve, B2[:], a[:], -1.0, self.vcol("k_a", c), ALU.add, ALU.mult))
            A(lambda: kb.stt(kb.dve, k[:], B2[:], 1.0, k[:], ALU.add, ALU.mult))
            A(lambda: kb.tt(kb.dve, a[:], B1[:], a[:], ALU.mult))
            src = lw; pp2 = [B2, B3]; sh = 1; qi = 0
            while sh < 128:
                dst = pp2[qi % 2]; qi += 1
                A(lambda dst=dst, src=src, sh=sh: kb.tt(kb.dve, dst.v(v3(dst)[:, :, sh:128]), src.v(v3(src)[:, :, sh:128]), src.v(v3(src)[:, :, 0:128 - sh]), ALU.add))
                A(lambda dst=dst, src=src, sh=sh: kb.copy(kb.act, dst.v(v3(dst)[:, :, 0:sh]), src.v(v3(src)[:, :, 0:sh])))
                src = dst; sh *= 2
            cum = src
            A(lambda: kb.activation(B4[:], cum[:], AF.Exp))
            A(lambda: kb.activation(B5[:], cum[:], AF.Exp, scale=-1.0))
            A(lambda: kb.tt(kb.dve, lw[:], cum[:], lw[:], ALU.subtract))
            A(lambda: kb.activation(lw[:], lw[:], AF.Exp))
            A(lambda: kb.copy(kb.dve, S_["gC"][:], B4.v(v3(B4)[:, :, 127])))
            A(lambda: kb.stt(kb.dve, B1[:], B1[:], -1.0, lw[:], ALU.mult, ALU.mult))
            A(lambda: kb.tt(kb.dve, a[:], a[:], B5[:], ALU.mult))
            A(lambda: kb.tt(kb.dve, B5[:], k[:], B5[:], ALU.mult))
            A(lambda: kb.tt(kb.dve, B4[:], r[:], B4[:], ALU.mult))
            return ops

        def front_a(gi, sl_):
            S_ = sets[gi % NSETS]
            At, Bt, Kt, Rt, v = S_["B1"], S_["a"], S_["B5"], S_["B4"], S_["v"]
            Vtm, VP0, VP1, Btm, Ktm = TMs[sl_]; UMs = UM[sl_]
            for bi in range(GB):
                bs = slice(bi * 128, (bi + 1) * 128)
                kb.mm(ptr[:, 0:128], v[:, bs], ident)
                kb.copy(kb.act, Vtm[:, bi, :], ptr[:, 0:128])
                kb.copy(kb.dve, VP0[:, bi, 0:64], Vtm[:, bi, 0:64])
                kb.copy(kb.dve, VP1[:, bi, 64:128], Vtm[:, bi, 64:128])
                kb.mm(ptr[:, 0:128], Bt[:, bs], ident)
                kb.copy(kb.dve, Btm[:, bi, :], ptr[:, 0:128])
                kb.mm(ptr[:, 0:128], Kt[:, bs], ident)
                kb.copy(kb.act, Ktm[:, bi, :], ptr[:, 0:128])
            gi_ = 0
            for u in range(2 * GB):
                bi, hh = u // 2, u % 2
                bs = slice(bi * 128, (bi + 1) * 128); hp = slice(64 * hh, 64 * hh + 64)
                Uak, Urb, Urk, XT = UMs[u]; Qa, Qb, La, Lb, Pa, Pb = QQ[u]
                for (lt, rt, msk, dst) in ((Bt, At, SU, Qa), (At, Bt, SL, La), (Kt, At, SU, Uak), (Bt, Rt, IU, Urb), (Kt, Rt, IU, Urk)):
                    p = pg[gi_ % 2]; gi_ += 1
                    kb.mm(p[:], lt[hp, bs], rt[hp, bs])
                    kb.tt(kb.dve, dst[:], p[:], msk, ALU.mult)
                kb.tt(kb.dve, Pa[:], Qa[:], ident, ALU.add)

        dstate = {"di": 0}

        def dbl_level(sl_, lev):
            for u in range(2 * GB):
                src_i = lev % 2
                Qc, Lc, Pc = QQ[u][0 + src_i], QQ[u][2 + src_i], QQ[u][4 + src_i]
                Qn, Ln, Pn = QQ[u][1 - src_i], QQ[u][3 - src_i], QQ[u][5 - src_i]
                last = (lev == 5)
                p = pd[dstate["di"] % 3]; dstate["di"] += 1
                kb.mm(p[:], Qc[:], Lc[:])
                kb.copy(kb.act, Ln[:], p[:])
                if not last:
                    p = pd[dstate["di"] % 3]; dstate["di"] += 1
                    kb.mm(p[:], Lc[:], Qc[:])
                    kb.copy(kb.act if u % 2 else kb.dve, Qn[:], p[:])
                p = pd[dstate["di"] % 3]; dstate["di"] += 1
                kb.mm(p[:], Ln[:], Pc[:])
                outP = UM[sl_][u][3] if last else Pn
                kb.tt(kb.dve, outP[:], p[:], Pc[:], ALU.add)

        def seq_block(gi, sl_, bi):
            c, q = groups[gi]; S_ = sets[gi % NSETS]
            At, Rt, Y, gC = S_["B1"], S_["B4"], S_["lw"], S_["gC"]
            Vtm, VP0, VP1, Btm, Ktm = TMs[sl_]; UMs = UM[sl_]
            bs = slice(bi * 128, (bi + 1) * 128)
            if q == 0 and bi == 0:
                kb.memset(kb.dve, STP[:], 0.0)
            p1 = pq[0]; p2 = pq[1]
            for hh in range(2):
                hp = slice(64 * hh, 64 * hh + 64); hc = slice(64 * hh, 64 * hh + 64)
                kb.mm(p1[:, hc], At[hp, bs], STP[hp, hc], start=True, stop=False)
                kb.mm(p1[:, hc], UMs[2 * bi + hh][0][:], Vtm[:, bi, hc], start=False, stop=True)
            kb.copy(kb.act, RH[:], p1[:, 0:128])
            for hh in range(2):
                hc = slice(64 * hh, 64 * hh + 64)
                kb.mm(p2[:, hc], UMs[2 * bi + hh][3][:], RH[:, hc])
            kb.copy(kb.act, SAB[:], p2[:, 0:128])
            kb.copy(kb.dve, SAP[0][:, 0:64], SAB[:, 0:64])
            kb.copy(kb.act, SAP[1][:, 64:128], SAB[:, 64:128])
            py = p1
            kb.mm(py[:, 128:256], STP[:], Rt[:, bs], start=True, stop=False)
            for hh in range(2):
                kb.mm(py[:, 128:256], SAP[hh][:], UMs[2 * bi + hh][1][:], start=False, stop=False)
            kb.mm(py[:, 128:256], VP0[:, bi, :], UMs[2 * bi][2][:], start=False, stop=False)
            kb.mm(py[:, 128:256], VP1[:, bi, :], UMs[2 * bi + 1][2][:], start=False, stop=True)
            kb.copy(kb.act, Y[:, bs], py[:, 128:256])
            pst = p2
            kb.mm(pst[:, 128:256], Btm[:, bi, :], SAB[:], start=True, stop=False)
            kb.mm(pst[:, 128:256], Ktm[:, bi, :], Vtm[:, bi, :], start=False, stop=True)
            kb.stt(kb.dve, t1[:], pst[:, 128:256], gC[:, bi:bi + 1], BD1, ALU.mult, ALU.mult)
            kb.stt(kb.dve, STP[:], STP[:], gC[:, bi:bi + 1], t1[:], ALU.mult, ALU.add)

        def post_ops(gi):
            c, q = groups[gi]; S_ = sets[gi % NSETS]
            r, k, v, Y, g, B1, B2, B3, yg = S_["r"], S_["k"], S_["v"], S_["lw"], S_["g"], S_["B1"], S_["B2"], S_["B3"], S_["yg"]
            ops = []
            A = ops.append
            A(lambda: kb.mm(ptr[:], BD, Y[:]))
            A(lambda: kb.tt(kb.dve, B2[:], Y[:], ptr[:], ALU.subtract))
            A(lambda: kb.activation(B3[:], B2[:], AF.Square))
            A(lambda: kb.mm(ptr[:], BD, B3[:]))
            A(lambda: kb.activation(B3[:], ptr[:], AF.Sqrt, bias=self.consts[:, C_GNE:C_GNE + 1]))
            A(lambda: recip(B3[:]))
            A(lambda: kb.tt(kb.dve, B2[:], B2[:], B3[:], ALU.mult))
            A(lambda: kb.ts(kb.dve, B2[:], B2[:], self.vcol("ln_g", c), self.vcol("ln_b", c), ALU.mult, ALU.add))
            A(lambda: kb.tt(kb.dve, B3[:], r[:], k[:], ALU.mult))
            A(lambda: kb.ts(kb.dve, B3[:], B3[:], self.vcol("r_k", c), None, ALU.mult))
            A(lambda: kb.mm(ptr[:], BD1, B3[:]))
            A(lambda: kb.tt(kb.dve, B1[:], ptr[:], v[:], ALU.mult))
            A(lambda: kb.tt(kb.dve, B2[:], B2[:], B1[:], ALU.add))
            A(lambda: kb.tt(kb.dve, yg[:], B2[:], g[:], ALU.mult))
            A(lambda: kb.dma(kb.sp, YG[c, :, q * GW:(q + 1) * GW], yg[:]))
            return ops

        if NGR:
            for gi in range(min(NSETS - 1, NGR)): load(gi)
            for gi in range(min(2, NGR)):
                for o in prep_ops(gi): o()
            front_a(0, 0)
            for lev in range(6): dbl_level(0, lev)
        for i in range(NGR):
            sl_ = i % 2
            items = []
            if i + 1 < NGR:
                front_a(i + 1, 1 - sl_)
                for o in ("L0", "L1", "S0", "L2", "S1", "L3", "S2", "L4", "S3", "L5"):
                    if o[0] == "L": items.append(lambda lev=int(o[1]): dbl_level(1 - sl_, lev))
                    else: items.append(lambda bi=int(o[1]): seq_block(i, sl_, bi))
            else:
                for bi in range(GB): items.append(lambda bi=bi: seq_block(i, sl_, bi))
            extras = []
            if i >= 1: extras += post_ops(i - 1)
            if i + 2 < NGR: extras += prep_ops(i + 2)
            per = (len(extras) + len(items) - 1) // len(items) if extras else 0
            ei = 0
            for it_ in items:
                it_()
                for _ in range(per):
                    if ei < len(extras): extras[ei](); ei += 1
            while ei < len(extras): extras[ei](); ei += 1
            if i + NSETS - 1 < NGR: load(i + NSETS - 1)
        if NGR:
            for o in post_ops(NGR - 1): o()
        ph.close()
        Wo = rearr(self.W["rwkv_w_o"][0])
        for s in (range(NPASS) if "C" in parts else []):
            t0 = s * T
            ph = Phase(kb, "rwC%d" % s)
            HT = ph.sb([128, NCH, T], BF16, "HT")
            slots = [ph.sb([128, NCH * 128], BF16, "ws") for _ in range(3)]
            xr = [ph.sb([128, T], F32, "xr") for _ in range(2)]
            pss = [ph.ps([128, TT], F32, "ps") for _ in range(8)]
            for c in range(NCH):
                kb.dma(kb.sp, HT[:, c, :], YG[c, :, t0:t0 + T])
            self.proj_residual(ph, HT, Wo, NCH, t0, pss, slots, xr)
            ph.close()
        self.xcur = self.xT

    def final(self):
        kb = self.kb
        for s in range(NPASS):
            ph = Phase(kb, "fin%d" % s)
            pss = [ph.ps([128, TT], F32, "ps") for _ in range(2)]
            self.norm_tiles(ph)
            self.norm(ph, self.xcur, s * T, "final_norm", None, pss, out_dram=self.out)
            ph.close()

    def copy_x(self):
        kb = self.kb
        ph = Phase(kb, "cp")
        xt = [ph.sb([128, S], F32, "x") for _ in range(2)]
        for c in range(NCH):
            kb.dma(kb.sp, xt[c % 2][:], self.xcur[c, :, :])
            kb.dma(kb.sp, self.out[c, :, :], xt[c % 2][:])
        ph.close()

    def build(self):
        for p in self.phases:
            if p[0] == "ffn": self.ffn(p[1])
            elif p[0] == "ple": self.ple(p[1])
            elif p[0] == "attn": self.attn(*p[1:])
            elif p[0] == "rwkv": self.rwkv(*p[1:])
            elif p[0] == "conv": self.conv()
            elif p[0] == "final": self.final()
            elif p[0] == "copy": self.copy_x()
        self.kb.barrier()
        self.pp.es.close()


WSHAPES = {
    "attn_w_qkv": [2, D, 9 * D], "attn_w_o": [2, D, D],
    "rwkv_w_rkv": [1, 3, D, D], "rwkv_w_w1": [1, D, 96], "rwkv_w_w2": [1, 96, D],
    "rwkv_w_a1": [1, D, 96], "rwkv_w_a2": [1, 96, D], "rwkv_w_g1": [1, D, 256], "rwkv_w_g2": [1, 256, D],
    "rwkv_w_o": [1, D, D],
    "conv_w_in": [1, D, 3 * D], "conv_w_out": [1, D, D],
    "ffn_w_gu": [DEPTH, D, 2 * FF], "ffn_w_down": [DEPTH, FF, D],
    "ple_w_proj": [DEPTH, PLE, D], "ple_w_gate": [DEPTH, D, D],
}

def wnames(phases):
    pre = {"ffn": "ffn_", "ple": "ple_", "attn": "attn_", "rwkv": "rwkv_", "conv": "conv_"}
    used = set(pre[p[0]] for p in phases if p[0] in pre)
    return [n for n in WSHAPES if any(n.startswith(u) for u in used)]


FULL_PHASES = []
for _i in range(DEPTH):
    _k, _j = _i % 3, _i // 3
    FULL_PHASES.append(("attn", _j, _i) if _k == 0 else (("rwkv",) if _k == 1 else ("conv",)))
    FULL_PHASES.append(("ffn", _i)); FULL_PHASES.append(("ple", _i))
FULL_PHASES.append(("final",))


def build_nc(phases):
    nc = bass.Bass("TRN2", target_bir_lowering=False)
    es = contextlib.ExitStack()
    Prog(nc, es, phases)
    es.close()
    return nc


def make_vecs(inp):
    v = np.zeros((128, NV), np.float32)

    def put(name, arr, n=NCH):
        v[:, VCOL[name]:VCOL[name] + n] = pc(arr, n)
    for j in range(2): put("attn_norm%d" % j, inp["attn_norm"][j])
    put("rwkv_norm", inp["rwkv_norm"][0]); put("conv_norm", inp["conv_norm"][0])
    for i in range(DEPTH):
        put("ffn_norm%d" % i, inp["ffn_norm"][i]); put("ple_norm%d" % i, inp["ple_norm"][i])
        for j in range(3): put("fcw%d_%d" % (i, j), inp["ffn_conv_w"][i, j], NFC)
        put("fcb%d" % i, inp["ffn_conv_b"][i], NFC)
    put("final_norm", inp["final_norm"])
    for j in range(6): put("mu%d" % j, inp["rwkv_mu"][0, j])
    for nm in ("w0", "a0", "k_k", "k_a", "ln_g", "ln_b"): put(nm, inp["rwkv_" + nm][0])
    put("r_k", np.asarray(inp["rwkv_r_k"][0]).reshape(-1))
    for j in range(3): put("conv_w%d" % j, inp["conv_w"][0, j])
    return v


def run(inputs, phases, ncores=8):
    inp = {k: np.asarray(v) for k, v in inputs.items()}
    nc = build_nc(phases)
    vecs = make_vecs(inp); consts = build_consts()
    in_maps = []
    for c in range(ncores):
        b = c % NB
        m = {"xT": np.ascontiguousarray(inp["x"][b].T.reshape(NCH, 128, S)),
             "pT": np.ascontiguousarray(inp["p"][:, b].transpose(0, 2, 1).reshape(DEPTH, 2, 128, S)),
             "vecs": vecs, "consts": consts}
        for name in wnames(phases):
            m[name] = np.ascontiguousarray(inp[name], dtype=np.float32)
        in_maps.append(m)
    res = run_bass_kernel_spmd(nc, in_maps, core_ids=list(range(ncores)))
    outs = [res.results[b]["outT"].reshape(D, S).T for b in range(min(NB, ncores))]
    return np.stack(outs, 0)


def kernel(**inputs):
    out = run(inputs, FULL_PHASES)
    return np.ascontiguousarray(out.astype(np.float32))
```

```python
import contextlib
import numpy as np
import concourse.bass as bass
import concourse.mybir as mybir
from concourse.bass_utils import run_bass_kernel_spmd

F32 = mybir.dt.float32
BF16 = mybir.dt.bfloat16
AF = mybir.ActivationFunctionType
ALU = mybir.AluOpType
AX = mybir.AxisListType

D = 2048; S = 2048; NB = 4; DEPTH = 4; FF = 5632; PLE = 256
NCH = 16; NFC = 44; T = 1024; NTT = 2; TT = 512
EPS = 1e-6
RW_NSETS = 5
NPASS = S // T


class Tile:
    def __init__(self, t, name):
        self.t = t; self.name = name; self.w = {}; self.r = {}; self.dsem = None

    def __getitem__(self, idx):
        return View(self, self.t[idx])

    def v(self, ap):
        return View(self, ap)


class View:
    def __init__(self, tile, ap):
        self.tile = tile; self.ap = ap


class Eng:
    def __init__(self, kb, h, name):
        self.h = h; self.name = name; self.sid = kb.new_sem("e_" + name); self.waited = {}


class KB:
    def __init__(self, nc, es):
        self.nc = nc; self.es = es
        self.sems = []; self.cnt = []
        self.pe = Eng(self, nc.tensor, "pe"); self.act = Eng(self, nc.scalar, "act")
        self.dve = Eng(self, nc.vector, "dve"); self.pool = Eng(self, nc.gpsimd, "pool")
        self.sp = Eng(self, nc.sync, "sp")
        self.engs = [self.pe, self.act, self.dve, self.pool, self.sp]
        self.free_dsems = {"sp": [self.new_sem("d%d" % i) for i in range(64)], "pool": [self.new_sem("g%d" % i) for i in range(32)]}
        self.n_inst = 0

    def new_sem(self, name):
        h = self.es.enter_context(self.nc.semaphore(name))
        self.sems.append(h); self.cnt.append(0)
        return len(self.sems) - 1

    def op(self, E, fn, reads, writes, inc=True, dma_tile=None, nowait_writes=False):
        waits = {}
        for v in reads:
            for k, val in v.tile.w.items():
                if waits.get(k, 0) < val: waits[k] = val
        for v in ([] if nowait_writes else writes):
            for dd in (v.tile.w, v.tile.r):
                for k, val in dd.items():
                    if waits.get(k, 0) < val: waits[k] = val
        for k, val in waits.items():
            if E.waited.get(k, 0) < val:
                E.h.wait_ge(self.sems[k], val); E.waited[k] = val
        ins = fn()
        self.n_inst += 1
        if dma_tile is not None:
            if dma_tile.dsem is None:
                dma_tile.dsem = {}
            if E.name not in dma_tile.dsem:
                dma_tile.dsem[E.name] = self.free_dsems[E.name].pop()
            k = dma_tile.dsem[E.name]
            self.cnt[k] += 16
            ins.then_inc(self.sems[k], 16)
            tok = (k, self.cnt[k])
        elif inc:
            k = E.sid
            self.cnt[k] += 1
            ins.then_inc(self.sems[k], 1)
            tok = (k, self.cnt[k])
        else:
            tok = (E.sid, self.cnt[E.sid] + 1)
        for v in reads:
            if v.tile.r.get(tok[0], 0) < tok[1]: v.tile.r[tok[0]] = tok[1]
        for v in writes:
            v.tile.w = {tok[0]: tok[1]}; v.tile.r = {}
        return ins

    def release(self, tiles):
        for t in tiles:
            if t.dsem is not None:
                for q, k in t.dsem.items(): self.free_dsems[q].append(k)
                t.dsem = None

    def barrier(self):
        for E in self.engs:
            for k in range(len(self.sems)):
                val = self.cnt[k]
                if val > 0 and E.waited.get(k, 0) < val:
                    E.h.wait_ge(self.sems[k], val); E.waited[k] = val

    def mm(self, out, lhsT, rhs, start=True, stop=True):
        wr = [out] if (start or stop) else []
        return self.op(self.pe, lambda: self.nc.tensor.matmul(out.ap, lhsT.ap, rhs.ap, start=start, stop=stop),
                       [lhsT, rhs], wr, inc=True, nowait_writes=not start)

    def transpose(self, out, in_, ident):
        return self.op(self.pe, lambda: self.nc.tensor.transpose(out.ap, in_.ap, ident.ap), [in_, ident], [out])

    def activation(self, out, in_, func, bias=None, scale=1.0, accum=None, E=None):
        E = E or self.act
        reads = [in_]; kw = {}
        if isinstance(bias, View): reads.append(bias); kw["bias"] = bias.ap
        elif bias is not None: kw["bias"] = bias
        if isinstance(scale, View): reads.append(scale); kw["scale"] = scale.ap
        else: kw["scale"] = scale
        writes = [out]
        if accum is not None: writes.append(accum); kw["accum_out"] = accum.ap
        return self.op(E, lambda: E.h.activation(out=out.ap, in_=in_.ap, func=func, **kw), reads, writes)

    def tt(self, E, out, in0, in1, op):
        return self.op(E, lambda: E.h.tensor_tensor(out.ap, in0.ap, in1.ap, op), [in0, in1], [out])

    def ts(self, E, out, in0, s1, s2, op0, op1=None):
        reads = [in0]
        a1 = s1.ap if isinstance(s1, View) else s1
        a2 = s2.ap if isinstance(s2, View) else s2
        if isinstance(s1, View): reads.append(s1)
        if isinstance(s2, View): reads.append(s2)
        if op1 is None:
            return self.op(E, lambda: E.h.tensor_scalar(out.ap, in0.ap, a1, None, op0), reads, [out])
        return self.op(E, lambda: E.h.tensor_scalar(out.ap, in0.ap, a1, a2, op0, op1), reads, [out])

    def stt(self, E, out, in0, sc, in1, op0, op1):
        reads = [in0, in1]
        a = sc.ap if isinstance(sc, View) else sc
        if isinstance(sc, View): reads.append(sc)
        return self.op(E, lambda: E.h.scalar_tensor_tensor(out.ap, in0.ap, a, in1.ap, op0, op1), reads, [out])

    def copy(self, E, out, in_):
        if E is self.act:
            return self.activation(out, in_, AF.Copy)
        return self.op(E, lambda: E.h.tensor_copy(out.ap, in_.ap), [in_], [out])

    def reduce(self, E, out, in_, op, axis=AX.X):
        return self.op(E, lambda: E.h.tensor_reduce(out.ap, in_.ap, axis, op), [in_], [out])

    def memset(self, E, out, val):
        return self.op(E, lambda: E.h.memset(out.ap, val), [], [out])

    def dma(self, Q, out, in_, sb=None):
        if sb is None:
            sb = out.tile if getattr(out.tile, "is_sb", False) else in_.tile
        return self.op(Q, lambda: Q.h.dma_start(out=out.ap, in_=in_.ap), [in_], [out], dma_tile=sb)


class Phase:
    def __init__(self, kb, name):
        self.kb = kb; self.name = name; self.es = contextlib.ExitStack(); self.tiles = []; self.n = 0

    def sb(self, shape, dt, name=None):
        self.n += 1
        nm = "%s_%s%d" % (self.name, name or "t", self.n)
        t = Tile(self.es.enter_context(self.kb.nc.sbuf_tensor(nm, list(shape), dt)), nm)
        t.is_sb = True
        self.tiles.append(t)
        return t

    def ps(self, shape, dt, name=None):
        self.n += 1
        nm = "%s_%s%d" % (self.name, name or "p", self.n)
        t = Tile(self.es.enter_context(self.kb.nc.psum_tensor(nm, list(shape), dt)), nm)
        self.tiles.append(t)
        return t

    def close(self):
        self.kb.barrier()
        self.kb.release(self.tiles)
        self.es.close()


def vec_layout():
    cols = {}; n = 0

    def add(name, w):
        nonlocal n
        cols[name] = n; n += w
    for j in range(2): add("attn_norm%d" % j, NCH)
    add("rwkv_norm", NCH); add("conv_norm", NCH)
    for i in range(DEPTH): add("ffn_norm%d" % i, NCH)
    for i in range(DEPTH): add("ple_norm%d" % i, NCH)
    add("final_norm", NCH)
    for j in range(6): add("mu%d" % j, NCH)
    for nm in ("w0", "a0", "k_k", "k_a", "ln_g", "ln_b", "r_k"): add(nm, NCH)
    for j in range(3): add("conv_w%d" % j, NCH)
    for i in range(DEPTH):
        for j in range(3): add("fcw%d_%d" % (i, j), NFC)
        add("fcb%d" % i, NFC)
    return cols, n


VCOL, NV = vec_layout()
C_ID = 0; C_ONES = 128; C_D1 = 256; C_SU = 512; C_IU = 640; C_SL = 768; C_BD = 896; C_EPS = 1024; C_GNE = 1025; C_BD1 = 1032; NCONST = 1160


def build_consts():
    c = np.zeros((128, NCONST), np.float32)
    c[:, C_ID:C_ID + 128] = np.eye(128)
    c[:, C_ONES:C_ONES + 128] = 1.0 / D
    qi = np.arange(128)[:, None]; kj = np.arange(256)[None, :]
    dist = qi - kj + 128
    c[:, C_D1:C_D1 + 256] = np.where((dist >= 0) & (dist <= 128), -dist.astype(np.float32), -1.0e6)
    i = np.arange(128)[:, None]; t = np.arange(128)[None, :]
    c[:, C_SU:C_SU + 128] = (i < t); c[:, C_IU:C_IU + 128] = (i <= t); c[:, C_SL:C_SL + 128] = (i > t)
    c[:, C_BD:C_BD + 128] = ((i // 64) == (t // 64)) / 64.0
    c[:, C_EPS] = EPS
    c[:, C_GNE] = 6.4e-4
    c[:, C_BD1:C_BD1 + 128] = ((i // 64) == (t // 64))
    return c


def pc(v, n):
    return np.ascontiguousarray(np.asarray(v, np.float32).reshape(n, 128).T)


class Prog:
    def __init__(self, nc, es, phases):
        self.nc = nc; self.kb = KB(nc, es); self.es = es; self.phases = phases
        kb = self.kb
        dt = lambda name, shape, kind="ExternalInput": nc.dram_tensor(name, list(shape), F32, kind=kind).ap()
        self.xT_in = dt("xT", [NCH, 128, S])
        self.pT = dt("pT", [DEPTH, 2, 128, S])
        self.vecs_d = dt("vecs", [128, NV]); self.consts_d = dt("consts", [128, NCONST])
        self.W = {}
        for name in wnames(phases):
            self.W[name] = dt(name, WSHAPES[name])
        self.out = Tile(dt("outT", [NCH, 128, S], kind="ExternalOutput"), "outT")
        self.xT = Tile(nc.dram_tensor("xT_s", [NCH, 128, S], F32).ap(), "xT_s")
        self.xin = Tile(self.xT_in, "xT_in")
        self.wtile = Tile(None, "weights")
        self.pp = Phase(kb, "g")
        self.vecs = self.pp.sb([128, NV], F32, "vecs"); self.consts = self.pp.sb([128, NCONST], F32, "consts")
        self.ffn_halo = self.pp.sb([128, NFC, 2], F32, "fhalo")
        self.identb = self.pp.sb([128, 128], BF16, "identb")
        kb.dma(kb.sp, self.vecs[:], View(Tile(self.vecs_d, "vd"), self.vecs_d))
        kb.dma(kb.sp, self.consts[:], View(Tile(self.consts_d, "cd"), self.consts_d))
        kb.copy(kb.dve, self.identb[:], self.consts[:, C_ID:C_ID + 128])
        self.xcur = self.xin
        self.build()

    def vcol(self, name, c=0):
        k = VCOL[name] + c
        return self.vecs[:, k:k + 1]

    def norm(self, ph, xsrc, t0, gname, HT, ps_pool, out_dram=None):
        kb = self.kb
        ones = self.consts[:, C_ONES:C_ONES + 128]
        k = 0
        for j in range(NTT):
            ps = ps_pool[j % len(ps_pool)]
            tsl = slice(t0 + j * TT, t0 + (j + 1) * TT)
            for c in range(NCH):
                xt = ph.xn[k % 4]; k += 1
                kb.dma(kb.sp, xt[:], xsrc[c, :, tsl])
                sq = ph.sq[c % 2]
                kb.activation(sq[:], xt[:], AF.Square)
                kb.mm(ps[:], ones, sq[:], start=(c == 0), stop=(c == NCH - 1))
            rstd = ph.rstd
            kb.activation(rstd[:], ps[:], AF.Sqrt, bias=self.consts[:, C_EPS:C_EPS + 1])
            kb.op(kb.dve, lambda: self.nc.vector.reciprocal(rstd.t[:], rstd.t[:]), [rstd[:]], [rstd[:]])
            for c in range(NCH):
                xt = ph.xn[k % 4]; k += 1
                kb.dma(kb.sp, xt[:], xsrc[c, :, tsl])
                E = kb.dve
                if out_dram is None:
                    kb.stt(E, HT[:, c, j * TT:(j + 1) * TT], xt[:], self.vcol(gname, c), rstd[:], ALU.mult, ALU.mult)
                else:
                    kb.stt(E, xt[:], xt[:], self.vcol(gname, c), rstd[:], ALU.mult, ALU.mult)
                    kb.dma(kb.sp, out_dram[c, :, tsl], xt[:])

    def norm_tiles(self, ph):
        ph.xn = [ph.sb([128, TT], F32, "xn") for _ in range(4)]
        ph.sq = [ph.sb([128, TT], F32, "sq") for _ in range(2)]
        ph.rstd = ph.sb([128, TT], F32, "rstd")

    def stream(self, ph, slots, loads, compute):
        kb = self.kb
        ns = len(slots)
        n = len(loads)
        for i in range(min(ns - 1, n)):
            loads[i](slots[i % ns])
        for i in range(n):
            if i + ns - 1 < n:
                loads[i + ns - 1](slots[(i + ns - 1) % ns])
            compute(i, slots[i % ns])

    def wload(self, slot_view, wap):
        kb = self.kb
        kb.dma(kb.pool, slot_view, View(self.wtile, wap), sb=slot_view.tile)

    def ffn(self, li):
        kb = self.kb
        Wgu = self.W["ffn_w_gu"][li].rearrange("(kc p) n -> p kc n", p=128)
        Wdn = self.W["ffn_w_down"][li].rearrange("(kc p) n -> p kc n", p=128)
        for s in range(NPASS):
            t0 = s * T
            ph = Phase(kb, "ffn%d_%d" % (li, s))
            HT = ph.sb([128, NCH, T], BF16, "HT")
            ACT = ph.sb([128, NFC, T], BF16, "ACT")
            slots = [ph.sb([128, NFC * 128], BF16, "ws") for _ in range(3)]
            G = [ph.sb([128, T + 2], F32, "G") for _ in range(2)]
            Cv = [ph.sb([128, T], F32, "Cv") for _ in range(2)]
            xr = [ph.sb([128, T], F32, "xr") for _ in range(2)]
            pss = [ph.ps([128, TT], F32, "ps") for _ in range(8)]
            self.norm_tiles(ph)
            self.norm(ph, self.xcur, t0, "ffn_norm%d" % li, HT, pss[0:2])
            def mk_load_gu(f):
                def ld(slot):
                    sv = slot.t[:, 0:NCH * 256].rearrange("p (k n) -> p k n", k=NCH)
                    self.wload(slot.v(sv[:, :, 0:128]), Wgu[:, :, f * 128:(f + 1) * 128])
                    self.wload(slot.v(sv[:, :, 128:256]), Wgu[:, :, FF + f * 128:FF + (f + 1) * 128])
                return ld

            def comp_gu(f, slot):
                sv = slot.t[:, 0:NCH * 256].rearrange("p (k n) -> p k n", k=NCH)
                g = G[f % 2]; cv = Cv[f % 2]
                if s == 0:
                    kb.memset(kb.dve, g[:, 0:2], 0.0)
                else:
                    kb.copy(kb.dve, g[:, 0:2], self.ffn_halo[:, f, :])
                pg = [pss[(4 * f + j) % 8] for j in range(2)]
                pu = [pss[(4 * f + 2 + j) % 8] for j in range(2)]
                for j in range(NTT):
                    for c in range(NCH):
                        kb.mm(pg[j][:], slot.v(sv[:, c, 0:128]), HT[:, c, j * TT:(j + 1) * TT], start=(c == 0), stop=(c == NCH - 1))
                    kb.copy(kb.act, g[:, 2 + j * TT: 2 + (j + 1) * TT], pg[j][:])
                for j in range(NTT):
                    for c in range(NCH):
                        kb.mm(pu[j][:], slot.v(sv[:, c, 128:256]), HT[:, c, j * TT:(j + 1) * TT], start=(c == 0), stop=(c == NCH - 1))
                kb.copy(kb.dve, self.ffn_halo[:, f, :], g[:, T:T + 2])
                w = lambda j: self.vcol("fcw%d_%d" % (li, j), f)
                kb.ts(kb.dve, cv[:], g[:, 2:T + 2], w(2), self.vcol("fcb%d" % li, f), ALU.mult, ALU.add)
                kb.stt(kb.dve, cv[:], g[:, 1:T + 1], w(1), cv[:], ALU.mult, ALU.add)
                kb.stt(kb.dve, cv[:], g[:, 0:T], w(0), cv[:], ALU.mult, ALU.add)
                kb.activation(cv[:], cv[:], AF.Silu)
                for j in range(NTT):
                    kb.tt(kb.dve, ACT[:, f, j * TT:(j + 1) * TT], cv[:, j * TT:(j + 1) * TT], pu[j][:], ALU.mult)
            self.stream(ph, slots, [mk_load_gu(f) for f in range(NFC)], comp_gu)

            def mk_load_dn(d):
                def ld(slot):
                    sv = slot.t[:, :].rearrange("p (k n) -> p k n", k=NFC)
                    self.wload(slot.v(sv), Wdn[:, :, d * 128:(d + 1) * 128])
                return ld

            def comp_dn(d, slot):
                sv = slot.t[:, :].rearrange("p (k n) -> p k n", k=NFC)
                x = xr[d % 2]
                kb.dma(kb.sp, x[:], self.xcur[d, :, t0:t0 + T])
                for j in range(NTT):
                    p = pss[(2 * d + j) % 8]
                    for c in range(NFC):
                        kb.mm(p[:], slot.v(sv[:, c, :]), ACT[:, c, j * TT:(j + 1) * TT], start=(c == 0), stop=(c == NFC - 1))
                    kb.tt(kb.dve, x[:, j * TT:(j + 1) * TT], x[:, j * TT:(j + 1) * TT], p[:], ALU.add)
                kb.dma(kb.sp, self.xT[d, :, t0:t0 + T], x[:])
            self.stream(ph, slots, [mk_load_dn(d) for d in range(NCH)], comp_dn)
            ph.close()
        self.xcur = self.xT

    def ple(self, li):
        kb = self.kb
        Wg = self.W["ple_w_gate"][li].rearrange("(kc p) n -> p kc n", p=128)
        Wp = self.W["ple_w_proj"][li].rearrange("(kc p) n -> p kc n", p=128)
        pT = Tile(self.pT, "pT")
        ones = self.consts[:, C_ONES:C_ONES + 128]
        for s in range(NPASS):
            t0 = s * T
            ph = Phase(kb, "ple%d_%d" % (li, s))
            HT = ph.sb([128, NCH, T], BF16, "HT")
            PT = ph.sb([128, 2, T], BF16, "PT")
            X = [ph.sb([128, T], F32, "X") for _ in range(NCH)]
            slots = [ph.sb([128, 18 * 128], BF16, "ws") for _ in range(3)]
            sg = [ph.sb([128, T], F32, "sg") for _ in range(2)]
            sq = [ph.sb([128, TT], F32, "sq") for _ in range(2)]
            rstd = [ph.sb([128, TT], F32, "rstd") for _ in range(NTT)]
            pss = [ph.ps([128, TT], F32, "ps") for _ in range(8)]
            for c in range(2):
                kb.dma(kb.pool, PT[:, c, :], pT[li, c, :, t0:t0 + T])
            for c in range(NCH):
                kb.dma(kb.sp, X[c][:], self.xcur[c, :, t0:t0 + T])
            for j in range(NTT):
                sl = slice(j * TT, (j + 1) * TT)
                for c in range(NCH):
                    kb.activation(sq[c % 2][:], X[c][:, sl], AF.Square)
                    kb.mm(pss[j][:], ones, sq[c % 2][:], start=(c == 0), stop=(c == NCH - 1))
                kb.activation(rstd[j][:], pss[j][:], AF.Sqrt, bias=self.consts[:, C_EPS:C_EPS + 1])
                kb.op(kb.dve, lambda r_=rstd[j]: self.nc.vector.reciprocal(r_.t[:], r_.t[:]), [rstd[j][:]], [rstd[j][:]])
                for c in range(NCH):
                    kb.stt(kb.dve, HT[:, c, sl], X[c][:, sl], self.vcol("ple_norm%d" % li, c), rstd[j][:], ALU.mult, ALU.mult)

            def mk_load(d):
                def ld(slot):
                    sv = slot.t[:, :].rearrange("p (k n) -> p k n", k=18)
                    self.wload(slot.v(sv[:, 0:16, :]), Wg[:, :, d * 128:(d + 1) * 128])
                    self.wload(slot.v(sv[:, 16:18, :]), Wp[:, :, d * 128:(d + 1) * 128])
                return ld

            def comp(d, slot):
                sv = slot.t[:, :].rearrange("p (k n) -> p k n", k=18)
                x = X[d]; g = sg[d % 2]
                for j in range(NTT):
                    pa = pss[(4 * d + j) % 8]; pb = pss[(4 * d + 2 + j) % 8]
                    sl = slice(j * TT, (j + 1) * TT)
                    for c in range(NCH):
                        kb.mm(pa[:], slot.v(sv[:, c, :]), HT[:, c, sl], start=(c == 0), stop=(c == NCH - 1))
                    for c in range(2):
                        kb.mm(pb[:], slot.v(sv[:, 16 + c, :]), PT[:, c, sl], start=(c == 0), stop=(c == 1))
                    kb.activation(g[:, sl], pa[:], AF.Sigmoid)
                    kb.tt(kb.dve, g[:, sl], g[:, sl], pb[:], ALU.mult)
                    kb.tt(kb.dve, x[:, sl], x[:, sl], g[:, sl], ALU.add)
                kb.dma(kb.sp, self.xT[d, :, t0:t0 + T], x[:])
            self.stream(ph, slots, [mk_load(d) for d in range(NCH)], comp)
            ph.close()
        self.xcur = self.xT

    def proj_to_dram(self, ph, HT, Wv, KC, ntiles, colfn, dstfn, pss, slots, stage, scalefn=None, dil=None):
        kb = self.kb
        def mk_load(i):
            def ld(slot):
                sv = slot.t[:, 0:KC * 128].rearrange("p (k n) -> p k n", k=KC)
                self.wload(slot.v(sv), Wv[:, :, colfn(i):colfn(i) + 128])
            return ld

        def comp(i, slot):
            sv = slot.t[:, 0:KC * 128].rearrange("p (k n) -> p k n", k=KC)
            st = stage[i % len(stage)]
            d = dil(i) if dil else 1
            for j in range(NTT):
                p = pss[(2 * i + j) % len(pss)]
                for c in range(KC):
                    kb.mm(p[:], slot.v(sv[:, c, :]), HT[:, c, j * TT:(j + 1) * TT], start=(c == 0), stop=(c == KC - 1))
                sc = scalefn(i) if scalefn else 1.0
                if d == 1:
                    kb.activation(st[:, j * TT:(j + 1) * TT], p[:], AF.Copy, scale=sc)
                else:
                    o = st.t[:, :].rearrange("p (r l) -> p r l", r=d)[:, :, j * TT // d:(j + 1) * TT // d]
                    i_ = p.t[:, :].rearrange("p (l r) -> p r l", r=d)
                    kb.activation(st.v(o), p.v(i_), AF.Copy, scale=sc)
            dstfn(i, st)
        self.stream(ph, slots, [mk_load(i) for i in range(ntiles)], comp)

    def proj_residual(self, ph, HT, Wv, KC, t0, pss, slots, xr):
        kb = self.kb
        def mk_load(d):
            def ld(slot):
                sv = slot.t[:, 0:KC * 128].rearrange("p (k n) -> p k n", k=KC)
                self.wload(slot.v(sv), Wv[:, :, d * 128:(d + 1) * 128])
            return ld

        def comp(d, slot):
            sv = slot.t[:, 0:KC * 128].rearrange("p (k n) -> p k n", k=KC)
            x = xr[d % 2]
            kb.dma(kb.sp, x[:], self.xcur[d, :, t0:t0 + T])
            for j in range(NTT):
                p = pss[(2 * d + j) % len(pss)]
                for c in range(KC):
                    kb.mm(p[:], slot.v(sv[:, c, :]), HT[:, c, j * TT:(j + 1) * TT], start=(c == 0), stop=(c == KC - 1))
                kb.tt(kb.dve, x[:, j * TT:(j + 1) * TT], x[:, j * TT:(j + 1) * TT], p[:], ALU.add)
            kb.dma(kb.sp, self.xT[d, :, t0:t0 + T], x[:])
        self.stream(ph, slots, [mk_load(d) for d in range(NCH)], comp)

    def attn(self, j_, li, parts="ABC", nheads=16):
        kb = self.kb; nc = self.nc
        DIL = (1, 4, 16)
        slopes = [[2.0 ** (-8.0 * (g * 16 + h + 1) / 48.0) for h in range(16)] for g in range(3)]
        if not hasattr(self, "qkvT"):
            self.qkvT = Tile(nc.dram_tensor("qkvT_s", [144, 128, S], BF16).ap(), "qkvT")
            self.oT = Tile(nc.dram_tensor("oT_s", [NCH, 128, S], BF16).ap(), "oT")
        Wqkv = self.W["attn_w_qkv"][j_].rearrange("(kc p) n -> p kc n", p=128)
        Wo = self.W["attn_w_o"][j_].rearrange("(kc p) n -> p kc n", p=128)
        for s in (range(NPASS) if "A" in parts else []):
            t0 = s * T
            ph = Phase(kb, "atA%d_%d" % (li, s))
            HT = ph.sb([128, NCH, T], BF16, "HT")
            slots = [ph.sb([128, NCH * 128], BF16, "ws") for _ in range(4)]
            stage = [ph.sb([128, T], BF16, "st") for _ in range(3)]
            pss = [ph.ps([128, TT], F32, "ps") for _ in range(8)]
            self.norm_tiles(ph)
            self.norm(ph, self.xcur, t0, "attn_norm%d" % j_, HT, pss[0:2])

            def dst(i, st):
                d = DIL[i // 48]; L = S // d; Lh = T // d
                dv = self.qkvT.t[i].rearrange("p (r l) -> p r l", r=d)[:, :, s * Lh:(s + 1) * Lh]
                sv = st.t[:, :].rearrange("p (r l) -> p r l", r=d)
                kb.dma(kb.sp, self.qkvT.v(dv), st.v(sv))
            self.proj_to_dram(ph, HT, Wqkv, NCH, 144, lambda i: i * 128, dst, pss, slots, stage,
                              scalefn=lambda i: (128.0 ** -0.5 if (i // 16) % 3 == 0 else 1.0),
                              dil=lambda i: DIL[i // 48])
            ph.close()
        ph = Phase(kb, "atB%d" % li)
        ident = self.consts[:, C_ID:C_ID + 128]
        ones1 = ph.sb([128, 128], F32, "ones1"); kb.memset(kb.dve, ones1[:], 1.0)
        qkv = [[ph.sb([128, S], BF16, "qkv") for _ in range(9)] for _ in range(2)]
        OgT = [ph.sb([128, S], F32, "OgT") for _ in range(3)]
        LgT2 = [[ph.sb([128, S], F32, "LgT") for _ in range(3)] for _ in range(2)]
        Vtm = [ph.sb([128, 16, 128], BF16, "Vtm") for _ in range(2)]
        Sb = [ph.sb([128, 256], F32, "Sb") for _ in range(3)]
        Pb = [ph.sb([128, 256], BF16, "Pb") for _ in range(3)]
        PT = [ph.sb([128, 2, 128], BF16, "PT") for _ in range(3)]
        Otm = [ph.sb([128, 128], F32, "Otm") for _ in range(3)]
        Dg = [ph.sb([128, 128], F32, "Dg") for _ in range(3)]
        m3 = ph.sb([128, S], F32, "m3"); ssum = ph.sb([128, S], F32, "ssum")
        Ob = [ph.sb([128, S], BF16, "Ob") for _ in range(2)]
        ps_s = [ph.ps([128, 256], F32, "pS") for _ in range(2)]
        ps_t = [ph.ps([128, 2, 128], BF16, "pT") for _ in range(2)]
        ps_v = [ph.ps([128, 128], BF16, "pV") for _ in range(1)]
        ps_o = [ph.ps([128, 128], F32, "pO") for _ in range(1)]
        ps_ot = [ph.ps([128, 128], F32, "pOT") for _ in range(1)]
        ps_l = [ph.ps([128, 128], F32, "pL") for _ in range(1)]
        sm = [ph.sb([128, 8], F32, "sm") for _ in range(10)]
        units = []
        for h in (range(nheads) if "B" in parts else []):
            for g in range(3):
                for i in range(16):
                    units.append((h, g, i))

        def load_head(h):
            bufs = qkv[h % 2]
            for g in range(3):
                for jj in range(3):
                    kb.dma(kb.sp, bufs[g * 3 + jj][:], self.qkvT[(g * 3 + jj) * 16 + h])

        def prep_group(h, g):
            VT_ = qkv[h % 2][g * 3 + 2]
            vt = Vtm[(h * 3 + g) % 2]
            for i in range(16):
                pv = ps_v[0]
                kb.transpose(pv[:], VT_[:, i * 128:(i + 1) * 128], self.identb[:])
                kb.copy(kb.act, vt[:, i, :], pv[:])

        def geom(u):
            h, g, i = units[u]
            d = DIL[g]; nb = (S // d) // 128
            n = i % nb; nk = 2 if n > 0 else 1
            return h, g, i, d, nb, n, nk

        def st0(u):
            h, g, i, d, nb, n, nk = geom(u)
            if g == 0 and i == 0 and h + 1 < nheads: load_head(h + 1)
            if i == 0: prep_group(h, g)
            QT_, KT_ = qkv[h % 2][g * 3], qkv[h % 2][g * 3 + 1]
            k0 = (i - 1) * 128 if n > 0 else i * 128
            kb.mm(ps_s[u % 2][:, 0:nk * 128], QT_[:, i * 128:(i + 1) * 128], KT_[:, k0:k0 + nk * 128])

        def st1(u):
            h, g, i, d, nb, n, nk = geom(u)
            W_ = nk * 128; b = u % 3; s4 = sm[u % 10]; pS = ps_s[u % 2]
            coef = slopes[g][h] * d
            dv = self.consts[:, C_D1:C_D1 + 256] if nk == 2 else self.consts[:, C_D1 + 128:C_D1 + 256]
            kb.stt(kb.dve, Sb[b][:, 0:W_], dv, coef, pS[:, 0:W_], ALU.mult, ALU.add)
            kb.op(kb.dve, lambda a=s4, sb_=Sb[b], w_=W_: nc.vector.tensor_reduce(a.t[:, 1:2], sb_.t[:, 0:w_], AX.X, ALU.max, negate=True), [Sb[b][:, 0:W_]], [s4[:, 1:2]])

        def st2(u):
            h, g, i, d, nb, n, nk = geom(u)
            W_ = nk * 128; b = u % 3; s4 = sm[u % 10]
            kb.activation(Pb[b][:, 0:W_], Sb[b][:, 0:W_], AF.Exp, bias=s4[:, 1:2], accum=s4[:, 2:3])
            kb.activation(s4[:, 3:4], s4[:, 2:3], AF.Ln)

        def st3(u):
            h, g, i, d, nb, n, nk = geom(u)
            b = u % 3; pT = ps_t[u % 2]; s4 = sm[u % 10]
            for kt in range(nk):
                kb.transpose(pT[:, kt, :], Pb[b][:, kt * 128:(kt + 1) * 128], self.identb[:])
            kb.tt(kb.dve, s4[:, 4:5], s4[:, 3:4], s4[:, 1:2], ALU.subtract)
            kb.op(kb.dve, lambda a=s4: nc.vector.reciprocal(a.t[:, 5:6], a.t[:, 2:3]), [s4[:, 2:3]], [s4[:, 5:6]])

        def st4(u):
            h, g, i, d, nb, n, nk = geom(u)
            b = u % 3; pT = ps_t[u % 2]; s4 = sm[u % 10]
            kb.copy(kb.act, PT[b][:, 0:nk, :], pT[:, 0:nk, :])
            kb.activation(Dg[b][:], ident, AF.Copy, scale=s4[:, 4:5])

        def st5(u):
            h, g, i, d, nb, n, nk = geom(u)
            b = u % 3; pO = ps_o[0]; vt = Vtm[(h * 3 + g) % 2]
            for kt in range(nk):
                kb.mm(pO[:], PT[b][:, kt, :], vt[:, i - (nk - 1) + kt, :], start=(kt == 0), stop=(kt == nk - 1))
            kb.mm(ps_l[0][:], ones1[:], Dg[b][:])

        def st6(u):
            h, g, i, d, nb, n, nk = geom(u)
            b = u % 3; s4 = sm[u % 10]; pO = ps_o[0]
            kb.ts(kb.dve, Otm[b][:], pO[:], s4[:, 5:6], None, ALU.mult)
            r = i // nb
            c0 = r + d * 128 * n
            LgT = LgT2[h % 2]
            kb.copy(kb.dve, LgT[g].v(LgT[g].t[:, c0:c0 + d * 127 + 1:d]), ps_l[0][:])

        def st7(u):
            b = u % 3
            kb.transpose(ps_ot[0][:], Otm[b][:], ident)

        def st8(u):
            h, g, i, d, nb, n, nk = geom(u)
            r = i // nb
            c0 = r + d * 128 * n
            kb.copy(kb.act, OgT[g].v(OgT[g].t[:, c0:c0 + d * 127 + 1:d]), ps_ot[0][:])
            if g == 2 and i == 15: combine(h)

        def combine(h):
            LgT = LgT2[h % 2]
            kb.tt(kb.dve, m3[:], LgT[0][:], LgT[1][:], ALU.max)
            kb.tt(kb.dve, m3[:], m3[:], LgT[2][:], ALU.max)
            for g in range(3):
                kb.tt(kb.dve, LgT[g][:], LgT[g][:], m3[:], ALU.subtract)
                kb.activation(LgT[g][:], LgT[g][:], AF.Exp)
            kb.tt(kb.dve, ssum[:], LgT[0][:], LgT[1][:], ALU.add)
            kb.tt(kb.dve, ssum[:], ssum[:], LgT[2][:], ALU.add)
            kb.op(kb.dve, lambda: nc.vector.reciprocal(ssum.t[:], ssum.t[:]), [ssum[:]], [ssum[:]])
            for g in range(3):
                kb.tt(kb.dve if g != 1 else kb.dve, OgT[g][:], OgT[g][:], LgT[g][:], ALU.mult)
            kb.tt(kb.dve, OgT[0][:], OgT[0][:], OgT[1][:], ALU.add)
            kb.tt(kb.dve, OgT[0][:], OgT[0][:], OgT[2][:], ALU.add)
            ob = Ob[h % 2]
            kb.tt(kb.dve, ob[:], OgT[0][:], ssum[:], ALU.mult)
            kb.dma(kb.sp, self.oT[h], ob[:])

        stages = [st0, st1, st2, st3, st4, st5, st6, st7, st8]
        if units: load_head(0)
        for tick in range(len(units) + len(stages) - 1):
            for si in reversed(range(len(stages))):
                u = tick - si
                if 0 <= u < len(units): stages[si](u)
        ph.close()
        for s in (range(NPASS) if "C" in parts else []):
            t0 = s * T
            ph = Phase(kb, "atC%d_%d" % (li, s))
            HT = ph.sb([128, NCH, T], BF16, "HT")
            slots = [ph.sb([128, NCH * 128], BF16, "ws") for _ in range(3)]
            xr = [ph.sb([128, T], F32, "xr") for _ in range(2)]
            pss = [ph.ps([128, TT], F32, "ps") for _ in range(8)]
            for c in range(NCH):
                kb.dma(kb.sp, HT[:, c, :], self.oT[c, :, t0:t0 + T])
            self.proj_residual(ph, HT, Wo, NCH, t0, pss, slots, xr)
            ph.close()
        self.xcur = self.xT

    def conv(self):
        kb = self.kb
        Win = self.W["conv_w_in"][0].rearrange("(kc p) n -> p kc n", p=128)
        Wout = self.W["conv_w_out"][0].rearrange("(kc p) n -> p kc n", p=128)
        if not hasattr(self, "cv_halo"):
            self.cv_halo = self.pp.sb([128, NCH, 2], F32, "chalo")
        for s in range(NPASS):
            t0 = s * T
            ph = Phase(kb, "cv%d" % s)
            HT = ph.sb([128, NCH, T], BF16, "HT")
            ZT = ph.sb([128, NCH, T], BF16, "ZT")
            slots = [ph.sb([128, NCH * 3 * 128], BF16, "ws") for _ in range(3)]
            G = [ph.sb([128, T + 2], F32, "G") for _ in range(2)]
            Cv = [ph.sb([128, T], F32, "Cv") for _ in range(2)]
            Cs = [ph.sb([128, T], F32, "Cs") for _ in range(2)]
            xr = [ph.sb([128, T], F32, "xr") for _ in range(2)]
            pss = [ph.ps([128, TT], F32, "ps") for _ in range(8)]
            self.norm_tiles(ph)
            self.norm(ph, self.xcur, t0, "conv_norm", HT, pss[0:2])

            def mk_load(f):
                def ld(slot):
                    sv = slot.t[:, :].rearrange("p (k n) -> p k n", k=NCH)
                    for q in range(3):
                        self.wload(slot.v(sv[:, :, q * 128:(q + 1) * 128]), Win[:, :, q * D + f * 128:q * D + (f + 1) * 128])
                return ld

            def comp(f, slot):
                sv = slot.t[:, :].rearrange("p (k n) -> p k n", k=NCH)
                g = G[f % 2]; cv = Cv[f % 2]; cs = Cs[f % 2]
                if s == 0:
                    kb.memset(kb.dve, g[:, 0:2], 0.0)
                else:
                    kb.copy(kb.dve, g[:, 0:2], self.cv_halo[:, f, :])
                for j in range(NTT):
                    sl = slice(j * TT, (j + 1) * TT)
                    pc_, pu_, pb_ = pss[(3 * j) % 8], pss[(3 * j + 1) % 8], pss[(3 * j + 2) % 8]
                    for q, p in ((1, pc_), (2, pu_), (0, pb_)):
                        for c in range(NCH):
                            kb.mm(p[:], slot.v(sv[:, c, q * 128:(q + 1) * 128]), HT[:, c, sl], start=(c == 0), stop=(c == NCH - 1))
                    kb.copy(kb.act, cs[:, sl], pc_[:])
                    kb.tt(kb.dve, g[:, 2 + j * TT:2 + (j + 1) * TT], cs[:, sl], pu_[:], ALU.mult)
                    kb.copy(kb.act, cs[:, sl], pb_[:])
                kb.copy(kb.dve, self.cv_halo[:, f, :], g[:, T:T + 2])
                w = lambda j: self.vcol("conv_w%d" % j, f)
                kb.ts(kb.dve, cv[:], g[:, 2:T + 2], w(2), None, ALU.mult)
                kb.stt(kb.dve, cv[:], g[:, 1:T + 1], w(1), cv[:], ALU.mult, ALU.add)
                kb.stt(kb.dve, cv[:], g[:, 0:T], w(0), cv[:], ALU.mult, ALU.add)
                kb.tt(kb.dve, ZT[:, f, :], cv[:], cs[:], ALU.mult)
            self.stream(ph, slots, [mk_load(f) for f in range(NCH)], comp)
            self.proj_residual(ph, ZT, Wout, NCH, t0, pss, slots, xr)
            ph.close()
        self.xcur = self.xT


    def rwkv(self, nchunks=16, parts="ABC", dbg=9):
        kb = self.kb; nc = self.nc
        scr = lambda name, dt=F32: Tile(nc.dram_tensor(name, [NCH, 128, S], dt).ap(), name)
        R_, K_, V_, LW_, A_, G_ = [scr("rw_%s" % n) for n in "rkvwag"]
        YG = scr("rw_yg", BF16)
        hlast = self.pp.sb([128, NCH, 1], BF16, "hlast")
        negw0 = self.pp.sb([128, NCH], F32, "negw0")
        kb.ts(kb.dve, negw0[:], self.vecs[:, VCOL["w0"]:VCOL["w0"] + NCH], -1.0, None, ALU.mult)
        rearr = lambda w: w.rearrange("(kc p) n -> p kc n", p=128)
        for s in (range(NPASS) if "A" in parts else []):
            t0 = s * T
            ph = Phase(kb, "rwA%d" % s)
            HT = ph.sb([128, NCH, T], BF16, "HT"); XX = ph.sb([128, NCH, T], BF16, "XX"); Xn = ph.sb([128, NCH, T], BF16, "Xn")
            slots = [ph.sb([128, NCH * 256], BF16, "ws") for _ in range(3)]
            stage = [ph.sb([128, T], F32, "st") for _ in range(2)]
            T1 = ph.sb([128, 2, T], BF16, "T1")
            W2 = ph.sb([128, 2, D], BF16, "W2")
            pss = [ph.ps([128, TT], F32, "ps") for _ in range(8)]
            self.norm_tiles(ph)
            self.norm(ph, self.xcur, t0, "rwkv_norm", HT, pss[0:2])
            for c in range(NCH):
                if s == 0:
                    kb.ts(kb.dve, XX[:, c, 0:1], HT[:, c, 0:1], -1.0, None, ALU.mult)
                else:
                    kb.tt(kb.dve, XX[:, c, 0:1], hlast[:, c, :], HT[:, c, 0:1], ALU.subtract)
                kb.tt(kb.dve, XX[:, c, 1:T], HT[:, c, 0:T - 1], HT[:, c, 1:T], ALU.subtract)
            for c in range(NCH):
                kb.copy(kb.dve, hlast[:, c, :], HT[:, c, T - 1:T])

            def mix(n):
                for c in range(NCH):
                    kb.stt(kb.dve, Xn[:, c, :], XX[:, c, :], self.vcol("mu%d" % n, c), HT[:, c, :], ALU.mult, ALU.add)
            for n, dst in enumerate([R_, K_, V_]):
                mix(n)
                self.proj_to_dram(ph, Xn, rearr(self.W["rwkv_w_rkv"][0][n]), NCH, 16, lambda i: i * 128,
                                  lambda i, st, dst=dst: kb.dma(kb.sp, dst[i, :, t0:t0 + T], st[:]), pss, slots, stage)

            def lora(n, w1name, w2name, rnk, func1, epi, dst):
                mix(n)
                nm = (rnk + 127) // 128; rp = min(rnk, 128)
                sl = slots[0]
                sv = sl.t[:, 0:NCH * rnk].rearrange("p (k n) -> p k n", k=NCH)
                self.wload(sl.v(sv), rearr(self.W[w1name][0]))
                w2v = self.W[w2name][0].rearrange("(m p) n -> p m n", p=rp)
                self.wload(W2[0:rp, 0:nm, :], w2v)
                k = 0
                for m in range(nm):
                    for j in range(NTT):
                        p = pss[k % 8]; k += 1
                        for c in range(NCH):
                            kb.mm(p[0:rp, :], sl.v(sv[:, c, m * 128:m * 128 + rp]), Xn[:, c, j * TT:(j + 1) * TT], start=(c == 0), stop=(c == NCH - 1))
                        kb.activation(T1[0:rp, m, j * TT:(j + 1) * TT], p[0:rp, :], func1)
                for i in range(NCH):
                    st = stage[i % 2]
                    for j in range(NTT):
                        p = pss[k % 8]; k += 1
                        for m in range(nm):
                            kb.mm(p[:], W2[0:rp, m, i * 128:(i + 1) * 128], T1[0:rp, m, j * TT:(j + 1) * TT], start=(m == 0), stop=(m == nm - 1))
                        epi(i, st[:, j * TT:(j + 1) * TT], p)
                    kb.dma(kb.sp, dst[i, :, t0:t0 + T], st[:])

            def epi_w(i, o, p):
                kb.activation(o, p[:], AF.Exp, bias=negw0[:, i:i + 1], scale=-1.0)
                kb.ts(kb.dve, o, o, 1.0, None, ALU.add)
                kb.op(kb.dve, lambda: nc.vector.reciprocal(o.ap, o.ap), [o], [o])
                kb.ts(kb.dve, o, o, -0.6065306597126334, None, ALU.mult)
            lora(3, "rwkv_w_w1", "rwkv_w_w2", 96, AF.Tanh, epi_w, LW_)
            lora(4, "rwkv_w_a1", "rwkv_w_a2", 96, AF.Copy, lambda i, o, p: kb.activation(o, p[:], AF.Sigmoid, bias=self.vcol("a0", i)), A_)
            lora(5, "rwkv_w_g1", "rwkv_w_g2", 256, AF.Sigmoid, lambda i, o, p: kb.copy(kb.act, o, p[:]), G_)
            ph.close()
        ph = Phase(kb, "rwB")
        cst = lambda c0: self.consts[:, c0:c0 + 128]
        ident, SU, IU, SL, BD, BD1 = cst(C_ID), cst(C_SU), cst(C_IU), cst(C_SL), cst(C_BD), cst(C_BD1)
        GB = 4; GW = GB * 128; NQ = S // GW
        NSETS = RW_NSETS
        BN = ("r", "k", "v", "lw", "a", "g", "B1", "B2", "B3", "B4", "B5")
        sets = []
        for si in range(NSETS):
            d_ = {n: ph.sb([128, GW], F32, n) for n in BN}
            d_["gC"] = ph.sb([128, GB], F32, "gC"); d_["yg"] = ph.sb([128, GW], BF16, "yg")
            sets.append(d_)
        TMs = [[ph.sb([128, GB, 128], F32, n) for n in ("Vtm", "VP0", "VP1", "Btm", "Ktm")] for _ in range(2)]
        UM = [[[ph.sb([128, 128], F32, "um") for _ in range(4)] for _ in range(2 * GB)] for _ in range(2)]
        QQ = [[ph.sb([128, 128], F32, "qq") for _ in range(6)] for _ in range(2 * GB)]
        STP = ph.sb([128, 128], F32, "STP"); RH = ph.sb([128, 128], F32, "RH"); SAB = ph.sb([128, 128], F32, "SAB")
        SAP = [ph.sb([128, 128], F32, "SAP") for _ in range(2)]
        t1 = ph.sb([128, 128], F32, "t1")
        pg = [ph.ps([128, 128], F32, "pg") for _ in range(2)]
        pd = [ph.ps([128, 128], F32, "pd") for _ in range(3)]
        ptr = ph.ps([128, 512], F32, "ptr")
        pq = [ph.ps([128, 512], F32, "pq") for _ in range(2)]
        for q in range(2):
            kb.memset(kb.dve, TMs[q][1][:], 0.0); kb.memset(kb.dve, TMs[q][2][:], 0.0)
        kb.memset(kb.dve, SAP[0][:], 0.0); kb.memset(kb.dve, SAP[1][:], 0.0)
        recip = lambda o: kb.op(kb.dve, lambda: nc.vector.reciprocal(o.ap, o.ap), [o], [o])
        groups = [(c, q) for c in (range(nchunks) if "B" in parts else []) for q in range(NQ)]
        NGR = len(groups)
        SRC = (("r", R_), ("k", K_), ("v", V_), ("lw", LW_), ("a", A_), ("g", G_))

        def load(gi):
            c, q = groups[gi]; S_ = sets[gi % NSETS]
            for n, src in SRC:
                kb.dma(kb.sp, S_[n][:], src[c, :, q * GW:(q + 1) * GW])

        def prep_ops(gi):
            c, q = groups[gi]; S_ = sets[gi % NSETS]
            r, k, v, lw, a, g, B1, B2, B3, B4, B5 = [S_[n] for n in BN]
            v3 = lambda t_: t_.t[:, :].rearrange("p (n q) -> p n q", q=128)
            ops = []
            A = ops.append
            A(lambda: kb.ts(kb.dve, B1[:], k[:], self.vcol("k_k", c), None, ALU.mult))
            A(lambda: kb.activation(B2[:], B1[:], AF.Square))
            A(lambda: kb.mm(ptr[:], BD1, B2[:]))
            A(lambda: kb.activation(B3[:], ptr[:], AF.Sqrt))
            A(lambda: kb.ts(kb.dve, B3[:], B3[:], 1e-12, None, ALU.max))
            A(lambda: recip(B3[:]))
            A(lambda: kb.tt(kb.dve, B1[:], B1[:], B3[:], ALU.mult))
            A(lambda: kb.ts(kb.dve, B2[:], a[:], -1.0, self.vcol("k_a", c), ALU.add, ALU.mult))
            A(lambda: kb.stt(kb.dve, k[:], B2[:], 1.0, k[:], ALU.add, ALU.mult))
            A(lambda: kb.tt(kb.dve, a[:], B1[:], a[:], ALU.mult))
            src = lw; pp2 = [B2, B3]; sh = 1; qi = 0
            while sh < 128:
                dst = pp2[qi % 2]; qi += 1
                A(lambda dst=dst, src=src, sh=sh: kb.tt(kb.dve, dst.v(v3(dst)[:, :, sh:128]), src.v(v3(src)[:, :, sh:128]), src.v(v3(src)[:, :, 0:128 - sh]), ALU.add))
                A(lambda dst=dst, src=src, sh=sh: kb.copy(kb.act, dst.v(v3(dst)[:, :, 0:sh]), src.v(v3(src)[:, :, 0:sh])))
                src = dst; sh *= 2
            cum = src
            A(lambda: kb.activation(B4[:], cum[:], AF.Exp))
            A(lambda: kb.activation(B5[:], cum[:], AF.Exp, scale=-1.0))
            A(lambda: kb.tt(kb.dve, lw[:], cum[:], lw[:], ALU.subtract))
            A(lambda: kb.activation(lw[:], lw[:], AF.Exp))
            A(lambda: kb.copy(kb.dve, S_["gC"][:], B4.v(v3(B4)[:, :, 127])))
            A(lambda: kb.stt(kb.dve, B1[:], B1[:], -1.0, lw[:], ALU.mult, ALU.mult))
            A(lambda: kb.tt(kb.dve, a[:], a[:], B5[:], ALU.mult))
            A(lambda: kb.tt(kb.dve, B5[:], k[:], B5[:], ALU.mult))
            A(lambda: kb.tt(kb.dve, B4[:], r[:], B4[:], ALU.mult))
            return ops

        def front_a(gi, sl_):
            S_ = sets[gi % NSETS]
            At, Bt, Kt, Rt, v = S_["B1"], S_["a"], S_["B5"], S_["B4"], S_["v"]
            Vtm, VP0, VP1, Btm, Ktm = TMs[sl_]; UMs = UM[sl_]
            for bi in range(GB):
                bs = slice(bi * 128, (bi + 1) * 128)
                kb.mm(ptr[:, 0:128], v[:, bs], ident)
                kb.copy(kb.act, Vtm[:, bi, :], ptr[:, 0:128])
                kb.copy(kb.dve, VP0[:, bi, 0:64], Vtm[:, bi, 0:64])
                kb.copy(kb.dve, VP1[:, bi, 64:128], Vtm[:, bi, 64:128])
                kb.mm(ptr[:, 0:128], Bt[:, bs], ident)
                kb.copy(kb.dve, Btm[:, bi, :], ptr[:, 0:128])
                kb.mm(ptr[:, 0:128], Kt[:, bs], ident)
                kb.copy(kb.act, Ktm[:, bi, :], ptr[:, 0:128])
            gi_ = 0
            for u in range(2 * GB):
                bi, hh = u // 2, u % 2
                bs = slice(bi * 128, (bi + 1) * 128); hp = slice(64 * hh, 64 * hh + 64)
                Uak, Urb, Urk, XT = UMs[u]; Qa, Qb, La, Lb, Pa, Pb = QQ[u]
                for (lt, rt, msk, dst) in ((Bt, At, SU, Qa), (At, Bt, SL, La), (Kt, At, SU, Uak), (Bt, Rt, IU, Urb), (Kt, Rt, IU, Urk)):
                    p = pg[gi_ % 2]; gi_ += 1
                    kb.mm(p[:], lt[hp, bs], rt[hp, bs])
                    kb.tt(kb.dve, dst[:], p[:], msk, ALU.mult)
                kb.tt(kb.dve, Pa[:], Qa[:], ident, ALU.add)

        dstate = {"di": 0}

        def dbl_level(sl_, lev, only=None):
            for u in (range(2 * GB) if only is None else [only]):
                src_i = lev % 2
                Qc, Lc, Pc = QQ[u][0 + src_i], QQ[u][2 + src_i], QQ[u][4 + src_i]
                Qn, Ln, Pn = QQ[u][1 - src_i], QQ[u][3 - src_i], QQ[u][5 - src_i]
                last = (lev == 5)
                p = pd[dstate["di"] % 3]; dstate["di"] += 1
                kb.mm(p[:], Qc[:], Lc[:])
                kb.copy(kb.act, Ln[:], p[:])
                if not last:
                    p = pd[dstate["di"] % 3]; dstate["di"] += 1
                    kb.mm(p[:], Lc[:], Qc[:])
                    kb.copy(kb.act if u % 2 else kb.dve, Qn[:], p[:])
                p = pd[dstate["di"] % 3]; dstate["di"] += 1
                kb.mm(p[:], Ln[:], Pc[:])
                outP = UM[sl_][u][3] if last else Pn
                kb.tt(kb.dve, outP[:], p[:], Pc[:], ALU.add)

        def seq_block(gi, sl_, bi, part=None):
            c, q = groups[gi]; S_ = sets[gi % NSETS]
            At, Rt, Y, gC = S_["B1"], S_["B4"], S_["lw"], S_["gC"]
            Vtm, VP0, VP1, Btm, Ktm = TMs[sl_]; UMs = UM[sl_]
            bs = slice(bi * 128, (bi + 1) * 128)
            p1 = pq[0]; p2 = pq[1]
            if part in (None, 0):
                if q == 0 and bi == 0:
                    kb.memset(kb.dve, STP[:], 0.0)
                for hh in range(2):
                    hp = slice(64 * hh, 64 * hh + 64); hc = slice(64 * hh, 64 * hh + 64)
                    kb.mm(p1[:, hc], At[hp, bs], STP[hp, hc], start=True, stop=False)
                    kb.mm(p1[:, hc], UMs[2 * bi + hh][0][:], Vtm[:, bi, hc], start=False, stop=True)
                kb.copy(kb.act, RH[:], p1[:, 0:128])
            if part in (None, 1):
                for hh in range(2):
                    hc = slice(64 * hh, 64 * hh + 64)
                    kb.mm(p2[:, hc], UMs[2 * bi + hh][3][:], RH[:, hc])
                kb.copy(kb.act, SAB[:], p2[:, 0:128])
                kb.copy(kb.dve, SAP[0][:, 0:64], SAB[:, 0:64])
                kb.copy(kb.act, SAP[1][:, 64:128], SAB[:, 64:128])
            if part not in (None, 2): return
            py = p1
            kb.mm(py[:, 128:256], STP[:], Rt[:, bs], start=True, stop=False)
            for hh in range(2):
                kb.mm(py[:, 128:256], SAP[hh][:], UMs[2 * bi + hh][1][:], start=False, stop=False)
            kb.mm(py[:, 128:256], VP0[:, bi, :], UMs[2 * bi][2][:], start=False, stop=False)
            kb.mm(py[:, 128:256], VP1[:, bi, :], UMs[2 * bi + 1][2][:], start=False, stop=True)
            kb.copy(kb.act, Y[:, bs], py[:, 128:256])
            pst = p2
            kb.mm(pst[:, 128:256], Btm[:, bi, :], SAB[:], start=True, stop=False)
            kb.mm(pst[:, 128:256], Ktm[:, bi, :], Vtm[:, bi, :], start=False, stop=True)
            kb.stt(kb.dve, t1[:], pst[:, 128:256], gC[:, bi:bi + 1], BD1, ALU.mult, ALU.mult)
            kb.stt(kb.dve, STP[:], STP[:], gC[:, bi:bi + 1], t1[:], ALU.mult, ALU.add)

        def post_ops(gi):
            c, q = groups[gi]; S_ = sets[gi % NSETS]
            r, k, v, Y, g, B1, B2, B3, yg = S_["r"], S_["k"], S_["v"], S_["lw"], S_["g"], S_["B1"], S_["B2"], S_["B3"], S_["yg"]
            ops = []
            A = ops.append
            A(lambda: kb.mm(ptr[:], BD, Y[:]))
            A(lambda: kb.tt(kb.dve, B2[:], Y[:], ptr[:], ALU.subtract))
            A(lambda: kb.activation(B3[:], B2[:], AF.Square))
            A(lambda: kb.mm(ptr[:], BD, B3[:]))
            A(lambda: kb.activation(B3[:], ptr[:], AF.Sqrt, bias=self.consts[:, C_GNE:C_GNE + 1]))
            A(lambda: recip(B3[:]))
            A(lambda: kb.tt(kb.dve, B2[:], B2[:], B3[:], ALU.mult))
            A(lambda: kb.ts(kb.dve, B2[:], B2[:], self.vcol("ln_g", c), self.vcol("ln_b", c), ALU.mult, ALU.add))
            A(lambda: kb.tt(kb.dve, B3[:], r[:], k[:], ALU.mult))
            A(lambda: kb.ts(kb.dve, B3[:], B3[:], self.vcol("r_k", c), None, ALU.mult))
            A(lambda: kb.mm(ptr[:], BD1, B3[:]))
            A(lambda: kb.tt(kb.dve, B1[:], ptr[:], v[:], ALU.mult))
            A(lambda: kb.tt(kb.dve, B2[:], B2[:], B1[:], ALU.add))
            A(lambda: kb.tt(kb.dve, yg[:], B2[:], g[:], ALU.mult))
            A(lambda: kb.dma(kb.sp, YG[c, :, q * GW:(q + 1) * GW], yg[:]))
            return ops

        if NGR:
            for gi in range(min(NSETS - 1, NGR)): load(gi)
            for gi in range(min(2, NGR)):
                for o in prep_ops(gi): o()
            front_a(0, 0)
            for lev in range(6): dbl_level(0, lev)
        for i in range(NGR):
            sl_ = i % 2
            items = []
            if i + 1 < NGR:
                front_a(i + 1, 1 - sl_)
                Ls = [(lev, u) for lev in range(6) for u in range(2 * GB)]
                Ss = [(bi, part) for bi in range(GB) for part in range(3)]
                li_ = 0
                for (bi, part) in Ss:
                    for _ in range(len(Ls) // len(Ss)):
                        lev, u = Ls[li_]; li_ += 1
                        items.append(lambda lev=lev, u=u: dbl_level(1 - sl_, lev, u))
                    items.append(lambda bi=bi, part=part: seq_block(i, sl_, bi, part))
                while li_ < len(Ls):
                    lev, u = Ls[li_]; li_ += 1
                    items.append(lambda lev=lev, u=u: dbl_level(1 - sl_, lev, u))
            else:
                for bi in range(GB): items.append(lambda bi=bi: seq_block(i, sl_, bi))
            extras = []
            if i >= 1: extras += post_ops(i - 1)
            if i + 2 < NGR: extras += prep_ops(i + 2)
            per = (len(extras) + len(items) - 1) // len(items) if extras else 0
            ei = 0
            for it_ in items:
                it_()
                for _ in range(per):
                    if ei < len(extras): extras[ei](); ei += 1
            while ei < len(extras): extras[ei](); ei += 1
            if i + NSETS - 1 < NGR: load(i + NSETS - 1)
        if NGR:
            for o in post_ops(NGR - 1): o()
        ph.close()
        Wo = rearr(self.W["rwkv_w_o"][0])
        for s in (range(NPASS) if "C" in parts else []):
            t0 = s * T
            ph = Phase(kb, "rwC%d" % s)
            HT = ph.sb([128, NCH, T], BF16, "HT")
            slots = [ph.sb([128, NCH * 128], BF16, "ws") for _ in range(3)]
            xr = [ph.sb([128, T], F32, "xr") for _ in range(2)]
            pss = [ph.ps([128, TT], F32, "ps") for _ in range(8)]
            for c in range(NCH):
                kb.dma(kb.sp, HT[:, c, :], YG[c, :, t0:t0 + T])
            self.proj_residual(ph, HT, Wo, NCH, t0, pss, slots, xr)
            ph.close()
        self.xcur = self.xT

    def final(self):
        kb = self.kb
        for s in range(NPASS):
            ph = Phase(kb, "fin%d" % s)
            pss = [ph.ps([128, TT], F32, "ps") for _ in range(2)]
            self.norm_tiles(ph)
            self.norm(ph, self.xcur, s * T, "final_norm", None, pss, out_dram=self.out)
            ph.close()

    def copy_x(self):
        kb = self.kb
        ph = Phase(kb, "cp")
        xt = [ph.sb([128, S], F32, "x") for _ in range(2)]
        for c in range(NCH):
            kb.dma(kb.sp, xt[c % 2][:], self.xcur[c, :, :])
            kb.dma(kb.sp, self.out[c, :, :], xt[c % 2][:])
        ph.close()

    def build(self):
        for p in self.phases:
            if p[0] == "ffn": self.ffn(p[1])
            elif p[0] == "ple": self.ple(p[1])
            elif p[0] == "attn": self.attn(*p[1:])
            elif p[0] == "rwkv": self.rwkv(*p[1:])
            elif p[0] == "conv": self.conv()
            elif p[0] == "final": self.final()
            elif p[0] == "copy": self.copy_x()
        self.kb.barrier()
        self.pp.es.close()


WSHAPES = {
    "attn_w_qkv": [2, D, 9 * D], "attn_w_o": [2, D, D],
    "rwkv_w_rkv": [1, 3, D, D], "rwkv_w_w1": [1, D, 96], "rwkv_w_w2": [1, 96, D],
    "rwkv_w_a1": [1, D, 96], "rwkv_w_a2": [1, 96, D], "rwkv_w_g1": [1, D, 256], "rwkv_w_g2": [1, 256, D],
    "rwkv_w_o": [1, D, D],
    "conv_w_in": [1, D, 3 * D], "conv_w_out": [1, D, D],
    "ffn_w_gu": [DEPTH, D, 2 * FF], "ffn_w_down": [DEPTH, FF, D],
    "ple_w_proj": [DEPTH, PLE, D], "ple_w_gate": [DEPTH, D, D],
}

def wnames(phases):
    pre = {"ffn": "ffn_", "ple": "ple_", "attn": "attn_", "rwkv": "rwkv_", "conv": "conv_"}
    used = set(pre[p[0]] for p in phases if p[0] in pre)
    return [n for n in WSHAPES if any(n.startswith(u) for u in used)]


FULL_PHASES = []
for _i in range(DEPTH):
    _k, _j = _i % 3, _i // 3
    FULL_PHASES.append(("attn", _j, _i) if _k == 0 else (("rwkv",) if _k == 1 else ("conv",)))
    FULL_PHASES.append(("ffn", _i)); FULL_PHASES.append(("ple", _i))
FULL_PHASES.append(("final",))


def build_nc(phases):
    nc = bass.Bass("TRN2", target_bir_lowering=False)
    es = contextlib.ExitStack()
    Prog(nc, es, phases)
    es.close()
    return nc


def make_vecs(inp):
    v = np.zeros((128, NV), np.float32)

    def put(name, arr, n=NCH):
        v[:, VCOL[name]:VCOL[name] + n] = pc(arr, n)
    for j in range(2): put("attn_norm%d" % j, inp["attn_norm"][j])
    put("rwkv_norm", inp["rwkv_norm"][0]); put("conv_norm", inp["conv_norm"][0])
    for i in range(DEPTH):
        put("ffn_norm%d" % i, inp["ffn_norm"][i]); put("ple_norm%d" % i, inp["ple_norm"][i])
        for j in range(3): put("fcw%d_%d" % (i, j), inp["ffn_conv_w"][i, j], NFC)
        put("fcb%d" % i, inp["ffn_conv_b"][i], NFC)
    put("final_norm", inp["final_norm"])
    for j in range(6): put("mu%d" % j, inp["rwkv_mu"][0, j])
    for nm in ("w0", "a0", "k_k", "k_a", "ln_g", "ln_b"): put(nm, inp["rwkv_" + nm][0])
    put("r_k", np.asarray(inp["rwkv_r_k"][0]).reshape(-1))
    for j in range(3): put("conv_w%d" % j, inp["conv_w"][0, j])
    return v


def run(inputs, phases, ncores=8):
    inp = {k: np.asarray(v) for k, v in inputs.items()}
    nc = build_nc(phases)
    vecs = make_vecs(inp); consts = build_consts()
    in_maps = []
    for c in range(ncores):
        b = c % NB
        m = {"xT": np.ascontiguousarray(inp["x"][b].T.reshape(NCH, 128, S)),
             "pT": np.ascontiguousarray(inp["p"][:, b].transpose(0, 2, 1).reshape(DEPTH, 2, 128, S)),
             "vecs": vecs, "consts": consts}
        for name in wnames(phases):
            m[name] = np.ascontiguousarray(inp[name], dtype=np.float32)
        in_maps.append(m)
    res = run_bass_kernel_spmd(nc, in_maps, core_ids=list(range(ncores)))
    outs = [res.results[b]["outT"].reshape(D, S).T for b in range(min(NB, ncores))]
    return np.stack(outs, 0)


def kernel(**inputs):
    out = run(inputs, FULL_PHASES)
    return np.ascontiguousarray(out.astype(np.float32))
```

```python
import contextlib
import numpy as np
import concourse.bass as bass
import concourse.mybir as mybir
from concourse.bass_utils import run_bass_kernel_spmd

F32 = mybir.dt.float32
BF16 = mybir.dt.bfloat16
AF = mybir.ActivationFunctionType
ALU = mybir.AluOpType
AX = mybir.AxisListType

D = 2048; S = 2048; NB = 4; DEPTH = 4; FF = 5632; PLE = 256
NCH = 16; NFC = 44; T = 1024; NTT = 2; TT = 512
EPS = 1e-6
RW_NSETS = 5
NPASS = S // T


class Tile:
    def __init__(self, t, name):
        self.t = t; self.name = name; self.w = {}; self.r = {}; self.dsem = None

    def __getitem__(self, idx):
        return View(self, self.t[idx])

    def v(self, ap):
        return View(self, ap)


class View:
    def __init__(self, tile, ap):
        self.tile = tile; self.ap = ap


class Eng:
    def __init__(self, kb, h, name):
        self.h = h; self.name = name; self.sid = kb.new_sem("e_" + name); self.waited = {}


class KB:
    def __init__(self, nc, es):
        self.nc = nc; self.es = es
        self.sems = []; self.cnt = []
        self.pe = Eng(self, nc.tensor, "pe"); self.act = Eng(self, nc.scalar, "act")
        self.dve = Eng(self, nc.vector, "dve"); self.pool = Eng(self, nc.gpsimd, "pool")
        self.sp = Eng(self, nc.sync, "sp")
        self.engs = [self.pe, self.act, self.dve, self.pool, self.sp]
        self.free_dsems = {"sp": [self.new_sem("d%d" % i) for i in range(64)], "pool": [self.new_sem("g%d" % i) for i in range(32)]}
        self.n_inst = 0

    def new_sem(self, name):
        h = self.es.enter_context(self.nc.semaphore(name))
        self.sems.append(h); self.cnt.append(0)
        return len(self.sems) - 1

    def op(self, E, fn, reads, writes, inc=True, dma_tile=None, nowait_writes=False):
        waits = {}
        for v in reads:
            for k, val in v.tile.w.items():
                if waits.get(k, 0) < val: waits[k] = val
        for v in ([] if nowait_writes else writes):
            for dd in (v.tile.w, v.tile.r):
                for k, val in dd.items():
                    if waits.get(k, 0) < val: waits[k] = val
        for k, val in waits.items():
            if E.waited.get(k, 0) < val:
                E.h.wait_ge(self.sems[k], val); E.waited[k] = val
        ins = fn()
        self.n_inst += 1
        if dma_tile is not None:
            if dma_tile.dsem is None:
                dma_tile.dsem = {}
            if E.name not in dma_tile.dsem:
                dma_tile.dsem[E.name] = self.free_dsems[E.name].pop()
            k = dma_tile.dsem[E.name]
            self.cnt[k] += 16
            ins.then_inc(self.sems[k], 16)
            tok = (k, self.cnt[k])
        elif inc:
            k = E.sid
            self.cnt[k] += 1
            ins.then_inc(self.sems[k], 1)
            tok = (k, self.cnt[k])
        else:
            tok = (E.sid, self.cnt[E.sid] + 1)
        for v in reads:
            if v.tile.r.get(tok[0], 0) < tok[1]: v.tile.r[tok[0]] = tok[1]
        for v in writes:
            v.tile.w = {tok[0]: tok[1]}; v.tile.r = {}
        return ins

    def release(self, tiles):
        for t in tiles:
            if t.dsem is not None:
                for q, k in t.dsem.items(): self.free_dsems[q].append(k)
                t.dsem = None

    def barrier(self):
        for E in self.engs:
            for k in range(len(self.sems)):
                val = self.cnt[k]
                if val > 0 and E.waited.get(k, 0) < val:
                    E.h.wait_ge(self.sems[k], val); E.waited[k] = val

    def mm(self, out, lhsT, rhs, start=True, stop=True):
        wr = [out] if (start or stop) else []
        return self.op(self.pe, lambda: self.nc.tensor.matmul(out.ap, lhsT.ap, rhs.ap, start=start, stop=stop),
                       [lhsT, rhs], wr, inc=True, nowait_writes=not start)

    def transpose(self, out, in_, ident):
        return self.op(self.pe, lambda: self.nc.tensor.transpose(out.ap, in_.ap, ident.ap), [in_, ident], [out])

    def activation(self, out, in_, func, bias=None, scale=1.0, accum=None, E=None):
        E = E or self.act
        reads = [in_]; kw = {}
        if isinstance(bias, View): reads.append(bias); kw["bias"] = bias.ap
        elif bias is not None: kw["bias"] = bias
        if isinstance(scale, View): reads.append(scale); kw["scale"] = scale.ap
        else: kw["scale"] = scale
        writes = [out]
        if accum is not None: writes.append(accum); kw["accum_out"] = accum.ap
        return self.op(E, lambda: E.h.activation(out=out.ap, in_=in_.ap, func=func, **kw), reads, writes)

    def tt(self, E, out, in0, in1, op):
        return self.op(E, lambda: E.h.tensor_tensor(out.ap, in0.ap, in1.ap, op), [in0, in1], [out])

    def ts(self, E, out, in0, s1, s2, op0, op1=None):
        reads = [in0]
        a1 = s1.ap if isinstance(s1, View) else s1
        a2 = s2.ap if isinstance(s2, View) else s2
        if isinstance(s1, View): reads.append(s1)
        if isinstance(s2, View): reads.append(s2)
        if op1 is None:
            return self.op(E, lambda: E.h.tensor_scalar(out.ap, in0.ap, a1, None, op0), reads, [out])
        return self.op(E, lambda: E.h.tensor_scalar(out.ap, in0.ap, a1, a2, op0, op1), reads, [out])

    def stt(self, E, out, in0, sc, in1, op0, op1):
        reads = [in0, in1]
        a = sc.ap if isinstance(sc, View) else sc
        if isinstance(sc, View): reads.append(sc)
        return self.op(E, lambda: E.h.scalar_tensor_tensor(out.ap, in0.ap, a, in1.ap, op0, op1), reads, [out])

    def copy(self, E, out, in_):
        if E is self.act:
            return self.activation(out, in_, AF.Copy)
        return self.op(E, lambda: E.h.tensor_copy(out.ap, in_.ap), [in_], [out])

    def reduce(self, E, out, in_, op, axis=AX.X):
        return self.op(E, lambda: E.h.tensor_reduce(out.ap, in_.ap, axis, op), [in_], [out])

    def memset(self, E, out, val):
        return self.op(E, lambda: E.h.memset(out.ap, val), [], [out])

    def dma(self, Q, out, in_, sb=None):
        if sb is None:
            sb = out.tile if getattr(out.tile, "is_sb", False) else in_.tile
        return self.op(Q, lambda: Q.h.dma_start(out=out.ap, in_=in_.ap), [in_], [out], dma_tile=sb)


class Phase:
    def __init__(self, kb, name):
        self.kb = kb; self.name = name; self.es = contextlib.ExitStack(); self.tiles = []; self.n = 0

    def sb(self, shape, dt, name=None):
        self.n += 1
        nm = "%s_%s%d" % (self.name, name or "t", self.n)
        t = Tile(self.es.enter_context(self.kb.nc.sbuf_tensor(nm, list(shape), dt)), nm)
        t.is_sb = True
        self.tiles.append(t)
        return t

    def ps(self, shape, dt, name=None):
        self.n += 1
        nm = "%s_%s%d" % (self.name, name or "p", self.n)
        t = Tile(self.es.enter_context(self.kb.nc.psum_tensor(nm, list(shape), dt)), nm)
        self.tiles.append(t)
        return t

    def close(self):
        self.kb.barrier()
        self.kb.release(self.tiles)
        self.es.close()


def vec_layout():
    cols = {}; n = 0

    def add(name, w):
        nonlocal n
        cols[name] = n; n += w
    for j in range(2): add("attn_norm%d" % j, NCH)
    add("rwkv_norm", NCH); add("conv_norm", NCH)
    for i in range(DEPTH): add("ffn_norm%d" % i, NCH)
    for i in range(DEPTH): add("ple_norm%d" % i, NCH)
    add("final_norm", NCH)
    for j in range(6): add("mu%d" % j, NCH)
    for nm in ("w0", "a0", "k_k", "k_a", "ln_g", "ln_b", "r_k"): add(nm, NCH)
    for j in range(3): add("conv_w%d" % j, NCH)
    for i in range(DEPTH):
        for j in range(3): add("fcw%d_%d" % (i, j), NFC)
        add("fcb%d" % i, NFC)
    return cols, n


VCOL, NV = vec_layout()
C_ID = 0; C_ONES = 128; C_D1 = 256; C_SU = 512; C_IU = 640; C_SL = 768; C_BD = 896; C_EPS = 1024; C_GNE = 1025; C_BD1 = 1032; NCONST = 1160


def build_consts():
    c = np.zeros((128, NCONST), np.float32)
    c[:, C_ID:C_ID + 128] = np.eye(128)
    c[:, C_ONES:C_ONES + 128] = 1.0 / D
    qi = np.arange(128)[:, None]; kj = np.arange(256)[None, :]
    dist = qi - kj + 128
    c[:, C_D1:C_D1 + 256] = np.where((dist >= 0) & (dist <= 128), -dist.astype(np.float32), -1.0e6)
    i = np.arange(128)[:, None]; t = np.arange(128)[None, :]
    c[:, C_SU:C_SU + 128] = (i < t); c[:, C_IU:C_IU + 128] = (i <= t); c[:, C_SL:C_SL + 128] = (i > t)
    c[:, C_BD:C_BD + 128] = ((i // 64) == (t // 64)) / 64.0
    c[:, C_EPS] = EPS
    c[:, C_GNE] = 6.4e-4
    c[:, C_BD1:C_BD1 + 128] = ((i // 64) == (t // 64))
    return c


def pc(v, n):
    return np.ascontiguousarray(np.asarray(v, np.float32).reshape(n, 128).T)


class Prog:
    def __init__(self, nc, es, phases):
        self.nc = nc; self.kb = KB(nc, es); self.es = es; self.phases = phases
        kb = self.kb
        dt = lambda name, shape, kind="ExternalInput": nc.dram_tensor(name, list(shape), F32, kind=kind).ap()
        self.xT_in = dt("xT", [NCH, 128, S])
        self.pT = dt("pT", [DEPTH, 2, 128, S])
        self.vecs_d = dt("vecs", [128, NV]); self.consts_d = dt("consts", [128, NCONST])
        self.W = {}
        for name in wnames(phases):
            self.W[name] = dt(name, WSHAPES[name])
        self.out = Tile(dt("outT", [NCH, 128, S], kind="ExternalOutput"), "outT")
        self.xT = Tile(nc.dram_tensor("xT_s", [NCH, 128, S], F32).ap(), "xT_s")
        self.xin = Tile(self.xT_in, "xT_in")
        self.wtile = Tile(None, "weights")
        self.pp = Phase(kb, "g")
        self.vecs = self.pp.sb([128, NV], F32, "vecs"); self.consts = self.pp.sb([128, NCONST], F32, "consts")
        self.ffn_halo = self.pp.sb([128, NFC, 2], F32, "fhalo")
        self.identb = self.pp.sb([128, 128], BF16, "identb")
        kb.dma(kb.sp, self.vecs[:], View(Tile(self.vecs_d, "vd"), self.vecs_d))
        kb.dma(kb.sp, self.consts[:], View(Tile(self.consts_d, "cd"), self.consts_d))
        kb.copy(kb.dve, self.identb[:], self.consts[:, C_ID:C_ID + 128])
        self.xcur = self.xin
        self.build()

    def vcol(self, name, c=0):
        k = VCOL[name] + c
        return self.vecs[:, k:k + 1]

    def norm(self, ph, xsrc, t0, gname, HT, ps_pool, out_dram=None):
        kb = self.kb
        ones = self.consts[:, C_ONES:C_ONES + 128]
        k = 0
        for j in range(NTT):
            ps = ps_pool[j % len(ps_pool)]
            tsl = slice(t0 + j * TT, t0 + (j + 1) * TT)
            for c in range(NCH):
                xt = ph.xn[k % 4]; k += 1
                kb.dma(kb.sp, xt[:], xsrc[c, :, tsl])
                sq = ph.sq[c % 2]
                kb.activation(sq[:], xt[:], AF.Square)
                kb.mm(ps[:], ones, sq[:], start=(c == 0), stop=(c == NCH - 1))
            rstd = ph.rstd
            kb.activation(rstd[:], ps[:], AF.Sqrt, bias=self.consts[:, C_EPS:C_EPS + 1])
            kb.op(kb.dve, lambda: self.nc.vector.reciprocal(rstd.t[:], rstd.t[:]), [rstd[:]], [rstd[:]])
            for c in range(NCH):
                xt = ph.xn[k % 4]; k += 1
                kb.dma(kb.sp, xt[:], xsrc[c, :, tsl])
                E = kb.dve
                if out_dram is None:
                    kb.stt(E, HT[:, c, j * TT:(j + 1) * TT], xt[:], self.vcol(gname, c), rstd[:], ALU.mult, ALU.mult)
                else:
                    kb.stt(E, xt[:], xt[:], self.vcol(gname, c), rstd[:], ALU.mult, ALU.mult)
                    kb.dma(kb.sp, out_dram[c, :, tsl], xt[:])

    def norm_tiles(self, ph):
        ph.xn = [ph.sb([128, TT], F32, "xn") for _ in range(4)]
        ph.sq = [ph.sb([128, TT], F32, "sq") for _ in range(2)]
        ph.rstd = ph.sb([128, TT], F32, "rstd")

    def stream(self, ph, slots, loads, compute):
        kb = self.kb
        ns = len(slots)
        n = len(loads)
        for i in range(min(ns - 1, n)):
            loads[i](slots[i % ns])
        for i in range(n):
            if i + ns - 1 < n:
                loads[i + ns - 1](slots[(i + ns - 1) % ns])
            compute(i, slots[i % ns])

    def wload(self, slot_view, wap):
        kb = self.kb
        kb.dma(kb.pool, slot_view, View(self.wtile, wap), sb=slot_view.tile)

    def ffn(self, li):
        kb = self.kb
        Wgu = self.W["ffn_w_gu"][li].rearrange("(kc p) n -> p kc n", p=128)
        Wdn = self.W["ffn_w_down"][li].rearrange("(kc p) n -> p kc n", p=128)
        for s in range(NPASS):
            t0 = s * T
            ph = Phase(kb, "ffn%d_%d" % (li, s))
            HT = ph.sb([128, NCH, T], BF16, "HT")
            ACT = ph.sb([128, NFC, T], BF16, "ACT")
            slots = [ph.sb([128, NFC * 128], BF16, "ws") for _ in range(3)]
            G = [ph.sb([128, T + 2], F32, "G") for _ in range(2)]
            Cv = [ph.sb([128, T], F32, "Cv") for _ in range(2)]
            xr = [ph.sb([128, T], F32, "xr") for _ in range(2)]
            pss = [ph.ps([128, TT], F32, "ps") for _ in range(8)]
            self.norm_tiles(ph)
            self.norm(ph, self.xcur, t0, "ffn_norm%d" % li, HT, pss[0:2])
            def mk_load_gu(f):
                def ld(slot):
                    sv = slot.t[:, 0:NCH * 256].rearrange("p (k n) -> p k n", k=NCH)
                    self.wload(slot.v(sv[:, :, 0:128]), Wgu[:, :, f * 128:(f + 1) * 128])
                    self.wload(slot.v(sv[:, :, 128:256]), Wgu[:, :, FF + f * 128:FF + (f + 1) * 128])
                return ld

            def comp_gu(f, slot):
                sv = slot.t[:, 0:NCH * 256].rearrange("p (k n) -> p k n", k=NCH)
                g = G[f % 2]; cv = Cv[f % 2]
                if s == 0:
                    kb.memset(kb.dve, g[:, 0:2], 0.0)
                else:
                    kb.copy(kb.dve, g[:, 0:2], self.ffn_halo[:, f, :])
                pg = [pss[(4 * f + j) % 8] for j in range(2)]
                pu = [pss[(4 * f + 2 + j) % 8] for j in range(2)]
                for j in range(NTT):
                    for c in range(NCH):
                        kb.mm(pg[j][:], slot.v(sv[:, c, 0:128]), HT[:, c, j * TT:(j + 1) * TT], start=(c == 0), stop=(c == NCH - 1))
                    kb.copy(kb.act, g[:, 2 + j * TT: 2 + (j + 1) * TT], pg[j][:])
                for j in range(NTT):
                    for c in range(NCH):
                        kb.mm(pu[j][:], slot.v(sv[:, c, 128:256]), HT[:, c, j * TT:(j + 1) * TT], start=(c == 0), stop=(c == NCH - 1))
                kb.copy(kb.dve, self.ffn_halo[:, f, :], g[:, T:T + 2])
                w = lambda j: self.vcol("fcw%d_%d" % (li, j), f)
                kb.ts(kb.dve, cv[:], g[:, 2:T + 2], w(2), self.vcol("fcb%d" % li, f), ALU.mult, ALU.add)
                kb.stt(kb.dve, cv[:], g[:, 1:T + 1], w(1), cv[:], ALU.mult, ALU.add)
                kb.stt(kb.dve, cv[:], g[:, 0:T], w(0), cv[:], ALU.mult, ALU.add)
                kb.activation(cv[:], cv[:], AF.Silu)
                for j in range(NTT):
                    kb.tt(kb.dve, ACT[:, f, j * TT:(j + 1) * TT], cv[:, j * TT:(j + 1) * TT], pu[j][:], ALU.mult)
            self.stream(ph, slots, [mk_load_gu(f) for f in range(NFC)], comp_gu)

            def mk_load_dn(d):
                def ld(slot):
                    sv = slot.t[:, :].rearrange("p (k n) -> p k n", k=NFC)
                    self.wload(slot.v(sv), Wdn[:, :, d * 128:(d + 1) * 128])
                return ld

            def comp_dn(d, slot):
                sv = slot.t[:, :].rearrange("p (k n) -> p k n", k=NFC)
                x = xr[d % 2]
                kb.dma(kb.sp, x[:], self.xcur[d, :, t0:t0 + T])
                for j in range(NTT):
                    p = pss[(2 * d + j) % 8]
                    for c in range(NFC):
                        kb.mm(p[:], slot.v(sv[:, c, :]), ACT[:, c, j * TT:(j + 1) * TT], start=(c == 0), stop=(c == NFC - 1))
                    kb.tt(kb.dve, x[:, j * TT:(j + 1) * TT], x[:, j * TT:(j + 1) * TT], p[:], ALU.add)
                kb.dma(kb.sp, self.xT[d, :, t0:t0 + T], x[:])
            self.stream(ph, slots, [mk_load_dn(d) for d in range(NCH)], comp_dn)
            ph.close()
        self.xcur = self.xT

    def ple(self, li):
        kb = self.kb
        Wg = self.W["ple_w_gate"][li].rearrange("(kc p) n -> p kc n", p=128)
        Wp = self.W["ple_w_proj"][li].rearrange("(kc p) n -> p kc n", p=128)
        pT = Tile(self.pT, "pT")
        ones = self.consts[:, C_ONES:C_ONES + 128]
        for s in range(NPASS):
            t0 = s * T
            ph = Phase(kb, "ple%d_%d" % (li, s))
            HT = ph.sb([128, NCH, T], BF16, "HT")
            PT = ph.sb([128, 2, T], BF16, "PT")
            X = [ph.sb([128, T], F32, "X") for _ in range(NCH)]
            slots = [ph.sb([128, 18 * 128], BF16, "ws") for _ in range(3)]
            sg = [ph.sb([128, T], F32, "sg") for _ in range(2)]
            sq = [ph.sb([128, TT], F32, "sq") for _ in range(2)]
            rstd = [ph.sb([128, TT], F32, "rstd") for _ in range(NTT)]
            pss = [ph.ps([128, TT], F32, "ps") for _ in range(8)]
            for c in range(2):
                kb.dma(kb.pool, PT[:, c, :], pT[li, c, :, t0:t0 + T])
            for c in range(NCH):
                kb.dma(kb.sp, X[c][:], self.xcur[c, :, t0:t0 + T])
            for j in range(NTT):
                sl = slice(j * TT, (j + 1) * TT)
                for c in range(NCH):
                    kb.activation(sq[c % 2][:], X[c][:, sl], AF.Square)
                    kb.mm(pss[j][:], ones, sq[c % 2][:], start=(c == 0), stop=(c == NCH - 1))
                kb.activation(rstd[j][:], pss[j][:], AF.Sqrt, bias=self.consts[:, C_EPS:C_EPS + 1])
                kb.op(kb.dve, lambda r_=rstd[j]: self.nc.vector.reciprocal(r_.t[:], r_.t[:]), [rstd[j][:]], [rstd[j][:]])
                for c in range(NCH):
                    kb.stt(kb.dve, HT[:, c, sl], X[c][:, sl], self.vcol("ple_norm%d" % li, c), rstd[j][:], ALU.mult, ALU.mult)

            def mk_load(d):
                def ld(slot):
                    sv = slot.t[:, :].rearrange("p (k n) -> p k n", k=18)
                    self.wload(slot.v(sv[:, 0:16, :]), Wg[:, :, d * 128:(d + 1) * 128])
                    self.wload(slot.v(sv[:, 16:18, :]), Wp[:, :, d * 128:(d + 1) * 128])
                return ld

            def comp(d, slot):
                sv = slot.t[:, :].rearrange("p (k n) -> p k n", k=18)
                x = X[d]; g = sg[d % 2]
                for j in range(NTT):
                    pa = pss[(4 * d + j) % 8]; pb = pss[(4 * d + 2 + j) % 8]
                    sl = slice(j * TT, (j + 1) * TT)
                    for c in range(NCH):
                        kb.mm(pa[:], slot.v(sv[:, c, :]), HT[:, c, sl], start=(c == 0), stop=(c == NCH - 1))
                    for c in range(2):
                        kb.mm(pb[:], slot.v(sv[:, 16 + c, :]), PT[:, c, sl], start=(c == 0), stop=(c == 1))
                    kb.activation(g[:, sl], pa[:], AF.Sigmoid)
                    kb.tt(kb.dve, g[:, sl], g[:, sl], pb[:], ALU.mult)
                    kb.tt(kb.dve, x[:, sl], x[:, sl], g[:, sl], ALU.add)
                kb.dma(kb.sp, self.xT[d, :, t0:t0 + T], x[:])
            self.stream(ph, slots, [mk_load(d) for d in range(NCH)], comp)
            ph.close()
        self.xcur = self.xT

    def proj_to_dram(self, ph, HT, Wv, KC, ntiles, colfn, dstfn, pss, slots, stage, scalefn=None, dil=None):
        kb = self.kb
        def mk_load(i):
            def ld(slot):
                sv = slot.t[:, 0:KC * 128].rearrange("p (k n) -> p k n", k=KC)
                self.wload(slot.v(sv), Wv[:, :, colfn(i):colfn(i) + 128])
            return ld

        def comp(i, slot):
            sv = slot.t[:, 0:KC * 128].rearrange("p (k n) -> p k n", k=KC)
            st = stage[i % len(stage)]
            d = dil(i) if dil else 1
            for j in range(NTT):
                p = pss[(2 * i + j) % len(pss)]
                for c in range(KC):
                    kb.mm(p[:], slot.v(sv[:, c, :]), HT[:, c, j * TT:(j + 1) * TT], start=(c == 0), stop=(c == KC - 1))
                sc = scalefn(i) if scalefn else 1.0
                if d == 1:
                    kb.activation(st[:, j * TT:(j + 1) * TT], p[:], AF.Copy, scale=sc)
                else:
                    o = st.t[:, :].rearrange("p (r l) -> p r l", r=d)[:, :, j * TT // d:(j + 1) * TT // d]
                    i_ = p.t[:, :].rearrange("p (l r) -> p r l", r=d)
                    kb.activation(st.v(o), p.v(i_), AF.Copy, scale=sc)
            dstfn(i, st)
        self.stream(ph, slots, [mk_load(i) for i in range(ntiles)], comp)

    def proj_residual(self, ph, HT, Wv, KC, t0, pss, slots, xr):
        kb = self.kb
        def mk_load(d):
            def ld(slot):
                sv = slot.t[:, 0:KC * 128].rearrange("p (k n) -> p k n", k=KC)
                self.wload(slot.v(sv), Wv[:, :, d * 128:(d + 1) * 128])
            return ld

        def comp(d, slot):
            sv = slot.t[:, 0:KC * 128].rearrange("p (k n) -> p k n", k=KC)
            x = xr[d % 2]
            kb.dma(kb.sp, x[:], self.xcur[d, :, t0:t0 + T])
            for j in range(NTT):
                p = pss[(2 * d + j) % len(pss)]
                for c in range(KC):
                    kb.mm(p[:], slot.v(sv[:, c, :]), HT[:, c, j * TT:(j + 1) * TT], start=(c == 0), stop=(c == KC - 1))
                kb.tt(kb.dve, x[:, j * TT:(j + 1) * TT], x[:, j * TT:(j + 1) * TT], p[:], ALU.add)
            kb.dma(kb.sp, self.xT[d, :, t0:t0 + T], x[:])
        self.stream(ph, slots, [mk_load(d) for d in range(NCH)], comp)

    def attn(self, j_, li, parts="ABC", nheads=16):
        kb = self.kb; nc = self.nc
        DIL = (1, 4, 16)
        slopes = [[2.0 ** (-8.0 * (g * 16 + h + 1) / 48.0) for h in range(16)] for g in range(3)]
        if not hasattr(self, "qkvT"):
            self.qkvT = Tile(nc.dram_tensor("qkvT_s", [144, 128, S], BF16).ap(), "qkvT")
            self.oT = Tile(nc.dram_tensor("oT_s", [NCH, 128, S], BF16).ap(), "oT")
        Wqkv = self.W["attn_w_qkv"][j_].rearrange("(kc p) n -> p kc n", p=128)
        Wo = self.W["attn_w_o"][j_].rearrange("(kc p) n -> p kc n", p=128)
        for s in (range(NPASS) if "A" in parts else []):
            t0 = s * T
            ph = Phase(kb, "atA%d_%d" % (li, s))
            HT = ph.sb([128, NCH, T], BF16, "HT")
            slots = [ph.sb([128, NCH * 128], BF16, "ws") for _ in range(4)]
            stage = [ph.sb([128, T], BF16, "st") for _ in range(3)]
            pss = [ph.ps([128, TT], F32, "ps") for _ in range(8)]
            self.norm_tiles(ph)
            self.norm(ph, self.xcur, t0, "attn_norm%d" % j_, HT, pss[0:2])

            def dst(i, st):
                d = DIL[i // 48]; L = S // d; Lh = T // d
                dv = self.qkvT.t[i].rearrange("p (r l) -> p r l", r=d)[:, :, s * Lh:(s + 1) * Lh]
                sv = st.t[:, :].rearrange("p (r l) -> p r l", r=d)
                kb.dma(kb.sp, self.qkvT.v(dv), st.v(sv))
            self.proj_to_dram(ph, HT, Wqkv, NCH, 144, lambda i: i * 128, dst, pss, slots, stage,
                              scalefn=lambda i: (128.0 ** -0.5 if (i // 16) % 3 == 0 else 1.0),
                              dil=lambda i: DIL[i // 48])
            ph.close()
        ph = Phase(kb, "atB%d" % li)
        ident = self.consts[:, C_ID:C_ID + 128]
        ones1 = ph.sb([128, 128], F32, "ones1"); kb.memset(kb.dve, ones1[:], 1.0)
        qkv = [[ph.sb([128, S], BF16, "qkv") for _ in range(9)] for _ in range(2)]
        OgT = [ph.sb([128, S], F32, "OgT") for _ in range(3)]
        LgT2 = [[ph.sb([128, S], F32, "LgT") for _ in range(3)] for _ in range(2)]
        Vtm = [ph.sb([128, 16, 128], BF16, "Vtm") for _ in range(2)]
        Sb = [ph.sb([128, 256], F32, "Sb") for _ in range(3)]
        Pb = [ph.sb([128, 256], BF16, "Pb") for _ in range(3)]
        PT = [ph.sb([128, 2, 128], BF16, "PT") for _ in range(3)]
        Otm = [ph.sb([128, 128], F32, "Otm") for _ in range(3)]
        Dg = [ph.sb([128, 128], F32, "Dg") for _ in range(3)]
        m3 = ph.sb([128, S], F32, "m3"); ssum = ph.sb([128, S], F32, "ssum")
        Ob = [ph.sb([128, S], BF16, "Ob") for _ in range(2)]
        ps_s = [ph.ps([128, 256], F32, "pS") for _ in range(2)]
        ps_t = [ph.ps([128, 2, 128], BF16, "pT") for _ in range(2)]
        ps_v = [ph.ps([128, 128], BF16, "pV") for _ in range(1)]
        ps_o = [ph.ps([128, 128], F32, "pO") for _ in range(1)]
        ps_ot = [ph.ps([128, 128], F32, "pOT") for _ in range(1)]
        ps_l = [ph.ps([128, 128], F32, "pL") for _ in range(1)]
        sm = [ph.sb([128, 8], F32, "sm") for _ in range(10)]
        units = []
        for h in (range(nheads) if "B" in parts else []):
            for g in range(3):
                for i in range(16):
                    units.append((h, g, i))

        def load_head(h):
            bufs = qkv[h % 2]
            for g in range(3):
                for jj in range(3):
                    kb.dma(kb.sp, bufs[g * 3 + jj][:], self.qkvT[(g * 3 + jj) * 16 + h])

        def prep_group(h, g):
            VT_ = qkv[h % 2][g * 3 + 2]
            vt = Vtm[(h * 3 + g) % 2]
            for i in range(16):
                pv = ps_v[0]
                kb.transpose(pv[:], VT_[:, i * 128:(i + 1) * 128], self.identb[:])
                kb.copy(kb.act, vt[:, i, :], pv[:])

        def geom(u):
            h, g, i = units[u]
            d = DIL[g]; nb = (S // d) // 128
            n = i % nb; nk = 2 if n > 0 else 1
            return h, g, i, d, nb, n, nk

        def st0(u):
            h, g, i, d, nb, n, nk = geom(u)
            if g == 0 and i == 0 and h + 1 < nheads: load_head(h + 1)
            if i == 0: prep_group(h, g)
            QT_, KT_ = qkv[h % 2][g * 3], qkv[h % 2][g * 3 + 1]
            k0 = (i - 1) * 128 if n > 0 else i * 128
            kb.mm(ps_s[u % 2][:, 0:nk * 128], QT_[:, i * 128:(i + 1) * 128], KT_[:, k0:k0 + nk * 128])

        def st1(u):
            h, g, i, d, nb, n, nk = geom(u)
            W_ = nk * 128; b = u % 3; s4 = sm[u % 10]; pS = ps_s[u % 2]
            coef = slopes[g][h] * d
            dv = self.consts[:, C_D1:C_D1 + 256] if nk == 2 else self.consts[:, C_D1 + 128:C_D1 + 256]
            kb.stt(kb.dve, Sb[b][:, 0:W_], dv, coef, pS[:, 0:W_], ALU.mult, ALU.add)
            kb.op(kb.dve, lambda a=s4, sb_=Sb[b], w_=W_: nc.vector.tensor_reduce(a.t[:, 1:2], sb_.t[:, 0:w_], AX.X, ALU.max, negate=True), [Sb[b][:, 0:W_]], [s4[:, 1:2]])

        def st2(u):
            h, g, i, d, nb, n, nk = geom(u)
            W_ = nk * 128; b = u % 3; s4 = sm[u % 10]
            kb.activation(Pb[b][:, 0:W_], Sb[b][:, 0:W_], AF.Exp, bias=s4[:, 1:2], accum=s4[:, 2:3])
            kb.activation(s4[:, 3:4], s4[:, 2:3], AF.Ln)

        def st3(u):
            h, g, i, d, nb, n, nk = geom(u)
            b = u % 3; pT = ps_t[u % 2]; s4 = sm[u % 10]
            for kt in range(nk):
                kb.transpose(pT[:, kt, :], Pb[b][:, kt * 128:(kt + 1) * 128], self.identb[:])
            kb.tt(kb.dve, s4[:, 4:5], s4[:, 3:4], s4[:, 1:2], ALU.subtract)
            kb.op(kb.dve, lambda a=s4: nc.vector.reciprocal(a.t[:, 5:6], a.t[:, 2:3]), [s4[:, 2:3]], [s4[:, 5:6]])

        def st4(u):
            h, g, i, d, nb, n, nk = geom(u)
            b = u % 3; pT = ps_t[u % 2]; s4 = sm[u % 10]
            kb.copy(kb.act, PT[b][:, 0:nk, :], pT[:, 0:nk, :])
            kb.activation(Dg[b][:], ident, AF.Copy, scale=s4[:, 4:5])

        def st5(u):
            h, g, i, d, nb, n, nk = geom(u)
            b = u % 3; pO = ps_o[0]; vt = Vtm[(h * 3 + g) % 2]
            for kt in range(nk):
                kb.mm(pO[:], PT[b][:, kt, :], vt[:, i - (nk - 1) + kt, :], start=(kt == 0), stop=(kt == nk - 1))
            kb.mm(ps_l[0][:], ones1[:], Dg[b][:])

        def st6(u):
            h, g, i, d, nb, n, nk = geom(u)
            b = u % 3; s4 = sm[u % 10]; pO = ps_o[0]
            kb.ts(kb.dve, Otm[b][:], pO[:], s4[:, 5:6], None, ALU.mult)
            r = i // nb
            c0 = r + d * 128 * n
            LgT = LgT2[h % 2]
            kb.copy(kb.dve, LgT[g].v(LgT[g].t[:, c0:c0 + d * 127 + 1:d]), ps_l[0][:])

        def st7(u):
            b = u % 3
            kb.transpose(ps_ot[0][:], Otm[b][:], ident)

        def st8(u):
            h, g, i, d, nb, n, nk = geom(u)
            r = i // nb
            c0 = r + d * 128 * n
            kb.copy(kb.act, OgT[g].v(OgT[g].t[:, c0:c0 + d * 127 + 1:d]), ps_ot[0][:])
            if g == 2 and i == 15: combine(h)

        def combine(h):
            LgT = LgT2[h % 2]
            kb.tt(kb.dve, m3[:], LgT[0][:], LgT[1][:], ALU.max)
            kb.tt(kb.dve, m3[:], m3[:], LgT[2][:], ALU.max)
            for g in range(3):
                kb.tt(kb.dve, LgT[g][:], LgT[g][:], m3[:], ALU.subtract)
                kb.activation(LgT[g][:], LgT[g][:], AF.Exp)
            kb.tt(kb.dve, ssum[:], LgT[0][:], LgT[1][:], ALU.add)
            kb.tt(kb.dve, ssum[:], ssum[:], LgT[2][:], ALU.add)
            kb.op(kb.dve, lambda: nc.vector.reciprocal(ssum.t[:], ssum.t[:]), [ssum[:]], [ssum[:]])
            for g in range(3):
                kb.tt(kb.dve if g != 1 else kb.dve, OgT[g][:], OgT[g][:], LgT[g][:], ALU.mult)
            kb.tt(kb.dve, OgT[0][:], OgT[0][:], OgT[1][:], ALU.add)
            kb.tt(kb.dve, OgT[0][:], OgT[0][:], OgT[2][:], ALU.add)
            ob = Ob[h % 2]
            kb.tt(kb.dve, ob[:], OgT[0][:], ssum[:], ALU.mult)
            kb.dma(kb.sp, self.oT[h], ob[:])

        stages = [st0, st1, st2, st3, st4, st5, st6, st7, st8]
        if units: load_head(0)
        for tick in range(len(units) + len(stages) - 1):
            for si in reversed(range(len(stages))):
                u = tick - si
                if 0 <= u < len(units): stages[si](u)
        ph.close()
        for s in (range(NPASS) if "C" in parts else []):
            t0 = s * T
            ph = Phase(kb, "atC%d_%d" % (li, s))
            HT = ph.sb([128, NCH, T], BF16, "HT")
            slots = [ph.sb([128, NCH * 128], BF16, "ws") for _ in range(3)]
            xr = [ph.sb([128, T], F32, "xr") for _ in range(2)]
            pss = [ph.ps([128, TT], F32, "ps") for _ in range(8)]
            for c in range(NCH):
                kb.dma(kb.sp, HT[:, c, :], self.oT[c, :, t0:t0 + T])
            self.proj_residual(ph, HT, Wo, NCH, t0, pss, slots, xr)
            ph.close()
        self.xcur = self.xT

    def conv(self):
        kb = self.kb
        Win = self.W["conv_w_in"][0].rearrange("(kc p) n -> p kc n", p=128)
        Wout = self.W["conv_w_out"][0].rearrange("(kc p) n -> p kc n", p=128)
        if not hasattr(self, "cv_halo"):
            self.cv_halo = self.pp.sb([128, NCH, 2], F32, "chalo")
        for s in range(NPASS):
            t0 = s * T
            ph = Phase(kb, "cv%d" % s)
            HT = ph.sb([128, NCH, T], BF16, "HT")
            ZT = ph.sb([128, NCH, T], BF16, "ZT")
            slots = [ph.sb([128, NCH * 3 * 128], BF16, "ws") for _ in range(3)]
            G = [ph.sb([128, T + 2], F32, "G") for _ in range(2)]
            Cv = [ph.sb([128, T], F32, "Cv") for _ in range(2)]
            Cs = [ph.sb([128, T], F32, "Cs") for _ in range(2)]
            xr = [ph.sb([128, T], F32, "xr") for _ in range(2)]
            pss = [ph.ps([128, TT], F32, "ps") for _ in range(8)]
            self.norm_tiles(ph)
            self.norm(ph, self.xcur, t0, "conv_norm", HT, pss[0:2])

            def mk_load(f):
                def ld(slot):
                    sv = slot.t[:, :].rearrange("p (k n) -> p k n", k=NCH)
                    for q in range(3):
                        self.wload(slot.v(sv[:, :, q * 128:(q + 1) * 128]), Win[:, :, q * D + f * 128:q * D + (f + 1) * 128])
                return ld

            def comp(f, slot):
                sv = slot.t[:, :].rearrange("p (k n) -> p k n", k=NCH)
                g = G[f % 2]; cv = Cv[f % 2]; cs = Cs[f % 2]
                if s == 0:
                    kb.memset(kb.dve, g[:, 0:2], 0.0)
                else:
                    kb.copy(kb.dve, g[:, 0:2], self.cv_halo[:, f, :])
                for j in range(NTT):
                    sl = slice(j * TT, (j + 1) * TT)
                    pc_, pu_, pb_ = pss[(3 * j) % 8], pss[(3 * j + 1) % 8], pss[(3 * j + 2) % 8]
                    for q, p in ((1, pc_), (2, pu_), (0, pb_)):
                        for c in range(NCH):
                            kb.mm(p[:], slot.v(sv[:, c, q * 128:(q + 1) * 128]), HT[:, c, sl], start=(c == 0), stop=(c == NCH - 1))
                    kb.copy(kb.act, cs[:, sl], pc_[:])
                    kb.tt(kb.dve, g[:, 2 + j * TT:2 + (j + 1) * TT], cs[:, sl], pu_[:], ALU.mult)
                    kb.copy(kb.act, cs[:, sl], pb_[:])
                kb.copy(kb.dve, self.cv_halo[:, f, :], g[:, T:T + 2])
                w = lambda j: self.vcol("conv_w%d" % j, f)
                kb.ts(kb.dve, cv[:], g[:, 2:T + 2], w(2), None, ALU.mult)
                kb.stt(kb.dve, cv[:], g[:, 1:T + 1], w(1), cv[:], ALU.mult, ALU.add)
                kb.stt(kb.dve, cv[:], g[:, 0:T], w(0), cv[:], ALU.mult, ALU.add)
                kb.tt(kb.dve, ZT[:, f, :], cv[:], cs[:], ALU.mult)
            self.stream(ph, slots, [mk_load(f) for f in range(NCH)], comp)
            self.proj_residual(ph, ZT, Wout, NCH, t0, pss, slots, xr)
            ph.close()
        self.xcur = self.xT


    def rwkv(self, nchunks=16, parts="ABC", dbg=9):
        kb = self.kb; nc = self.nc
        scr = lambda name, dt=F32: Tile(nc.dram_tensor(name, [NCH, 128, S], dt).ap(), name)
        R_, K_, V_, LW_, A_, G_ = [scr("rw_%s" % n) for n in "rkvwag"]
        YG = scr("rw_yg", BF16)
        hlast = self.pp.sb([128, NCH, 1], BF16, "hlast")
        negw0 = self.pp.sb([128, NCH], F32, "negw0")
        kb.ts(kb.dve, negw0[:], self.vecs[:, VCOL["w0"]:VCOL["w0"] + NCH], -1.0, None, ALU.mult)
        rearr = lambda w: w.rearrange("(kc p) n -> p kc n", p=128)
        for s in (range(NPASS) if "A" in parts else []):
            t0 = s * T
            ph = Phase(kb, "rwA%d" % s)
            HT = ph.sb([128, NCH, T], BF16, "HT"); XX = ph.sb([128, NCH, T], BF16, "XX"); Xn = ph.sb([128, NCH, T], BF16, "Xn")
            slots = [ph.sb([128, NCH * 256], BF16, "ws") for _ in range(3)]
            stage = [ph.sb([128, T], F32, "st") for _ in range(2)]
            T1 = ph.sb([128, 2, T], BF16, "T1")
            W2 = ph.sb([128, 2, D], BF16, "W2")
            pss = [ph.ps([128, TT], F32, "ps") for _ in range(8)]
            self.norm_tiles(ph)
            self.norm(ph, self.xcur, t0, "rwkv_norm", HT, pss[0:2])
            for c in range(NCH):
                if s == 0:
                    kb.ts(kb.dve, XX[:, c, 0:1], HT[:, c, 0:1], -1.0, None, ALU.mult)
                else:
                    kb.tt(kb.dve, XX[:, c, 0:1], hlast[:, c, :], HT[:, c, 0:1], ALU.subtract)
                kb.tt(kb.dve, XX[:, c, 1:T], HT[:, c, 0:T - 1], HT[:, c, 1:T], ALU.subtract)
            for c in range(NCH):
                kb.copy(kb.dve, hlast[:, c, :], HT[:, c, T - 1:T])

            def mix(n):
                for c in range(NCH):
                    kb.stt(kb.dve, Xn[:, c, :], XX[:, c, :], self.vcol("mu%d" % n, c), HT[:, c, :], ALU.mult, ALU.add)
            for n, dst in enumerate([R_, K_, V_]):
                mix(n)
                self.proj_to_dram(ph, Xn, rearr(self.W["rwkv_w_rkv"][0][n]), NCH, 16, lambda i: i * 128,
                                  lambda i, st, dst=dst: kb.dma(kb.sp, dst[i, :, t0:t0 + T], st[:]), pss, slots, stage)

            def lora(n, w1name, w2name, rnk, func1, epi, dst):
                mix(n)
                nm = (rnk + 127) // 128; rp = min(rnk, 128)
                sl = slots[0]
                sv = sl.t[:, 0:NCH * rnk].rearrange("p (k n) -> p k n", k=NCH)
                self.wload(sl.v(sv), rearr(self.W[w1name][0]))
                w2v = self.W[w2name][0].rearrange("(m p) n -> p m n", p=rp)
                self.wload(W2[0:rp, 0:nm, :], w2v)
                k = 0
                for m in range(nm):
                    for j in range(NTT):
                        p = pss[k % 8]; k += 1
                        for c in range(NCH):
                            kb.mm(p[0:rp, :], sl.v(sv[:, c, m * 128:m * 128 + rp]), Xn[:, c, j * TT:(j + 1) * TT], start=(c == 0), stop=(c == NCH - 1))
                        kb.activation(T1[0:rp, m, j * TT:(j + 1) * TT], p[0:rp, :], func1)
                for i in range(NCH):
                    st = stage[i % 2]
                    for j in range(NTT):
                        p = pss[k % 8]; k += 1
                        for m in range(nm):
                            kb.mm(p[:], W2[0:rp, m, i * 128:(i + 1) * 128], T1[0:rp, m, j * TT:(j + 1) * TT], start=(m == 0), stop=(m == nm - 1))
                        epi(i, st[:, j * TT:(j + 1) * TT], p)
                    kb.dma(kb.sp, dst[i, :, t0:t0 + T], st[:])

            def epi_w(i, o, p):
                kb.activation(o, p[:], AF.Exp, bias=negw0[:, i:i + 1], scale=-1.0)
                kb.ts(kb.dve, o, o, 1.0, None, ALU.add)
                kb.op(kb.dve, lambda: nc.vector.reciprocal(o.ap, o.ap), [o], [o])
                kb.ts(kb.dve, o, o, -0.6065306597126334, None, ALU.mult)
            lora(3, "rwkv_w_w1", "rwkv_w_w2", 96, AF.Tanh, epi_w, LW_)
            lora(4, "rwkv_w_a1", "rwkv_w_a2", 96, AF.Copy, lambda i, o, p: kb.activation(o, p[:], AF.Sigmoid, bias=self.vcol("a0", i)), A_)
            lora(5, "rwkv_w_g1", "rwkv_w_g2", 256, AF.Sigmoid, lambda i, o, p: kb.copy(kb.act, o, p[:]), G_)
            ph.close()
        ph = Phase(kb, "rwB")
        cst = lambda c0: self.consts[:, c0:c0 + 128]
        ident, SU, IU, SL, BD, BD1 = cst(C_ID), cst(C_SU), cst(C_IU), cst(C_SL), cst(C_BD), cst(C_BD1)
        GB = 4; GW = GB * 128; NQ = S // GW
        NSETS = RW_NSETS
        BN = ("r", "k", "v", "lw", "a", "g", "B1", "B2", "B3", "B4", "B5")
        sets = []
        for si in range(NSETS):
            d_ = {n: ph.sb([128, GW], F32, n) for n in BN}
            d_["gC"] = ph.sb([128, GB], F32, "gC"); d_["yg"] = ph.sb([128, GW], BF16, "yg")
            sets.append(d_)
        TMs = [[ph.sb([128, GB, 128], F32, n) for n in ("Vtm", "VP0", "VP1", "Btm", "Ktm")] for _ in range(2)]
        UM = [[[ph.sb([128, 128], F32, "um") for _ in range(4)] for _ in range(2 * GB)] for _ in range(2)]
        QQ = [[ph.sb([128, 128], F32, "qq") for _ in range(6)] for _ in range(2 * GB)]
        STP = ph.sb([128, 128], F32, "STP"); RH = ph.sb([128, 128], F32, "RH"); SAB = ph.sb([128, 128], F32, "SAB")
        SAP = [ph.sb([128, 128], F32, "SAP") for _ in range(2)]
        t1 = ph.sb([128, 128], F32, "t1")
        pg = [ph.ps([128, 128], F32, "pg") for _ in range(2)]
        pd = [ph.ps([128, 128], F32, "pd") for _ in range(3)]
        ptr = ph.ps([128, 512], F32, "ptr")
        pq = [ph.ps([128, 512], F32, "pq") for _ in range(2)]
        for q in range(2):
            kb.memset(kb.dve, TMs[q][1][:], 0.0); kb.memset(kb.dve, TMs[q][2][:], 0.0)
        kb.memset(kb.dve, SAP[0][:], 0.0); kb.memset(kb.dve, SAP[1][:], 0.0)
        recip = lambda o: kb.op(kb.dve, lambda: nc.vector.reciprocal(o.ap, o.ap), [o], [o])
        groups = [(c, q) for c in (range(nchunks) if "B" in parts else []) for q in range(NQ)]
        NGR = len(groups)
        SRC = (("r", R_), ("k", K_), ("v", V_), ("lw", LW_), ("a", A_), ("g", G_))

        def load(gi):
            c, q = groups[gi]; S_ = sets[gi % NSETS]
            for n, src in SRC:
                kb.dma(kb.sp, S_[n][:], src[c, :, q * GW:(q + 1) * GW])

        def prep_ops(gi):
            c, q = groups[gi]; S_ = sets[gi % NSETS]
            r, k, v, lw, a, g, B1, B2, B3, B4, B5 = [S_[n] for n in BN]
            v3 = lambda t_: t_.t[:, :].rearrange("p (n q) -> p n q", q=128)
            ops = []
            A = ops.append
            A(lambda: kb.ts(kb.dve, B1[:], k[:], self.vcol("k_k", c), None, ALU.mult))
            A(lambda: kb.activation(B2[:], B1[:], AF.Square))
            A(lambda: (kb.mm(ptr[:], BD1, B2[:]), kb.activation(B3[:], ptr[:], AF.Sqrt)))
            A(lambda: kb.ts(kb.dve, B3[:], B3[:], 1e-12, None, ALU.max))
            A(lambda: recip(B3[:]))
            A(lambda: kb.tt(kb.dve, B1[:], B1[:], B3[:], ALU.mult))
            A(lambda: kb.ts(kb.dve, B2[:], a[:], -1.0, self.vcol("k_a", c), ALU.add, ALU.mult))
            A(lambda: kb.stt(kb.dve, k[:], B2[:], 1.0, k[:], ALU.add, ALU.mult))
            A(lambda: kb.tt(kb.dve, a[:], B1[:], a[:], ALU.mult))
            src = lw; pp2 = [B2, B3]; sh = 1; qi = 0
            while sh < 128:
                dst = pp2[qi % 2]; qi += 1
                A(lambda dst=dst, src=src, sh=sh: kb.tt(kb.dve, dst.v(v3(dst)[:, :, sh:128]), src.v(v3(src)[:, :, sh:128]), src.v(v3(src)[:, :, 0:128 - sh]), ALU.add))
                A(lambda dst=dst, src=src, sh=sh: kb.copy(kb.act, dst.v(v3(dst)[:, :, 0:sh]), src.v(v3(src)[:, :, 0:sh])))
                src = dst; sh *= 2
            cum = src
            A(lambda: kb.activation(B4[:], cum[:], AF.Exp))
            A(lambda: kb.activation(B5[:], cum[:], AF.Exp, scale=-1.0))
            A(lambda: kb.tt(kb.dve, lw[:], cum[:], lw[:], ALU.subtract))
            A(lambda: kb.activation(lw[:], lw[:], AF.Exp))
            A(lambda: kb.copy(kb.dve, S_["gC"][:], B4.v(v3(B4)[:, :, 127])))
            A(lambda: kb.stt(kb.dve, B1[:], B1[:], -1.0, lw[:], ALU.mult, ALU.mult))
            A(lambda: kb.tt(kb.dve, a[:], a[:], B5[:], ALU.mult))
            A(lambda: kb.tt(kb.dve, B5[:], k[:], B5[:], ALU.mult))
            A(lambda: kb.tt(kb.dve, B4[:], r[:], B4[:], ALU.mult))
            return ops

        dstate = {"di": 0, "gi": 0}

        def front_a(gi, sl_, as_items=False):
            its = []
            S_ = sets[gi % NSETS]
            At, Bt, Kt, Rt, v = S_["B1"], S_["a"], S_["B5"], S_["B4"], S_["v"]
            Vtm, VP0, VP1, Btm, Ktm = TMs[sl_]; UMs = UM[sl_]
            def tr_item(bi):
                bs = slice(bi * 128, (bi + 1) * 128)
                kb.mm(ptr[:, 0:128], v[:, bs], ident)
                kb.copy(kb.act, Vtm[:, bi, :], ptr[:, 0:128])
                kb.copy(kb.dve, VP0[:, bi, 0:64], Vtm[:, bi, 0:64])
                kb.copy(kb.dve, VP1[:, bi, 64:128], Vtm[:, bi, 64:128])
                kb.mm(ptr[:, 0:128], Bt[:, bs], ident)
                kb.copy(kb.dve, Btm[:, bi, :], ptr[:, 0:128])
                kb.mm(ptr[:, 0:128], Kt[:, bs], ident)
                kb.copy(kb.act, Ktm[:, bi, :], ptr[:, 0:128])

            gbanks = pg + pd

            def gram_item(u):
                bi, hh = u // 2, u % 2
                bs = slice(bi * 128, (bi + 1) * 128); hp = slice(64 * hh, 64 * hh + 64)
                Uak, Urb, Urk, XT = UMs[u]; Qa, Qb, La, Lb, Pa, Pb = QQ[u]
                for (lt, rt, msk, dst) in ((Bt, At, SU, Qa), (At, Bt, SL, La), (Kt, At, SU, Uak), (Bt, Rt, IU, Urb), (Kt, Rt, IU, Urk)):
                    p = gbanks[dstate["gi"] % len(gbanks)]; dstate["gi"] += 1
                    kb.mm(p[:], lt[hp, bs], rt[hp, bs])
                    kb.tt(kb.dve, dst[:], p[:], msk, ALU.mult)
                kb.tt(kb.dve, Pa[:], Qa[:], ident, ALU.add)
            for bi in range(GB): its.append(lambda bi=bi: tr_item(bi))
            for u in range(2 * GB): its.append(lambda u=u: gram_item(u))
            if as_items: return its
            for f_ in its: f_()


        def dbl_level(sl_, lev, only=None):
            for u in (range(2 * GB) if only is None else [only]):
                src_i = lev % 2
                Qc, Lc, Pc = QQ[u][0 + src_i], QQ[u][2 + src_i], QQ[u][4 + src_i]
                Qn, Ln, Pn = QQ[u][1 - src_i], QQ[u][3 - src_i], QQ[u][5 - src_i]
                last = (lev == 5)
                p = pd[dstate["di"] % 3]; dstate["di"] += 1
                kb.mm(p[:], Qc[:], Lc[:])
                kb.copy(kb.act, Ln[:], p[:])
                if not last:
                    p = pd[dstate["di"] % 3]; dstate["di"] += 1
                    kb.mm(p[:], Lc[:], Qc[:])
                    kb.copy(kb.act if u % 2 else kb.dve, Qn[:], p[:])
                p = pd[dstate["di"] % 3]; dstate["di"] += 1
                kb.mm(p[:], Ln[:], Pc[:])
                outP = UM[sl_][u][3] if last else Pn
                kb.tt(kb.dve, outP[:], p[:], Pc[:], ALU.add)

        def seq_block(gi, sl_, bi, part=None):
            c, q = groups[gi]; S_ = sets[gi % NSETS]
            At, Rt, Y, gC = S_["B1"], S_["B4"], S_["lw"], S_["gC"]
            Vtm, VP0, VP1, Btm, Ktm = TMs[sl_]; UMs = UM[sl_]
            bs = slice(bi * 128, (bi + 1) * 128)
            p1 = pq[0]; p2 = pq[1]
            if part in (None, 0):
                if q == 0 and bi == 0:
                    kb.memset(kb.dve, STP[:], 0.0)
                for hh in range(2):
                    hp = slice(64 * hh, 64 * hh + 64); hc = slice(64 * hh, 64 * hh + 64)
                    kb.mm(p1[:, hc], At[hp, bs], STP[hp, hc], start=True, stop=False)
                    kb.mm(p1[:, hc], UMs[2 * bi + hh][0][:], Vtm[:, bi, hc], start=False, stop=True)
                kb.copy(kb.act, RH[:], p1[:, 0:128])
            if part in (None, 1):
                for hh in range(2):
                    hc = slice(64 * hh, 64 * hh + 64)
                    kb.mm(p2[:, hc], UMs[2 * bi + hh][3][:], RH[:, hc])
                kb.copy(kb.act, SAB[:], p2[:, 0:128])
                kb.copy(kb.dve, SAP[0][:, 0:64], SAB[:, 0:64])
                kb.copy(kb.act, SAP[1][:, 64:128], SAB[:, 64:128])
            if part not in (None, 2): return
            py = p1
            kb.mm(py[:, 128:256], STP[:], Rt[:, bs], start=True, stop=False)
            for hh in range(2):
                kb.mm(py[:, 128:256], SAP[hh][:], UMs[2 * bi + hh][1][:], start=False, stop=False)
            kb.mm(py[:, 128:256], VP0[:, bi, :], UMs[2 * bi][2][:], start=False, stop=False)
            kb.mm(py[:, 128:256], VP1[:, bi, :], UMs[2 * bi + 1][2][:], start=False, stop=True)
            kb.copy(kb.act, Y[:, bs], py[:, 128:256])
            pst = p2
            kb.mm(pst[:, 128:256], Btm[:, bi, :], SAB[:], start=True, stop=False)
            kb.mm(pst[:, 128:256], Ktm[:, bi, :], Vtm[:, bi, :], start=False, stop=True)
            kb.stt(kb.dve, t1[:], pst[:, 128:256], gC[:, bi:bi + 1], BD1, ALU.mult, ALU.mult)
            kb.stt(kb.dve, STP[:], STP[:], gC[:, bi:bi + 1], t1[:], ALU.mult, ALU.add)

        def post_ops(gi):
            c, q = groups[gi]; S_ = sets[gi % NSETS]
            r, k, v, Y, g, B1, B2, B3, yg = S_["r"], S_["k"], S_["v"], S_["lw"], S_["g"], S_["B1"], S_["B2"], S_["B3"], S_["yg"]
            ops = []
            A = ops.append
            A(lambda: (kb.mm(ptr[:], BD, Y[:]), kb.tt(kb.dve, B2[:], Y[:], ptr[:], ALU.subtract)))
            A(lambda: kb.activation(B3[:], B2[:], AF.Square))
            A(lambda: (kb.mm(ptr[:], BD, B3[:]), kb.activation(B3[:], ptr[:], AF.Sqrt, bias=self.consts[:, C_GNE:C_GNE + 1])))
            A(lambda: recip(B3[:]))
            A(lambda: kb.tt(kb.dve, B2[:], B2[:], B3[:], ALU.mult))
            A(lambda: kb.ts(kb.dve, B2[:], B2[:], self.vcol("ln_g", c), self.vcol("ln_b", c), ALU.mult, ALU.add))
            A(lambda: kb.tt(kb.dve, B3[:], r[:], k[:], ALU.mult))
            A(lambda: kb.ts(kb.dve, B3[:], B3[:], self.vcol("r_k", c), None, ALU.mult))
            A(lambda: (kb.mm(ptr[:], BD1, B3[:]), kb.tt(kb.dve, B1[:], ptr[:], v[:], ALU.mult)))
            A(lambda: kb.tt(kb.dve, B2[:], B2[:], B1[:], ALU.add))
            A(lambda: kb.tt(kb.dve, yg[:], B2[:], g[:], ALU.mult))
            A(lambda: kb.dma(kb.sp, YG[c, :, q * GW:(q + 1) * GW], yg[:]))
            return ops

        if NGR:
            for gi in range(min(NSETS - 1, NGR)): load(gi)
            for gi in range(min(2, NGR)):
                for o in prep_ops(gi): o()
            front_a(0, 0)
            for lev in range(6): dbl_level(0, lev)
        for i in range(NGR):
            sl_ = i % 2
            items = []
            if i + 1 < NGR:
                Fs = front_a(i + 1, 1 - sl_, as_items=True)
                Ls = [(lev, u) for lev in range(6) for u in range(2 * GB)]
                Ss = [(bi, part) for bi in range(GB) for part in range(3)]
                nfs = 4
                for k_ in range(nfs):
                    for f_ in Fs[k_ * 3:(k_ + 1) * 3]: items.append(f_)
                    bi, part = Ss[k_]
                    items.append(lambda bi=bi, part=part: seq_block(i, sl_, bi, part))
                Ss = Ss[nfs:]
                li_ = 0
                for (bi, part) in Ss:
                    for _ in range(len(Ls) // len(Ss)):
                        lev, u = Ls[li_]; li_ += 1
                        items.append(lambda lev=lev, u=u: dbl_level(1 - sl_, lev, u))
                    items.append(lambda bi=bi, part=part: seq_block(i, sl_, bi, part))
                while li_ < len(Ls):
                    lev, u = Ls[li_]; li_ += 1
                    items.append(lambda lev=lev, u=u: dbl_level(1 - sl_, lev, u))
            else:
                for bi in range(GB): items.append(lambda bi=bi: seq_block(i, sl_, bi))
            extras = []
            if i >= 1: extras += post_ops(i - 1)
            if i + 2 < NGR: extras += prep_ops(i + 2)
            per = (len(extras) + len(items) - 1) // len(items) if extras else 0
            ei = 0
            for it_ in items:
                it_()
                for _ in range(per):
                    if ei < len(extras): extras[ei](); ei += 1
            while ei < len(extras): extras[ei](); ei += 1
            if i + NSETS - 1 < NGR: load(i + NSETS - 1)
        if NGR:
            for o in post_ops(NGR - 1): o()
        ph.close()
        Wo = rearr(self.W["rwkv_w_o"][0])
        for s in (range(NPASS) if "C" in parts else []):
            t0 = s * T
            ph = Phase(kb, "rwC%d" % s)
            HT = ph.sb([128, NCH, T], BF16, "HT")
            slots = [ph.sb([128, NCH * 128], BF16, "ws") for _ in range(3)]
            xr = [ph.sb([128, T], F32, "xr") for _ in range(2)]
            pss = [ph.ps([128, TT], F32, "ps") for _ in range(8)]
            for c in range(NCH):
                kb.dma(kb.sp, HT[:, c, :], YG[c, :, t0:t0 + T])
            self.proj_residual(ph, HT, Wo, NCH, t0, pss, slots, xr)
            ph.close()
        self.xcur = self.xT

    def final(self):
        kb = self.kb
        for s in range(NPASS):
            ph = Phase(kb, "fin%d" % s)
            pss = [ph.ps([128, TT], F32, "ps") for _ in range(2)]
            self.norm_tiles(ph)
            self.norm(ph, self.xcur, s * T, "final_norm", None, pss, out_dram=self.out)
            ph.close()

    def copy_x(self):
        kb = self.kb
        ph = Phase(kb, "cp")
        xt = [ph.sb([128, S], F32, "x") for _ in range(2)]
        for c in range(NCH):
            kb.dma(kb.sp, xt[c % 2][:], self.xcur[c, :, :])
            kb.dma(kb.sp, self.out[c, :, :], xt[c % 2][:])
        ph.close()

    def build(self):
        for p in self.phases:
            if p[0] == "ffn": self.ffn(p[1])
            elif p[0] == "ple": self.ple(p[1])
            elif p[0] == "attn": self.attn(*p[1:])
            elif p[0] == "rwkv": self.rwkv(*p[1:])
            elif p[0] == "conv": self.conv()
            elif p[0] == "final": self.final()
            elif p[0] == "copy": self.copy_x()
        self.kb.barrier()
        self.pp.es.close()


WSHAPES = {
    "attn_w_qkv": [2, D, 9 * D], "attn_w_o": [2, D, D],
    "rwkv_w_rkv": [1, 3, D, D], "rwkv_w_w1": [1, D, 96], "rwkv_w_w2": [1, 96, D],
    "rwkv_w_a1": [1, D, 96], "rwkv_w_a2": [1, 96, D], "rwkv_w_g1": [1, D, 256], "rwkv_w_g2": [1, 256, D],
    "rwkv_w_o": [1, D, D],
    "conv_w_in": [1, D, 3 * D], "conv_w_out": [1, D, D],
    "ffn_w_gu": [DEPTH, D, 2 * FF], "ffn_w_down": [DEPTH, FF, D],
    "ple_w_proj": [DEPTH, PLE, D], "ple_w_gate": [DEPTH, D, D],
}

def wnames(phases):
    pre = {"ffn": "ffn_", "ple": "ple_", "attn": "attn_", "rwkv": "rwkv_", "conv": "conv_"}
    used = set(pre[p[0]] for p in phases if p[0] in pre)
    return [n for n in WSHAPES if any(n.startswith(u) for u in used)]


FULL_PHASES = []
for _i in range(DEPTH):
    _k, _j = _i % 3, _i // 3
    FULL_PHASES.append(("attn", _j, _i) if _k == 0 else (("rwkv",) if _k == 1 else ("conv",)))
    FULL_PHASES.append(("ffn", _i)); FULL_PHASES.append(("ple", _i))
FULL_PHASES.append(("final",))


def build_nc(phases):
    nc = bass.Bass("TRN2", target_bir_lowering=False)
    es = contextlib.ExitStack()
    Prog(nc, es, phases)
    es.close()
    return nc


def make_vecs(inp):
    v = np.zeros((128, NV), np.float32)

    def put(name, arr, n=NCH):
        v[:, VCOL[name]:VCOL[name] + n] = pc(arr, n)
    for j in range(2): put("attn_norm%d" % j, inp["attn_norm"][j])
    put("rwkv_norm", inp["rwkv_norm"][0]); put("conv_norm", inp["conv_norm"][0])
    for i in range(DEPTH):
        put("ffn_norm%d" % i, inp["ffn_norm"][i]); put("ple_norm%d" % i, inp["ple_norm"][i])
        for j in range(3): put("fcw%d_%d" % (i, j), inp["ffn_conv_w"][i, j], NFC)
        put("fcb%d" % i, inp["ffn_conv_b"][i], NFC)
    put("final_norm", inp["final_norm"])
    for j in range(6): put("mu%d" % j, inp["rwkv_mu"][0, j])
    for nm in ("w0", "a0", "k_k", "k_a", "ln_g", "ln_b"): put(nm, inp["rwkv_" + nm][0])
    put("r_k", np.asarray(inp["rwkv_r_k"][0]).reshape(-1))
    for j in range(3): put("conv_w%d" % j, inp["conv_w"][0, j])
    return v


def run(inputs, phases, ncores=8):
    inp = {k: np.asarray(v) for k, v in inputs.items()}
    nc = build_nc(phases)
    vecs = make_vecs(inp); consts = build_consts()
    in_maps = []
    for c in range(ncores):
        b = c % NB
        m = {"xT": np.ascontiguousarray(inp["x"][b].T.reshape(NCH, 128, S)),
             "pT": np.ascontiguousarray(inp["p"][:, b].transpose(0, 2, 1).reshape(DEPTH, 2, 128, S)),
             "vecs": vecs, "consts": consts}
        for name in wnames(phases):
            m[name] = np.ascontiguousarray(inp[name], dtype=np.float32)
        in_maps.append(m)
    res = run_bass_kernel_spmd(nc, in_maps, core_ids=list(range(ncores)))
    outs = [res.results[b]["outT"].reshape(D, S).T for b in range(min(NB, ncores))]
    return np.stack(outs, 0)


def kernel(**inputs):
    out = run(inputs, FULL_PHASES)
    return np.ascontiguousarray(out.astype(np.float32))
```
